# Optimizing a Trainium2 kernel written in Bass

```python
import jax
import jax.numpy as jnp
from jax import lax
import numpy as np

D_MODEL = 1024
BATCH = 2
SEQ = 8192
DEPTH = 2

N_BRANCH = 3
A_WIDTH = D_MODEL
A_GROUPS = 4
A_CHUNK = 128
B_HEADS = 8
B_HEAD_DIM = 64
B_WIDTH = B_HEADS * B_HEAD_DIM
Q_BLOCK = 128
C_HEADS = 8
C_HEAD_DIM = 64
C_WIDTH = C_HEADS * C_HEAD_DIM
C_DECAY_LORA = 64
C_AAA_LORA = 64
C_MV_LORA = 32
C_GATE_LORA = 128
C_GN_EPS = 64e-5
D_FF = ((8 * D_MODEL // 3 + 255) // 256) * 256

A_COLS = 2 * A_WIDTH
B_COLS = 3 * B_WIDTH + B_HEADS
C_COLS = 3 * C_WIDTH + C_DECAY_LORA + C_AAA_LORA + C_GATE_LORA
G_COLS = N_BRANCH * D_MODEL
IN_COLS = A_COLS + B_COLS + C_COLS + G_COLS
IN_SPLITS = (A_COLS, A_COLS + B_COLS, A_COLS + B_COLS + C_COLS)
B_SPLITS = (B_WIDTH, 2 * B_WIDTH, 3 * B_WIDTH)
C_SPLITS = (C_WIDTH, 2 * C_WIDTH, 3 * C_WIDTH, 3 * C_WIDTH + C_DECAY_LORA,
            3 * C_WIDTH + C_DECAY_LORA + C_AAA_LORA)
NORM_EPS = 1e-6
LN_EPS = 1e-5

kernel_name = 'hybrid_gated_gmlp_fox_rwkv7'


def rmsnorm(x, g):
    xf = x.astype(jnp.float32)
    y = xf * lax.rsqrt(jnp.mean(xf * xf, axis=-1, keepdims=True) + NORM_EPS)
    return (y * g.astype(jnp.float32)).astype(x.dtype)


def layernorm(x, g, b):
    xf = x.astype(jnp.float32)
    mu = jnp.mean(xf, axis=-1, keepdims=True)
    var = jnp.mean(jnp.square(xf - mu), axis=-1, keepdims=True)
    y = (xf - mu) * lax.rsqrt(var + LN_EPS)
    return (y * g.astype(jnp.float32) + b.astype(jnp.float32)).astype(x.dtype)


def token_shift(p):
    return jnp.pad(p, ((0, 0), (1, 0), (0, 0)))[:, :-1]


def spatial_gating_unit(u, v, ln_g, ln_b, w_s, b_s):
    bn, s, _ = v.shape
    n_chunks = s // A_CHUNK
    gd = A_WIDTH // A_GROUPS
    v = layernorm(v, ln_g, ln_b)
    causal = jnp.tril(jnp.ones((A_CHUNK, A_CHUNK), dtype=bool))
    w = jnp.where(causal[None], w_s, jnp.zeros_like(w_s)).astype(v.dtype)
    vc = v.reshape(bn, n_chunks, A_CHUNK, A_GROUPS, gd)
    mixed = jnp.einsum('gts,bcsgd->bctgd', w, vc) + b_s.T.astype(v.dtype)[None, None, :, :, None]
    return u * mixed.reshape(bn, s, A_WIDTH)


def forgetting_attention(q, k, v, f_logit, b_f):
    bn, s = q.shape[:2]
    n_blocks = s // Q_BLOCK
    log_f = jax.nn.log_sigmoid((f_logit + b_f).astype(jnp.float32))
    c = jnp.cumsum(log_f, axis=1)
    c_keys = c.transpose(0, 2, 1)
    kf = k.astype(jnp.float32)
    vf = v.astype(jnp.float32)
    key_pos = jnp.arange(s)
    qb = (q.astype(jnp.float32) * B_HEAD_DIM ** -0.5).reshape(
        bn, n_blocks, Q_BLOCK, B_HEADS, B_HEAD_DIM).swapaxes(0, 1)
    cb = c.reshape(bn, n_blocks, Q_BLOCK, B_HEADS).swapaxes(0, 1)

    def query_block(args):
        i, q_i, c_i = args
        logits = jnp.einsum('bqhd,bkhd->bhqk', q_i, kf)
        logits = logits + (c_i.transpose(0, 2, 1)[..., :, None] - c_keys[:, :, None, :])
        q_pos = i * Q_BLOCK + jnp.arange(Q_BLOCK)
        mask = key_pos[None, :] <= q_pos[:, None]
        logits = jnp.where(mask, logits, -jnp.inf)
        p = jax.nn.softmax(logits, axis=-1)
        return jnp.einsum('bhqk,bkhd->bqhd', p, vf)

    out = lax.map(query_block, (jnp.arange(n_blocks), qb, cb))
    return out.swapaxes(0, 1).reshape(bn, s, B_WIDTH).astype(q.dtype)


def rwkv7_time_mix(r, k, v, w_lo, a_lo, g_lo, w0, w_up, a0, a_up, g_up,
                   k_k, k_a, r_k, lnx_g, lnx_b):
    bn, s, _ = r.shape
    f32 = jnp.float32
    w = -jax.nn.softplus(-(w0 + jnp.tanh(w_lo) @ w_up).astype(f32)) - 0.5
    decay = jnp.exp(-jnp.exp(w))
    a = jax.nn.sigmoid((a0 + a_lo @ a_up).astype(f32))
    g = (jax.nn.sigmoid(g_lo) @ g_up).astype(f32)
    kk = (k * k_k).astype(f32).reshape(bn, s, C_HEADS, C_HEAD_DIM)
    kk = kk / jnp.maximum(jnp.sqrt(jnp.sum(kk * kk, axis=-1, keepdims=True)), 1e-12)
    k = k.astype(f32) * (1.0 + (a - 1.0) * k_a.astype(f32))

    def heads(t):
        return t.astype(f32).reshape(bn, s, C_HEADS, C_HEAD_DIM)

    rh, kh, vh, ah, dh = heads(r), heads(k), heads(v), heads(a), heads(decay)

    def step(state, inp):
        r_t, d_t, k_t, v_t, kk_t, a_t = inp
        sa = jnp.einsum('bhij,bhj->bhi', state, kk_t)
        state = (state * d_t[..., None, :]
                 - sa[..., :, None] * (kk_t * a_t)[..., None, :]
                 + v_t[..., :, None] * k_t[..., None, :])
        y_t = jnp.einsum('bhij,bhj->bhi', state, r_t)
        return state, y_t

    tm = lambda t: t.swapaxes(0, 1)
    s0 = jnp.zeros((bn, C_HEADS, C_HEAD_DIM, C_HEAD_DIM), f32)
    _, y = lax.scan(step, s0, (tm(rh), tm(dh), tm(kh), tm(vh), tm(kk), tm(ah)))
    y = y.swapaxes(0, 1)
    mu = jnp.mean(y, axis=-1, keepdims=True)
    var = jnp.mean(jnp.square(y - mu), axis=-1, keepdims=True)
    y = ((y - mu) * lax.rsqrt(var + C_GN_EPS)).reshape(bn, s, C_WIDTH)
    y = y * lnx_g.astype(f32) + lnx_b.astype(f32)
    bonus = jnp.sum(rh * kh * r_k.astype(f32), axis=-1, keepdims=True) * vh
    y = y + bonus.reshape(bn, s, C_WIDTH)
    return (y * g).astype(r.dtype)


def setup_inputs(seed: int = 0) -> dict:
    key = jax.random.key(seed)
    keys = list(jax.random.split(key, 40))
    f32 = jnp.float32

    def nrm(shape, scale):
        return scale * jax.random.normal(keys.pop(), shape, f32)

    def unif(shape, lo, hi):
        return jax.random.uniform(keys.pop(), shape, f32, lo, hi)

    L = DEPTH
    Lv = DEPTH - 1
    return {
        'x': nrm((BATCH, SEQ, D_MODEL), 1.0),
        'norm_mix': 1.0 + nrm((L, D_MODEL), 0.02),
        'w_in': nrm((L, D_MODEL, IN_COLS), D_MODEL ** -0.5),
        'gate_bias': nrm((L, N_BRANCH, D_MODEL), 0.02),
        'a_ln_g': 1.0 + nrm((L, A_WIDTH), 0.02),
        'a_ln_b': nrm((L, A_WIDTH), 0.02),
        'a_w_s': nrm((L, A_GROUPS, A_CHUNK, A_CHUNK), A_CHUNK ** -0.5),
        'a_b_s': 1.0 + nrm((L, A_GROUPS, A_CHUNK), 0.1),
        'b_f_bias': unif((L, B_HEADS), 1.0, 5.0),
        'c_mu': unif((L, C_COLS), 0.0, 1.0),
        'c_w0': unif((L, C_WIDTH), -3.0, 1.0),
        'c_w_up': nrm((L, C_DECAY_LORA, C_WIDTH), 0.1),
        'c_a0': nrm((L, C_WIDTH), 0.1),
        'c_a_up': nrm((L, C_AAA_LORA, C_WIDTH), C_AAA_LORA ** -0.5),
        'c_g_up': nrm((L, C_GATE_LORA, C_WIDTH), C_GATE_LORA ** -0.5),
        'c_k_k': 0.85 + nrm((L, C_WIDTH), 0.02),
        'c_k_a': 1.0 + nrm((L, C_WIDTH), 0.02),
        'c_r_k': nrm((L, C_HEADS, C_HEAD_DIM), 0.1),
        'c_lnx_g': 1.0 + nrm((L, C_WIDTH), 0.02),
        'c_lnx_b': nrm((L, C_WIDTH), 0.02),
        'c_v0': nrm((Lv, C_WIDTH), 0.1),
        'c_v_down': nrm((Lv, C_WIDTH, C_MV_LORA), C_WIDTH ** -0.5),
        'c_v_up': nrm((Lv, C_MV_LORA, C_WIDTH), C_MV_LORA ** -0.5),
        'p_a': nrm((L, A_WIDTH, D_MODEL), A_WIDTH ** -0.5),
        'p_b': nrm((L, B_WIDTH, D_MODEL), B_WIDTH ** -0.5),
        'p_c': nrm((L, C_WIDTH, D_MODEL), C_WIDTH ** -0.5),
        'w_out': nrm((L, D_MODEL, D_MODEL), D_MODEL ** -0.5),
        'norm_ffn': 1.0 + nrm((L, D_MODEL), 0.02),
        'w_gate_up': nrm((L, D_MODEL, 2 * D_FF), D_MODEL ** -0.5),
        'w_down': nrm((L, D_FF, D_MODEL), D_FF ** -0.5),
        'norm_final': 1.0 + nrm((D_MODEL,), 0.02),
    }


def reference(x, norm_mix, w_in, gate_bias, a_ln_g, a_ln_b, a_w_s, a_b_s, b_f_bias,
              c_mu, c_w0, c_w_up, c_a0, c_a_up, c_g_up, c_k_k, c_k_a, c_r_k,
              c_lnx_g, c_lnx_b, c_v0, c_v_down, c_v_up, p_a, p_b, p_c, w_out,
              norm_ffn, w_gate_up, w_down, norm_final):
    bn, s, _ = x.shape
    v_first = None
    for l in range(DEPTH):
        h = rmsnorm(x, norm_mix[l])
        proj = h @ w_in[l]
        pa, pb, pc, pg = jnp.split(proj, IN_SPLITS, axis=-1)

        ua, va = jnp.split(jax.nn.gelu(pa), 2, axis=-1)
        ya = spatial_gating_unit(ua, va, a_ln_g[l], a_ln_b[l], a_w_s[l], a_b_s[l])

        qb, kb, vb, fb = jnp.split(pb, B_SPLITS, axis=-1)
        hd = (bn, s, B_HEADS, B_HEAD_DIM)
        yb = forgetting_attention(qb.reshape(hd), kb.reshape(hd), vb.reshape(hd), fb, b_f_bias[l])

        pc = pc + (token_shift(pc) - pc) * c_mu[l]
        rc, kc, vc, wlo, alo, glo = jnp.split(pc, C_SPLITS, axis=-1)
        if l == 0:
            v_first = vc
        else:
            vc = vc + (v_first - vc) * jax.nn.sigmoid(c_v0[l - 1] + (vc @ c_v_down[l - 1]) @ c_v_up[l - 1])
        yc = rwkv7_time_mix(rc, kc, vc, wlo, alo, glo, c_w0[l], c_w_up[l], c_a0[l], c_a_up[l],
                            c_g_up[l], c_k_k[l], c_k_a[l], c_r_k[l], c_lnx_g[l], c_lnx_b[l])

        gates = jax.nn.sigmoid(pg.reshape(bn, s, N_BRANCH, D_MODEL) + gate_bias[l])
        merged = (gates[:, :, 0] * (ya @ p_a[l])
                  + gates[:, :, 1] * (yb @ p_b[l])
                  + gates[:, :, 2] * (yc @ p_c[l]))
        x = x + merged @ w_out[l]

        h = rmsnorm(x, norm_ffn[l])
        gt, up = jnp.split(h @ w_gate_up[l], 2, axis=-1)
        x = x + (jax.nn.silu(gt) * up) @ w_down[l]
    return rmsnorm(x, norm_final)
```

```python
import numpy as np
from contextlib import ExitStack
import concourse.bass as bass
import concourse.mybir as mybir
from concourse.bass_utils import run_bass_kernel_spmd

F32 = mybir.dt.float32
BF16 = mybir.dt.bfloat16
AF = mybir.ActivationFunctionType
ALU = mybir.AluOpType
AX = mybir.AxisListType

EPOCH = 20000
NDMASEM = 24


class Sched:
    ENGS = ("pe", "act", "dve", "pool", "sp")

    def __init__(self, nc, es):
        self.nc = nc
        self.es = es
        self.prog = {e: [] for e in self.ENGS}
        self.cnt = {e: 0 for e in self.ENGS}
        self.sem = {e: self._newsem("c_" + e) for e in self.ENGS}
        self.waited = {e: {} for e in self.ENGS}
        self.lastw = {}
        self.readers = {}
        self.dsem = {e: [self._newsem("d_%s%d" % (e, i)) for i in range(NDMASEM)] for e in ("sp", "pool", "act")}
        self.dcnt = {e: [0] * NDMASEM for e in ("sp", "pool", "act")}
        self.drr = {e: 0 for e in ("sp", "pool", "act")}
        self.nsem = 0
        self.out_tokens = []
        self.const = set()
        self.hist = {}
        self.excl = set()

    def _newsem(self, name):
        self._n = getattr(self, "_n", 0) + 1
        return self.es.enter_context(self.nc.semaphore("%s_%d" % (name, self._n)))

    def _need(self, eng, reads, writes, skip_same=False):
        toks = []
        for b in list(reads) + list(writes):
            t = self.lastw.get(b)
            if t is not None:
                toks.append(t)
        for b in writes:
            toks.extend(self.readers.get(b, ()))
        for (sem, val, teng) in toks:
            if skip_same and teng == eng:
                continue
            w = self.waited[eng]
            k = id(sem)
            if w.get(k, 0) >= val:
                continue
            w[k] = val
            self.prog[eng].append(("wait", sem, val))

    def _commit(self, tok, reads, writes):
        for b in writes:
            self.lastw[b] = tok
            self.readers[b] = []
        for b in reads:
            if b not in self.const:
                self.readers.setdefault(b, []).append(tok)

    def _excl(self, reads, writes):
        ex = [b for b in reads if isinstance(b, tuple) and b[0] in self.excl]
        if ex:
            writes = list(writes) + [b for b in ex if b not in writes]
            reads = [b for b in reads if b not in ex]
        return reads, writes

    def op(self, eng, fn, reads=(), writes=()):
        reads, writes = self._excl(reads, writes)
        self._need(eng, reads, writes, skip_same=(eng == "pe"))
        if self.cnt[eng] >= EPOCH:
            self.sem[eng] = self._newsem("c_" + eng)
            self.cnt[eng] = 0
        self.cnt[eng] += 1
        tok = (self.sem[eng], self.cnt[eng], eng)
        self.hist.setdefault(eng, {})[id(self.sem[eng])] = (self.sem[eng], self.cnt[eng])
        self.prog[eng].append(("op", fn, self.sem[eng], 1))
        self._commit(tok, reads, writes)
        return tok

    def dma(self, q, out, in_, reads=(), writes=(), is_output=False, custom=None, inc=16, **kw):
        i = self.drr[q]
        self.drr[q] = (i + 1) % NDMASEM
        sem = self.dsem[q][i]
        if self.dcnt[q][i] > 0:
            w = self.waited[q]
            if w.get(id(sem), 0) < self.dcnt[q][i]:
                w[id(sem)] = self.dcnt[q][i]
                self.prog[q].append(("wait", sem, self.dcnt[q][i]))
        self._need(q, reads, writes)
        self.dcnt[q][i] += inc
        tok = (sem, self.dcnt[q][i], "dma_" + q)
        if custom is not None:
            self.prog[q].append(("op", custom, sem, inc))
        else:
            self.prog[q].append(("op", lambda e, o=out, a=in_, k=kw: e.dma_start(out=o, in_=a, **k), sem, 16))
        self._commit(tok, reads, writes)
        if is_output:
            self.out_tokens.append(tok)
        return tok

    def barrier(self, skip_q=()):
        toks = []
        for e in self.ENGS:
            for (sem, val) in self.hist.get(e, {}).values():
                toks.append((sem, val))
        for q in self.dsem:
            if q in skip_q:
                continue
            for i in range(NDMASEM):
                if self.dcnt[q][i] > 0:
                    toks.append((self.dsem[q][i], self.dcnt[q][i]))
        for e in self.ENGS:
            w = self.waited[e]
            for (sem, val) in toks:
                if w.get(id(sem), 0) >= val:
                    continue
                w[id(sem)] = val
                self.prog[e].append(("wait", sem, val))

    def finish(self):
        for (sem, val, _) in self.out_tokens:
            self.prog["sp"].append(("wait", sem, val))
        nc = self.nc
        with nc.Block() as block:
            def replay(e, lst):
                for it in lst:
                    if it[0] == "wait":
                        e.wait_ge(it[1], it[2])
                    else:
                        it[1](e).then_inc(it[2], it[3])

            @block.sync
            def _(e):
                replay(e, self.prog["sp"])

            @block.tensor
            def _(e):
                replay(e, self.prog["pe"])

            @block.scalar
            def _(e):
                replay(e, self.prog["act"])

            @block.vector
            def _(e):
                replay(e, self.prog["dve"])

            @block.gpsimd
            def _(e):
                replay(e, self.prog["pool"])

    def stats(self):
        return {e: len(self.prog[e]) for e in self.ENGS}


class Ring:
    def __init__(self, name, tiles):
        self.name = name
        self.tiles = tiles
        self.i = 0

    def next(self):
        k = self.i % len(self.tiles)
        self.i += 1
        return self.tiles[k], (self.name, k)


def mk_alloc(nc, es):
    def sb(n, shp, dt=F32):
        return es.enter_context(nc.sbuf_tensor(n, shp, dt))

    def ps(n, shp, dt=F32):
        return es.enter_context(nc.psum_tensor(n, shp, dt))

    def sbring(name, n, shp, dt=F32):
        return Ring(name, [sb("%s%d" % (name, i), shp, dt) for i in range(n)])

    return sb, ps, sbring


def build_masks(S, sb):
    m = {}
    for n in ("ident", "ones", "uti", "uts", "lts", "blk"):
        m[n] = sb("m_" + n, [128, 128])
    m["mask2"] = sb("m_mask2", [128, 256])
    m["negm"] = sb("m_negm", [128, 128], BF16)
    m["identb"] = sb("m_identb", [128, 128], BF16)

    def sel(t, pat, cm, cmp, fill, key):
        S.op("pool", lambda e: e.affine_select(out=t, in_=t, pattern=[[pat, t.shape[-1]]], compare_op=cmp, fill=fill, base=0, channel_multiplier=cm), reads=[key], writes=[key])

    def ms(t, v, key):
        S.op("pool", lambda e: e.memset(t, v), writes=[key])

    ms(m["ident"][:], 0.0, "ident"); sel(m["ident"][:], -1, 1, ALU.not_equal, 1.0, "ident")
    ms(m["identb"][:], 0.0, "identb"); sel(m["identb"][:], -1, 1, ALU.not_equal, 1.0, "identb")
    ms(m["ones"][:], 1.0, "ones")
    ms(m["uti"][:], 1.0, "uti"); sel(m["uti"][:], 1, -1, ALU.is_ge, 0.0, "uti")
    ms(m["uts"][:], 1.0, "uts"); sel(m["uts"][:], 1, -1, ALU.is_gt, 0.0, "uts")
    ms(m["lts"][:], 1.0, "lts"); sel(m["lts"][:], -1, 1, ALU.is_gt, 0.0, "lts")
    ms(m["negm"][:], 0.0, "negm"); sel(m["negm"][:], 1, -1, ALU.is_ge, -30000.0, "negm")
    ms(m["blk"][:], 0.0, "blk")
    S.op("pool", lambda e: e.memset(m["blk"][0:64, 0:64], 1.0), reads=["blk"], writes=["blk"])
    S.op("pool", lambda e: e.memset(m["blk"][64:128, 64:128], 1.0), reads=["blk"], writes=["blk"])
    ms(m["mask2"][:], 1.0, "mask2")
    sel(m["mask2"][:, 0:128], 1, -1, ALU.is_gt, 0.0, "mask2")
    sel(m["mask2"][:, 128:256], 1, -1, ALU.is_ge, 0.0, "mask2")
    for k in m:
        S.const.add(k)
    return m


def mkops(S):
    class O:
        pass
    o = O()

    def MM(out, lhsT, rhs, start=True, stop=True, r=(), w=()):
        S.op("pe", lambda e: e.matmul(out, lhsT, rhs, start=start, stop=stop), r, w)

    def TR(out, in_, ident, r=(), w=()):
        S.op("pe", lambda e: e.transpose(out, in_, ident), r, w)

    def ACT(out, in_, func, bias=None, scale=1.0, accum=None, r=(), w=()):
        kw = dict(out=out, in_=in_, func=func, scale=scale)
        if bias is not None:
            kw["bias"] = bias
        if accum is not None:
            kw["accum_out"] = accum
        S.op("act", lambda e: e.activation(**kw), r, w)

    def TT(eng, out, in0, in1, op, r=(), w=()):
        S.op(eng, lambda e: e.tensor_tensor(out=out, in0=in0, in1=in1, op=op), r, w)

    def TS(eng, out, in0, s1, s2, op0, op1=None, r=(), w=()):
        if op1 is None:
            S.op(eng, lambda e: e.tensor_scalar(out=out, in0=in0, scalar1=s1, scalar2=None, op0=op0), r, w)
        else:
            S.op(eng, lambda e: e.tensor_scalar(out=out, in0=in0, scalar1=s1, scalar2=s2, op0=op0, op1=op1), r, w)

    def STT(eng, out, in0, scalar, in1, op0, op1, r=(), w=()):
        S.op(eng, lambda e: e.scalar_tensor_tensor(out=out, in0=in0, scalar=scalar, in1=in1, op0=op0, op1=op1), r, w)

    def CP(eng, out, in_, r=(), w=()):
        if eng == "act":
            S.op("act", lambda e: e.copy(out, in_), r, w)
        else:
            S.op(eng, lambda e: e.tensor_copy(out, in_), r, w)

    def MS(eng, out, val, w=()):
        S.op(eng, lambda e: e.memset(out, val), (), w)

    def RED(eng, out, in_, op, r=(), w=()):
        S.op(eng, lambda e: e.tensor_reduce(out=out, in_=in_, axis=AX.X, op=op), r, w)

    def RECIP(out, in_, r=(), w=()):
        S.op("dve", lambda e: e.reciprocal(out, in_), r, w)

    def DMA(out, in_, r=(), w=(), is_output=False, q="sp", **kw):
        S.dma(q, out, in_, reads=r, writes=w, is_output=is_output, **kw)

    o.MM, o.TR, o.ACT, o.TT, o.TS, o.STT, o.CP, o.MS, o.RED, o.RECIP, o.DMA = MM, TR, ACT, TT, TS, STT, CP, MS, RED, RECIP, DMA
    return o


from contextlib import contextmanager


_stage_n = [0]


@contextmanager
def stage(nc, S):
    _stage_n[0] += 1
    sfx = "_s%d" % _stage_n[0]
    with ExitStack() as es2:
        sb0, ps, sbring0 = mk_alloc(nc, es2)

        def sb(n, shp, dt=F32):
            return sb0(n + sfx, shp, dt)

        def sbring(name, n, shp, dt=F32):
            r = sbring0(name + sfx, n, shp, dt)
            return r

        yield sb, sbring
        S.barrier()


A_U, A_V, B_Q, B_K, B_V, B_F, C_R, C_K, C_V, C_WLO, C_ALO, C_GLO, G0 = 0, 1024, 2048, 2560, 3072, 3584, 3592, 4104, 4616, 5128, 5192, 5256, 5384
IN_COLS = 8456
NORM_EPS = 1e-6
LN_EPS = 1e-5
STG = 256
import os as _os
WQ2 = bool(_os.environ.get('WQ2'))


def emit_A(nc, S, M, T, layer1, NT=16, NPH=2, after_e=None):
    TOKF = NT * 128
    NT = NT // NPH
    TOK = NT * 128
    NG = TOK // 512
    x = T["x"]; xprev = T["xprev"]
    w_in = T["w_in"]; nmix = T["nmix"]; gate_bias = T["gate_bias"]
    a_ln_g = T["a_ln_g"]; a_ln_b = T["a_ln_b"]; a_w_s = T["a_w_s"]; a_b_s = T["a_b_s"]
    b_f_bias = T["b_f_bias"]; c_mu = T["c_mu"]; c_w0 = T["c_w0"]; c_w_up = T["c_w_up"]
    c_a0 = T["c_a0"]; c_a_up = T["c_a_up"]; c_g_up = T["c_g_up"]
    c_k_k = T["c_k_k"]; c_k_a = T["c_k_a"]; c_r_k = T["c_r_k"]; p_a = T["p_a"]
    if layer1:
        c_v0 = T["c_v0"]; c_v_down = T["c_v_down"]; c_v_up = T["c_v_up"]; vfirst = T["vfirst"]
    o_qT = T["qT"]; o_kT = T["kT"]; o_vv = T["vv"]; o_lf = T["lf"]
    o_rw = {n: T["rw_" + n] for n in ("r", "k", "v", "kk", "b", "lw")}
    o_g = T["g"]; o_bonus = T["bonus"]
    o_maT = T["maT"]; o_gbT = T["gbT"]; o_gcT = T["gcT"]
    HSV = lambda ap: ap.rearrange("p (h c) -> p h c", h=4)

    with ExitStack() as es:
        sb0, ps, sbring0 = mk_alloc(nc, es)
        _sfx = "_A%d" % (1 if layer1 else 0)
        sb_g = lambda n, shp, dt=F32: sb0(n + _sfx, shp, dt)
        sb = sb_g
        sbring = lambda name, n, shp, dt=F32: Ring(name, [sb_g("%s%d" % (name, i), shp, dt) for i in range(n)])
        O = mkops(S)
        MM, TR, ACT, TT, TS, STT, CP, MS, RED, RECIP, DMA = O.MM, O.TR, O.ACT, O.TT, O.TS, O.STT, O.CP, O.MS, O.RED, O.RECIP, O.DMA
        ident, ones, uti, identb = M["ident"], M["ones"], M["uti"], M["identb"]

        tp_banks = [ps("tpbA%d" % i + _sfx, [128, 1024], BF16) for i in range(2)]
        tp_ring = Ring("tp", [b[:] for b in tp_banks])
        mm_banks = [ps("mmbA%d" % i + _sfx, [128, 512]) for i in range(6)]
        mm_ring = Ring("mm", [b[:] for b in mm_banks])
        S.excl.update(["tp", "mm"])

        def col_load(name, vec_ap, ncol):
            t = sb(name, [128, ncol])
            DMA(t[:], vec_ap.rearrange("(c p) -> p c", p=128), w=[name], allow_slow_non_contiguous=True)
            S.const.add(name)
            return t

        def bc_load(name, vec_ap, n):
            t = sb(name, [128, n])
            DMA(t[:], vec_ap.partition_broadcast(128), w=[name])
            S.const.add(name)
            return t

        GB = [col_load("GBc%d" % k, gate_bias[k], 8) for k in range(3)]
        LNG = col_load("LNG", a_ln_g, 8)
        LNB = col_load("LNB", a_ln_b, 8)
        BFb = bc_load("BFb", b_f_bias, 8)
        WTb = sb("WTb", [128, 4, 128], BF16); T2 = sb("T2", [128, 8, 128])
        sm_ring = sbring("sm", 4, [128, 8])
        gsb, gsbring = sb, sbring
        st0 = stage(nc, S); sb, sbring = st0.__enter__()
        WSf = sb("WSf", [128, 4, 128]); WTf = sb("WTf", [128, 4, 128])
        BSb = sb("BSb", [128, 4, 128])
        DMA(WSf[:], a_w_s.rearrange("g t s -> t g s"), w=["WSf"])
        for g in range(4):
            DMA(BSb[:, g, :], a_b_s[g].partition_broadcast(128), w=[("BSb", g)])
        for g in range(4):
            pm, pk = mm_ring.next()
            TR(pm[:, 0:128], WSf[:, g, :], ident[:], r=["WSf", "ident"], w=[pk])
            TT("dve", WTf[:, g, :], pm[:, 0:128], uti[:], ALU.mult, r=[pk, "uti"], w=[("WTf", g)])
            CP("pool", WTb[:, g, :], WTf[:, g, :], r=[("WTf", g)], w=["WTb"])
            pm, pk = mm_ring.next()
            MM(pm[:, 0:128], ones[:], WTf[:, g, :], r=["ones", ("WTf", g)], w=[pk])
            for dc in (2 * g, 2 * g + 1):
                STT("dve", T2[:, dc, :], pm[:, 0:128], LNB[:, dc:dc + 1], BSb[:, g, :], ALU.mult, ALU.add, r=[pk, "LNB", ("BSb", g)], w=["T2"])
        S.const.add("WTb"); S.const.add("T2")
        st0.__exit__(None, None, None)

        HT = gsb("HT", [128, 8, TOK], BF16)
        HS = gsb("HS", [128, 8, TOK], BF16)
        stg_ring = gsbring("stg", 2, [128, 8, STG])
        wb_ring = gsbring("wb", 3, [128, 8, 1024], BF16)
        xF, vfirstF = x, (vfirst if layer1 else None)
        oF = dict(qT=o_qT, kT=o_kT, vv=o_vv, lf=o_lf, g=o_g, bonus=o_bonus, maT=o_maT, gbT=o_gbT, gcT=o_gcT)
        o_rwF = o_rw
        for ph in range(NPH):
            r0 = ph * TOK
            x = xF[r0:r0 + TOK, :]
            if layer1:
                vfirst = vfirstF[r0:r0 + TOK]
            o_qT = oF["qT"][:, :, r0:r0 + TOK]; o_kT = oF["kT"][:, :, r0:r0 + TOK]; o_vv = oF["vv"][r0:r0 + TOK]; o_lf = oF["lf"][r0:r0 + TOK]
            o_g = oF["g"][r0:r0 + TOK, :]; o_bonus = oF["bonus"][r0:r0 + TOK, :]
            o_maT = oF["maT"][:, r0:r0 + TOK]; o_gbT = oF["gbT"][:, r0:r0 + TOK]; o_gcT = oF["gcT"][:, r0:r0 + TOK]
            o_rw = {n: o_rwF[n][r0:r0 + TOK] for n in o_rwF}
            st1 = stage(nc, S); sb, sbring = st1.__enter__()
            def bc_load(name, vec_ap, n):
                t = sb(name, [128, n])
                DMA(t[:], vec_ap.partition_broadcast(128), w=[name])
                S.const.add(name)
                return t
            Gb = bc_load("Gb", nmix, 1024)
            x_ring = sbring("xt", 3, [128, 1024])
            xh_ring = sbring("xh", 2, [128, 1024], BF16)
            junk = sb("junk", [128, 1024], BF16)
            def s1_norm(t):
                xt, xk = x_ring.next()
                DMA(xt[:], xprev[ph] if t < 0 else x[t * 128:(t + 1) * 128, :], w=[xk])
                sm, smk = sm_ring.next()
                MS("pool", sm[:], 0.0, w=[smk])
                ACT(junk[:], xt[:], AF.Square, accum=sm[:, 0:1], r=[xk, smk], w=["junk", smk])
                ACT(sm[:, 1:2], sm[:, 0:1], AF.Sqrt, bias=NORM_EPS, scale=1.0 / 1024, r=[smk], w=[smk])
                RECIP(sm[:, 2:3], sm[:, 1:2], r=[smk], w=[smk])
                xh, xhk = xh_ring.next()
                STT("dve", xh[:], xt[:], sm[:, 2:3], Gb[:], ALU.mult, ALU.mult, r=[xk, smk, "Gb"], w=[xhk])
                return xh, xhk

            def s1_tr(t, xh, xhk):
                tp, tpk = tp_ring.next()
                for mc in range(8):
                    TR(tp[:, mc * 128:(mc + 1) * 128], xh[:, mc * 128:(mc + 1) * 128], identb[:], r=[xhk, "identb"], w=[tpk])
                tp3 = tp.rearrange("p (c n) -> p c n", c=8)
                eng = "act" if (t % 2 == 0) else "dve"
                if t < 0:
                    CP(eng, HS[:, :, 0:1], tp3[:, :, 0:1], r=[tpk], w=["HS"])
                else:
                    CP(eng, HT[:, :, t * 128:(t + 1) * 128], tp3, r=[tpk, "HTX"], w=["HTX"])

            nx = s1_norm(-1)
            for t in range(-1, NT):
                cur = nx
                if t + 1 < NT:
                    nx = s1_norm(t + 1)
                s1_tr(t, *cur)
            CP("dve", HS[:, :, 1:TOK], HT[:, :, 0:TOK - 1], r=["HTX", "HS"], w=["HS"])
            st1.__exit__(None, None, None)

            cast_i = [0]

            def load_w(src2d, c0, ncols, dst, dst_c0, dkey, mu=None):
                for o0 in range(0, ncols, STG):
                    n = min(STG, ncols - o0)
                    stg, sk = stg_ring.next()
                    DMA(stg[:, :, 0:n], src2d[:, c0 + o0:c0 + o0 + n].rearrange("(mc p) c -> p mc c", p=128), w=[sk], q=("pool" if (cast_i[0] % 2 and WQ2) else "sp"))
                    cast_i[0] += 1
                    ceng = "act" if cast_i[0] % 2 else "dve"
                    if mu is None:
                        CP(ceng, dst[:, :, dst_c0 + o0:dst_c0 + o0 + n], stg[:, :, 0:n], r=[sk, dkey], w=[dkey])
                    else:
                        TT("dve", dst[:, :, dst_c0 + o0:dst_c0 + o0 + n], stg[:, :, 0:n], mu[:, o0:o0 + n].unsqueeze(1).to_broadcast([128, 8, n]), ALU.mult,
                           r=[sk, dkey, "MU", "OMMU"], w=[dkey])

            def fm_group(wt, wk, c0, Mw, tg, wt2=None, c02=None):
                pm, pk = mm_ring.next()
                for mc in range(8):
                    MM(pm[0:Mw, :], wt[:, mc, c0:c0 + Mw], HT[:, mc, tg * 512:(tg + 1) * 512], start=(mc == 0), stop=(mc == 7 and wt2 is None),
                       r=[wk, "HTX", "HS"], w=[pk])
                if wt2 is not None:
                    for mc in range(8):
                        MM(pm[0:Mw, :], wt2[:, mc, c02:c02 + Mw], HS[:, mc, tg * 512:(tg + 1) * 512], start=False, stop=(mc == 7), r=[wk, "HTX", "HS"], w=[pk])
                return pm, pk

            def tm_group(t, wt, wk, c0, N, wt2=None, c02=None, wk2=None):
                pm, pk = mm_ring.next()
                for mc in range(8):
                    MM(pm[:, 0:N], HT[:, mc, t * 128:(t + 1) * 128], wt[:, mc, c0:c0 + N], start=(mc == 0), stop=(mc == 7 and wt2 is None),
                       r=[wk, "HTX", "HS"], w=[pk])
                if wt2 is not None:
                    for mc in range(8):
                        MM(pm[:, 0:N], HS[:, mc, t * 128:(t + 1) * 128], wt2[:, mc, c02:c02 + N], start=False, stop=(mc == 7), r=[wk2, "HTX", "HS"], w=[pk])
                return pm, pk

            ev_i = [0]

            def ev_eng():
                ev_i[0] += 1
                return "act" if ev_i[0] % 2 else "dve"

            st2 = stage(nc, S); sb, sbring = st2.__enter__()
            junk = sb("junk2", [128, 1024], BF16)
            UT = sb("UT", [128, 8, TOK], BF16)
            uw = []
            for half in range(2):
                wt, wk = wb_ring.next()
                load_w(w_in, A_U + half * 512, 512, wt, 0, wk)
                uw.append((wt, wk))
            vwt, vwk = wb_ring.next()
            load_w(w_in, A_V, 512, vwt, 0, vwk)
            load_w(w_in, A_V + 512, 512, vwt, 512, vwk)
            for half in range(2):
                wt, wk = uw[half]
                for jc in range(4):
                    for tg in range(NG):
                        pm, pk = fm_group(wt, wk, jc * 128, 128, tg)
                        ACT(UT[:, half * 4 + jc, tg * 512:(tg + 1) * 512], pm[:, :], AF.Gelu_apprx_tanh, r=[pk], w=[("UT", half * 4 + jc, tg)])
            pat, pak = wb_ring.next()
            load_w(p_a, 0, 512, pat, 0, pak)
            load_w(p_a, 512, 512, pat, 512, pak)
            g0t, g0k = wb_ring.next()
            load_w(w_in, G0, 512, g0t, 0, g0k)
            wt, wk = vwt, vwk
            gv_ring = sbring("gv", 2, [128, 1024])
            nb_ring = sbring("nb", 2, [128, 1024], BF16)
            tmp_ring = sbring("tmpA", 3, [128, 128])
            smb_ring = sbring("smb", 3, [128, 8])

            def b_proj(t):
                gv, gvk = gv_ring.next()
                sm, smk = smb_ring.next()
                MS("pool", sm[:], 0.0, w=[smk])
                for half in range(2):
                    pm, pk = tm_group(t, wt, wk, half * 512, 512)
                    ACT(gv[:, half * 512:(half + 1) * 512], pm[:, :], AF.Gelu_apprx_tanh, accum=sm[:, half:half + 1], r=[pk], w=[gvk, smk])
                return gv, gvk, sm, smk

            def b_mix(t, gv, gvk, sm, smk):
                ACT(junk[:], gv[:], AF.Square, accum=sm[:, 2:3], r=[gvk, smk], w=["junk", smk])
                TT("dve", sm[:, 3:4], sm[:, 0:1], sm[:, 1:2], ALU.add, r=[smk], w=[smk])
                TS("dve", sm[:, 3:4], sm[:, 3:4], 1.0 / 1024, None, ALU.mult, r=[smk], w=[smk])
                TT("dve", sm[:, 4:5], sm[:, 3:4], sm[:, 3:4], ALU.mult, r=[smk], w=[smk])
                STT("dve", sm[:, 5:6], sm[:, 2:3], 1.0 / 1024, sm[:, 4:5], ALU.mult, ALU.subtract, r=[smk], w=[smk])
                ACT(sm[:, 6:7], sm[:, 5:6], AF.Sqrt, bias=LN_EPS, scale=1.0, r=[smk], w=[smk])
                RECIP(sm[:, 7:8], sm[:, 6:7], r=[smk], w=[smk])
                nb, nbk = nb_ring.next()
                TS("dve", nb[:], gv[:], sm[:, 3:4], sm[:, 7:8], ALU.subtract, ALU.mult, r=[gvk, smk], w=[nbk])
                tg = t // 4
                for dc in range(8):
                    pm, pk = mm_ring.next()
                    MM(pm[:, 0:128], nb[:, dc * 128:(dc + 1) * 128], WTb[:, dc // 2, :], r=[nbk, "WTb"], w=[pk])
                    tmp, tk = tmp_ring.next()
                    STT("dve", tmp[:], pm[:, 0:128], LNG[:, dc:dc + 1], T2[:, dc, :], ALU.mult, ALU.add, r=[pk, "LNG", "T2"], w=[tk])
                    TT("dve", UT[:, dc, t * 128:(t + 1) * 128], tmp[:], UT[:, dc, t * 128:(t + 1) * 128], ALU.mult, r=[tk, ("UT", dc, tg)], w=[("UT", dc, tg)])

            pend_b = b_proj(0)
            for t in range(NT):
                cur = pend_b
                if t + 1 < NT:
                    pend_b = b_proj(t + 1)
                b_mix(t, *cur)
            gt_ring = sbring("gt", 3, [128, 512])
            def gate_load(gi, half):
                wt, wk = wb_ring.next()
                load_w(w_in, G0 + gi * 1024 + half * 512, 512, wt, 0, wk)
                return wt, wk
            gl = [(gi, half) for gi in range(3) for half in range(2)]
            g_next = (g0t, g0k)
            for gidx, (gi, half) in enumerate(gl):
                if True:
                    wt, wk = g_next
                    if gidx + 1 < len(gl):
                        g_next = gate_load(*gl[gidx + 1])
                    for jc in range(4):
                        nch = half * 4 + jc
                        for tg in range(NG):
                            pm, pk = fm_group(wt, wk, jc * 128, 128, tg)
                            gt, gk = gt_ring.next()
                            ACT(gt[:], pm[:, :], AF.Sigmoid, bias=GB[gi][:, nch:nch + 1], r=[pk, "GBc%d" % gi], w=[gk])
                            if gi == 0:
                                pm2, pk2 = mm_ring.next()
                                for dc in range(8):
                                    MM(pm2[:, :], pat[:, dc, nch * 128:(nch + 1) * 128], UT[:, dc, tg * 512:(tg + 1) * 512], start=(dc == 0), stop=(dc == 7),
                                       r=[pak, ("UT", dc, tg)], w=[pk2])
                                TT("dve", gt[:], gt[:], pm2[:, :], ALU.mult, r=[gk, pk2], w=[gk])
                            dst = (o_maT, o_gbT, o_gcT)[gi]
                            DMA(dst[nch * 128:(nch + 1) * 128, tg * 512:(tg + 1) * 512], gt[:], r=[gk], is_output=True)
            st2.__exit__(None, None, None)
            st3 = stage(nc, S); sb, sbring = st3.__enter__()
            qk_ring = sbring("qkb", 3, [128, 512], BF16)
            for qi, (c0, dst) in enumerate(((B_Q, o_qT), (B_K, o_kT))):
                wt, wk = wb_ring.next()
                load_w(w_in, c0, 512, wt, 0, wk)
                for jc in range(4):
                    for tg in range(NG):
                        pm, pk = fm_group(wt, wk, jc * 128, 128, tg)
                        qb, qbk = qk_ring.next()
                        CP(ev_eng(), qb[:], pm[:, :], r=[pk], w=[qbk])
                        DMA(dst[jc, :, tg * 512:(tg + 1) * 512], qb[:], r=[qbk], is_output=True)
            wt, wk = wb_ring.next()
            load_w(w_in, B_V, 512, wt, 0, wk)
            load_w(w_in, B_F, 8, wt, 512, wk)
            lf_ring = sbring("lfr", 3, [128, 8])
            for t in range(NT):
                pm, pk = tm_group(t, wt, wk, 0, 512)
                qb, qbk = qk_ring.next()
                CP(ev_eng(), qb[:], pm[:, :], r=[pk], w=[qbk])
                DMA(o_vv[t * 128:(t + 1) * 128], HSV(qb[:]), r=[qbk], is_output=True)
                pm, pk = tm_group(t, wt, wk, 512, 8)
                lf, lfk = lf_ring.next()
                TT("dve", lf[:], pm[:, 0:8], BFb[:], ALU.add, r=[pk, "BFb"], w=[lfk])
                ACT(lf[:], lf[:], AF.Exp, scale=-1.0, r=[lfk], w=[lfk])
                ACT(lf[:], lf[:], AF.Ln, bias=1.0, r=[lfk], w=[lfk])
                TS("dve", lf[:], lf[:], -1.0, None, ALU.mult, r=[lfk], w=[lfk])
                DMA(o_lf[t * 128:(t + 1) * 128], HSV(lf[:]), r=[lfk], is_output=True)
            st3.__exit__(None, None, None)
            if ph == NPH - 1 and after_e is not None:
                after_e()
            st4 = stage(nc, S); sb, sbring = st4.__enter__()
            def bc_load(name, vec_ap, n):
                t = sb(name, [128, n])
                DMA(t[:], vec_ap.partition_broadcast(128), w=[name])
                S.const.add(name)
                return t
            MU = bc_load("MU", c_mu, 1792)
            OMMU = sb("OMMU", [128, 1792])
            TS("dve", OMMU[:], MU[:], -1.0, 1.0, ALU.mult, ALU.add, r=["MU"], w=["OMMU"])
            S.const.add("OMMU")
            KKb = bc_load("KKb", c_k_k, 512); KAb = bc_load("KAb", c_k_a, 512); RKb = bc_load("RKb", c_r_k, 512)
            WUP = sb("WUP", [65, 512]); AUP = sb("AUP", [65, 512]); GUPf = sb("GUPf", [128, 512]); GUP = sb("GUP", [128, 512], BF16)
            DMA(WUP[0:64, :], c_w_up, w=["WUP"]); DMA(WUP[64:65, :], c_w0.partition_broadcast(1), r=["WUP"], w=["WUP"])
            DMA(AUP[0:64, :], c_a_up, w=["AUP"]); DMA(AUP[64:65, :], c_a0.partition_broadcast(1), r=["AUP"], w=["AUP"])
            DMA(GUPf[:], c_g_up, w=["GUPf"])
            CP("dve", GUP[:], GUPf[:], r=["GUPf"], w=["GUP"])
            for k in ("WUP", "AUP", "GUP"):
                S.const.add(k)
            if layer1:
                VUPf = sb("VUPf", [33, 512]); VUP = sb("VUP", [33, 512], BF16)
                DMA(VUPf[0:32, :], c_v_up, w=["VUPf"]); DMA(VUPf[32:33, :], c_v0.partition_broadcast(1), r=["VUPf"], w=["VUPf"])
                CP("dve", VUP[:], VUPf[:], r=["VUPf"], w=["VUP"])
                VDf = sb("VDf", [128, 4, 32]); VDb = sb("VDb", [128, 4, 32], BF16)
                DMA(VDf[:], c_v_down.rearrange("(c p) n -> p c n", p=128), w=["VDf"])
                CP("dve", VDb[:], VDf[:], r=["VDf"], w=["VDb"])
                S.const.add("VUP"); S.const.add("VDb")
            w1 = sb("w1t", [128, 8, 512], BF16); w1k = "w1t"
            load_w(w_in, C_WLO, 256, w1, 0, w1k, mu=OMMU[:, 1536:1792])
            load_w(w_in, C_WLO, 256, w1, 256, w1k, mu=MU[:, 1536:1792])
            if layer1:
                wv = sb("wvt", [128, 8, 1024], BF16); wvk = "wvt"
                load_w(w_in, C_V, 512, wv, 0, wvk, mu=OMMU[:, 1024:1536])
                load_w(w_in, C_V, 512, wv, 512, wvk, mu=MU[:, 1024:1536])
            wa, wak = wb_ring.next(); wb_, wbk = wb_ring.next(); wc, wck = wb_ring.next()
            load_w(w_in, C_R, 512, wa, 0, wak, mu=OMMU[:, 0:512]); load_w(w_in, C_K, 512, wa, 512, wak, mu=OMMU[:, 512:1024])
            load_w(w_in, C_V, 512, wb_, 0, wbk, mu=OMMU[:, 1024:1536]); load_w(w_in, C_R, 512, wb_, 512, wbk, mu=MU[:, 0:512])
            load_w(w_in, C_K, 512, wc, 0, wck, mu=MU[:, 512:1024]); load_w(w_in, C_V, 512, wc, 512, wck, mu=MU[:, 1024:1536])
            TW = sb("TW", [65, TOK]); TA = sb("TA", [65, TOK]); TG = sb("TG", [128, TOK], BF16)
            MS("dve", TW[64:65, :], 1.0, w=["TW"]); MS("dve", TA[64:65, :], 1.0, w=["TA"])
            for tg in range(NG):
                pm, pk = fm_group(w1, w1k, 0, 64, tg, wt2=w1, c02=256)
                ACT(TW[0:64, tg * 512:(tg + 1) * 512], pm[0:64, :], AF.Tanh, r=[pk, "TW"], w=["TW"])
                pm, pk = fm_group(w1, w1k, 64, 64, tg, wt2=w1, c02=256 + 64)
                CP("dve", TA[0:64, tg * 512:(tg + 1) * 512], pm[0:64, :], r=[pk, "TA"], w=["TA"])
                pm, pk = fm_group(w1, w1k, 128, 128, tg, wt2=w1, c02=256 + 128)
                ACT(TG[:, tg * 512:(tg + 1) * 512], pm[:, :], AF.Sigmoid, r=[pk, "TG"], w=["TG"])
            if layer1:
                vt_ring = sbring("VT", 2, [128, 4, 512], BF16)
                TV = sb("TV", [33, TOK], BF16)
                MS("dve", TV[32:33, :], 1.0, w=["TV"])
                for tg in range(NG):
                    VT, vtk = vt_ring.next()
                    for jc in range(4):
                        pm, pk = fm_group(wv, wvk, jc * 128, 128, tg, wt2=wv, c02=512 + jc * 128)
                        CP(ev_eng(), VT[:, jc, :], pm[:, :], r=[pk, vtk], w=[vtk])
                    pm, pk = mm_ring.next()
                    for jc in range(4):
                        MM(pm[0:32, :], VDb[:, jc, :], VT[:, jc, :], start=(jc == 0), stop=(jc == 3), r=["VDb", vtk], w=[pk])
                    CP("dve", TV[0:32, tg * 512:(tg + 1) * 512], pm[0:32, :], r=[pk, "TV"], w=["TV"])
            names = ("r", "k", "v", "a", "e", "g", "kq", "t", "rk")
            E = {n: sbring("e_" + n, 1, [128, 512]) for n in names}
            for n2, n1 in (("kk", "kq"), ("b", "a"), ("bon", "rk"), ("lw", "e")):
                E[n2] = E[n1]
            if layer1:
                E["vf"] = sbring("e_vf", 1, [128, 512]); E["s"] = sbring("e_s", 1, [128, 512])
            v8 = lambda ap: ap.rearrange("p (h d) -> p h d", h=8)
            bc8 = lambda ap: ap.unsqueeze(2).to_broadcast([128, 8, 64])
            for t in range(NT):
                rows = slice(t * 128, (t + 1) * 128)
                tl = slice(t * 128, (t + 1) * 128)
                r_, rk_ = E["r"].next(); k_, kk_ = E["k"].next(); v_, vk_ = E["v"].next(); a_, ak_ = E["a"].next(); e_, ek_ = E["e"].next()
                g_, gk_ = E["g"].next(); kq, kqk = E["kq"].next(); kkn, kknk = E["kk"].next(); b_, bk_ = E["b"].next(); tt_, ttk = E["t"].next()
                rkp, rkpk = E["rk"].next(); bon, bonk = E["bon"].next(); lw, lwk = E["lw"].next()
                sm, smk = sm_ring.next()
                pm, pk = tm_group(t, wa, wak, 0, 512, wt2=wb_, c02=512, wk2=wbk)
                CP("act", r_[:], pm[:, :], r=[pk], w=[rk_])
                DMA(o_rw["r"][rows], HSV(r_[:]), r=[rk_], is_output=True)
                pm, pk = tm_group(t, wa, wak, 512, 512, wt2=wc, c02=0, wk2=wck)
                CP("dve", k_[:], pm[:, :], r=[pk], w=[kk_])
                pm, pk = tm_group(t, wb_, wbk, 0, 512, wt2=wc, c02=512, wk2=wck)
                CP("act", v_[:], pm[:, :], r=[pk], w=[vk_])
                if layer1:
                    vf, vfk = E["vf"].next(); s_, sk_ = E["s"].next()
                    DMA(HSV(vf[:]), vfirst[rows], w=[vfk])
                    pm, pk = mm_ring.next()
                    MM(pm[:, :], TV[:, tl], VUP[:, :], r=["TV", "VUP"], w=[pk])
                    ACT(s_[:], pm[:, :], AF.Sigmoid, r=[pk], w=[sk_])
                    TT("dve", vf[:], vf[:], v_[:], ALU.subtract, r=[vfk, vk_], w=[vfk])
                    TT("dve", vf[:], vf[:], s_[:], ALU.mult, r=[vfk, sk_], w=[vfk])
                    TT("dve", v_[:], v_[:], vf[:], ALU.add, r=[vfk, vk_], w=[vk_])
                DMA(o_rw["v"][rows], HSV(v_[:]), r=[vk_], is_output=True)
                pm, pk = mm_ring.next()
                MM(pm[:, :], TA[:, tl], AUP[:, :], r=["TA", "AUP"], w=[pk])
                ACT(a_[:], pm[:, :], AF.Sigmoid, r=[pk], w=[ak_])
                pm, pk = mm_ring.next()
                MM(pm[:, :], TW[:, tl], WUP[:, :], r=["TW", "WUP"], w=[pk])
                ACT(e_[:], pm[:, :], AF.Exp, scale=-1.0, r=[pk], w=[ek_])
                ACT(e_[:], e_[:], AF.Ln, bias=1.0, r=[ek_], w=[ek_])
                ACT(e_[:], e_[:], AF.Exp, scale=-1.0, bias=-0.5, r=[ek_], w=[ek_])
                TS("dve", lw[:], e_[:], -1.0, None, ALU.mult, r=[ek_], w=[lwk])
                DMA(o_rw["lw"][rows], HSV(lw[:]), r=[lwk], is_output=True)
                pm, pk = mm_ring.next()
                MM(pm[:, :], TG[:, tl], GUP[:, :], r=["TG", "GUP"], w=[pk])
                CP("dve", g_[:], pm[:, :], r=[pk], w=[gk_])
                DMA(o_g[rows, :], g_[:], r=[gk_], is_output=True)
                TT("dve", kq[:], k_[:], KKb[:], ALU.mult, r=[kk_, "KKb"], w=[kqk])
                TT("dve", tt_[:], kq[:], kq[:], ALU.mult, r=[kqk], w=[ttk])
                RED("dve", sm[:, 0:8], v8(tt_[:]), ALU.add, r=[ttk], w=[smk])
                ACT(sm[:, 0:8], sm[:, 0:8], AF.Sqrt, r=[smk], w=[smk])
                TS("dve", sm[:, 0:8], sm[:, 0:8], 1e-12, None, ALU.max, r=[smk], w=[smk])
                RECIP(sm[:, 0:8], sm[:, 0:8], r=[smk], w=[smk])
                TT("dve", v8(kkn[:]), v8(kq[:]), bc8(sm[:, 0:8]), ALU.mult, r=[kqk, smk], w=[kknk])
                DMA(o_rw["kk"][rows], HSV(kkn[:]), r=[kknk], is_output=True)
                STT("dve", tt_[:], a_[:], -1.0, KAb[:], ALU.add, ALU.mult, r=[ak_, "KAb", ttk], w=[ttk])
                TT("dve", b_[:], kkn[:], a_[:], ALU.mult, r=[kknk, ak_, ttk], w=[bk_])
                DMA(o_rw["b"][rows], HSV(b_[:]), r=[bk_], is_output=True)
                STT("dve", k_[:], tt_[:], 1.0, k_[:], ALU.add, ALU.mult, r=[ttk, kk_], w=[kk_])
                DMA(o_rw["k"][rows], HSV(k_[:]), r=[kk_], is_output=True)
                TT("dve", rkp[:], r_[:], k_[:], ALU.mult, r=[rk_, kk_], w=[rkpk])
                TT("dve", rkp[:], rkp[:], RKb[:], ALU.mult, r=[rkpk, "RKb"], w=[rkpk])
                sm2, sm2k = sm_ring.next()
                RED("dve", sm2[:, 0:8], v8(rkp[:]), ALU.add, r=[rkpk], w=[sm2k])
                TT("dve", v8(bon[:]), v8(v_[:]), bc8(sm2[:, 0:8]), ALU.mult, r=[vk_, sm2k], w=[bonk])
                DMA(o_bonus[rows, :], bon[:], r=[bonk], is_output=True)
            st4.__exit__(None, None, None)
        S.barrier()


def emit_B(nc, S, M, T, SEQ, tag, do_attn=True, do_rwkv=True, after_chunk=None):
    import os
    NB = SEQ // 128
    qT = T["qT"]; kT = T["kT"]; vv = T["vv"]; lf = T["lf"]; crow = T["crow"]
    rw = {n: T["rw_" + n] for n in ("r", "k", "v", "kk", "b", "lw")}
    yb = T["yb"]; yc = T["yc"]

    with ExitStack() as es:
        sb0, ps, sbring0 = mk_alloc(nc, es)
        _sfx = "_B" + tag
        sb_g = lambda n, shp, dt=F32: sb0(n + _sfx, shp, dt)
        sb = sb_g
        sbring = lambda name, n, shp, dt=F32: Ring(name, [sb_g("%s%d" % (name, i), shp, dt) for i in range(n)])

        def MM(out, lhsT, rhs, start=True, stop=True, r=(), w=()):
            S.op("pe", lambda e: e.matmul(out, lhsT, rhs, start=start, stop=stop), r, w)

        def MMS(out, lhsT, rhs, start=True, stop=True, r=(), w=()):
            S.op("pe", lambda e: e.matmul(out, lhsT, rhs, start=start, stop=stop, skip_group_check=True), r, w)

        def TR(out, in_, ident, r=(), w=()):
            S.op("pe", lambda e: e.transpose(out, in_, ident), r, w)

        def ACT(out, in_, func, bias=None, scale=1.0, r=(), w=()):
            if bias is None:
                S.op("act", lambda e: e.activation(out=out, in_=in_, func=func, scale=scale), r, w)
            else:
                S.op("act", lambda e: e.activation(out=out, in_=in_, func=func, bias=bias, scale=scale), r, w)

        def TT(eng, out, in0, in1, op, r=(), w=()):
            S.op(eng, lambda e: e.tensor_tensor(out=out, in0=in0, in1=in1, op=op), r, w)

        def TS(eng, out, in0, s1, s2, op0, op1=None, r=(), w=()):
            if op1 is None:
                S.op(eng, lambda e: e.tensor_scalar(out=out, in0=in0, scalar1=s1, scalar2=None, op0=op0), r, w)
            else:
                S.op(eng, lambda e: e.tensor_scalar(out=out, in0=in0, scalar1=s1, scalar2=s2, op0=op0, op1=op1), r, w)

        def STT(eng, out, in0, scalar, in1, op0, op1, r=(), w=()):
            S.op(eng, lambda e: e.scalar_tensor_tensor(out=out, in0=in0, scalar=scalar, in1=in1, op0=op0, op1=op1), r, w)

        def CP(eng, out, in_, r=(), w=()):
            if eng == "act":
                S.op("act", lambda e: e.copy(out, in_), r, w)
            else:
                S.op(eng, lambda e: e.tensor_copy(out, in_), r, w)

        def MS(eng, out, val, w=()):
            S.op(eng, lambda e: e.memset(out, val), (), w)

        def DMA(out, in_, r=(), w=(), is_output=False):
            S.dma("sp", out, in_, reads=r, writes=w, is_output=is_output)

        ident, ones, uti, uts, lts, blk, mask2 = (M[k] for k in ("ident", "ones", "uti", "uts", "lts", "blk", "mask2"))

        banks = [ps("bankB%d" % i + _sfx, [128, 512]) for i in range(8)]
        if do_attn and do_rwkv:
            cfg = ((0, 1), (2, 3), (4, 5, 6, 7))
        elif do_attn:
            cfg = ((0, 1, 2), (3, 4), (5, 6, 7))
        else:
            cfg = ((), (), (0, 1, 2, 3, 4, 5, 6, 7))
        st_ring = Ring("st", [banks[b][:, 0:512] for b in cfg[0]])
        o_ring = Ring("o", [banks[b][:, 0:512] for b in cfg[1]])
        g_ring = Ring("g", [banks[b][:, 0:512] for b in cfg[2]])
        S.excl.update(["st", "o", "g"])

        def attn_gen():
            QTa = [sb("QTa%d" % h, [67, SEQ], BF16) for h in range(2)]
            KTa = [sb("KTa%d" % h, [67, SEQ], BF16) for h in range(2)]
            VP = sb("VP", [128, NB, 2, 66], BF16)
            LF = sb("LF", [128, 2 * NB])
            Csb = sb("Csb", [128, 2 * NB])
            NC = sb("NC", [128, 2 * NB])
            TOTT = sb("TOTT", [128, 128])
            CH = 2048 if SEQ >= 2048 else SEQ
            for h in range(2):
                for c0 in range(0, SEQ, CH):
                    DMA(QTa[h][0:64, c0:c0 + CH], qT[h * 64:(h + 1) * 64, c0:c0 + CH], w=[("QTa", h)])
                    DMA(KTa[h][0:64, c0:c0 + CH], kT[h * 64:(h + 1) * 64, c0:c0 + CH], w=[("KTa", h)])
                TS("dve", QTa[h][0:64, :], QTa[h][0:64, :], 0.125, None, ALU.mult, r=[("QTa", h)], w=[("QTa", h)])
                MS("dve", KTa[h][64:67, :], 1.0, w=[("KTa1", h)])
            MS("dve", VP[:, :, :, 64:66], 1.0, w=["VP1"])
            vsrc = vv.rearrange("(n p) (h d) -> p n h d", p=128, h=2)
            for n0 in range(0, NB, 8):
                n1 = min(NB, n0 + 8)
                for hh in range(2):
                    DMA(VP[:, n0:n1, hh, 0:64], vsrc[:, n0:n1, hh, :], r=["VP1"], w=[("VP", n0 // 8)])
            LF3 = sb("LF3", [128, NB, 2])
            lf3 = lf.rearrange("(n p) h -> p n h", p=128)
            for n0 in range(0, NB, 16):
                n1 = min(NB, n0 + 16)
                DMA(LF3[:, n0:n1, :], lf3[:, n0:n1, :], w=[("LF3", n0)])
            for hh in range(2):
                CP("dve", LF[:, hh * NB:(hh + 1) * NB], LF3[:, :, hh], r=[("LF3", n0) for n0 in range(0, NB, 16)] + ["LF"], w=["LF"])
            cs = {n: sb("cs_" + n, [128, NB]) for n in ("hf", "r1", "lf", "r2")}
            csb = {n: sb("csb_" + n, [128, NB], BF16) for n in ("h", "l")}
            CR = sb("CR", [NB, 3, 128], BF16)
            for h in range(2):
                lfh = LF[:, h * NB:(h + 1) * NB]
                Ch = Csb[:, h * NB:(h + 1) * NB]
                g, gk = g_ring.next()
                MM(g[0:NB, 0:128], lfh, ones[:], r=["LF", "ones"], w=[gk])
                CP("dve", TOTT[0:NB, :], g[0:NB, 0:128], r=[gk], w=["TOTT"])
                g, gk = g_ring.next()
                MM(g[:, 0:NB], uti[:], lfh, start=True, stop=False, r=["LF", "uti"], w=[gk])
                MM(g[:, 0:NB], TOTT[0:NB, :], uts[0:NB, 0:NB], start=False, stop=True, r=["TOTT", "uts"], w=[gk])
                CP("dve", Ch, g[:, 0:NB], r=[gk], w=["Csb"])
                TS("dve", NC[:, h * NB:(h + 1) * NB], Ch, -1.0, None, ALU.mult, r=["Csb"], w=["NC"])
                CP("dve", csb["h"][:], Ch, r=["Csb"], w=["csb_h"])
                CP("dve", cs["hf"][:], csb["h"][:], r=["csb_h"], w=["cs_hf"])
                TT("dve", cs["r1"][:], Ch, cs["hf"][:], ALU.subtract, r=["Csb", "cs_hf"], w=["cs_r1"])
                CP("dve", csb["l"][:], cs["r1"][:], r=["cs_r1"], w=["csb_l"])
                CP("dve", cs["lf"][:], csb["l"][:], r=["csb_l"], w=["cs_lf"])
                TT("dve", cs["r2"][:], cs["r1"][:], cs["lf"][:], ALU.subtract, r=["cs_r1", "cs_lf"], w=["cs_r2"])
                for ti, nm in enumerate(("hf", "lf", "r2")):
                    g, gk = g_ring.next()
                    TR(g[0:NB, 0:128], cs[nm][:], ident[:], r=["cs_" + nm, "ident"], w=[gk])
                    CP("dve", CR[:, ti, :], g[0:NB, 0:128], r=[gk, "CR"], w=["CR"])
                DMA(crow[h].rearrange("t (n p) -> n t p", p=128), CR[:], r=["CR"], w=[("crow", h)])
                DMA(QTa[h][64:67, :], crow[h], r=[("crow", h), ("QTa", h)], w=[("QTa", h)])
            yield
            pt_ring = sbring("pt", 3, [128, 512], BF16)
            rinv_ring = sbring("rinv", 4, [128, 1])
            ybt_ring = sbring("ybt", 4, [128, 64])
            if not do_rwkv:
                S.barrier(skip_q=("pool",))
                st_ring.tiles.extend(g_ring.tiles)
            LA = len(st_ring.tiles) - 1
            NG4 = NB // 4
            for h in range(2):
                steps = [(g4, j) for g4 in range(NG4) for j in range(4 * g4 + 4)]
                st_info = {}

                def issue_st(n):
                    g4, j = steps[n]
                    a = max(0, j - 4 * g4)
                    st, stk = st_ring.next()
                    diag = j >= 4 * g4
                    MM(st[:, a * 128:512], KTa[h][0:67, j * 128:(j + 1) * 128], QTa[h][0:67, g4 * 512 + a * 128:(g4 + 1) * 512], start=True, stop=not diag,
                       r=[("QTa", h), ("KTa", h), ("KTa1", h)], w=[stk])
                    if diag:
                        MM(st[:, a * 128:(a + 1) * 128], M["identb"][:], M["negm"][:], start=False, stop=True, r=["identb", "negm"], w=[stk])
                    st_info[n] = (st, stk)

                for n in range(min(LA, len(steps))):
                    issue_st(n)
                o = ok = None
                for n, (g4, j) in enumerate(steps):
                    a = max(0, j - 4 * g4)
                    if j == 0:
                        o, ok = o_ring.next()
                    st, stk = st_info.pop(n)
                    pt, ptk = pt_ring.next()
                    ACT(pt[:, a * 128:512], st[:, a * 128:512], AF.Exp, bias=NC[:, h * NB + j:h * NB + j + 1], scale=1.0, r=[stk, "NC"], w=[ptk])
                    if n + LA < len(steps):
                        issue_st(n + LA)
                    for ii in range(a, 4):
                        i = 4 * g4 + ii
                        MMS(o[:, ii * 128:ii * 128 + 65], pt[:, ii * 128:(ii + 1) * 128], VP[:, j, h, 0:65], start=(j == 0 and ii == 0), stop=(j == i),
                            r=[ptk, "VP1", ("VP", j // 8)], w=[ok])
                    if j == 4 * g4 + 3:
                        for ii in range(4):
                            i = 4 * g4 + ii
                            rinv, rk = rinv_ring.next()
                            ybt, yk = ybt_ring.next()
                            S.op("dve", (lambda a_, b_: (lambda e: e.reciprocal(a_, b_)))(rinv[:], o[:, ii * 128 + 64:ii * 128 + 65]), [ok], [rk])
                            TS("dve", ybt[:], o[:, ii * 128:ii * 128 + 64], rinv[:, 0:1], None, ALU.mult, r=[ok, rk], w=[yk])
                            DMA(yb[i * 128:(i + 1) * 128, h * 64:(h + 1) * 64], ybt[:], r=[yk], is_output=True)
                    yield

        def rwkv_gen():
            NL = 3
            names = ("r", "k", "v", "kk", "b", "lw")
            ld = {n: sbring("ld_" + n, NL, [128, 128]) for n in names}
            vpad = [sbring("vpad%d" % h, NL, [128, 128]) for h in range(2)]
            ST = sb("ST", [128, 128])
            MS("dve", ST[:], 0.0, w=["ST"])
            for h in range(2):
                for ti, t in enumerate(vpad[h].tiles):
                    MS("dve", t[:], 0.0, w=[(vpad[h].name, ti)])
            ahpad = [sbring("ahpad%d" % h, 2, [128, 128]) for h in range(2)]
            nutpad = [sbring("nutpad%d" % h, 2, [128, 128]) for h in range(2)]
            for h in range(2):
                for rg in (ahpad[h], nutpad[h]):
                    for ti, t in enumerate(rg.tiles):
                        MS("dve", t[:], 0.0, w=[(rg.name, ti)])
            ahcat_r = sbring("ahcat", 2, [128, 128])
            nutcat_r = sbring("nutcat", 2, [128, 128])
            R = {n: sbring("t_" + n, 2, [128, 128]) for n in
                 ("Lsb", "eL", "enL", "t1", "eLm", "t2", "eLCL", "al", "be", "ka", "rho", "bep", "kap", "RhT", "mtmp", "MT", "Ysb")}
            pC_r = sbring("pC", 2, [128, 1])
            ARt_r = sbring("ARt", 2, [128, 256])
            BKt_r = sbring("BKt", 2, [128, 256])
            AAr_r = [sbring("AAr%d" % h, 2, [128, 256]) for h in range(2)]
            BBr_r = [sbring("BBr%d" % h, 2, [128, 256]) for h in range(2)]
            P_r = [sbring("P2_%d" % par, 2, [128, 256]) for par in range(2)]
            PT_r = [sbring("PT2_%d" % par, 2, [128, 256]) for par in range(2)]
            X_r = [sbring("X2_%d" % par, 2, [128, 256]) for par in range(2)]
            loaded = {}

            def issue_loads(c):
                d = {}
                rows = slice(c * 128, (c + 1) * 128)
                def ldma(dst, src, r0_, r1_, c0_, c1_, k):
                    if callable(src):
                        S.dma("sp", None, None, writes=[k], custom=lambda e: e.dma_start(out=dst, in_=src(e)[r0_:r1_, c0_:c1_]))
                    else:
                        DMA(dst, src[r0_:r1_, c0_:c1_], w=[k])
                for n in names:
                    t, k = ld[n].next()
                    ldma(t[:], rw[n], c * 128, (c + 1) * 128, 0, 128, k)
                    d[n] = (t, k)
                for h in range(2):
                    t, k = vpad[h].next()
                    ldma(t[:, h * 64:(h + 1) * 64], rw["v"], c * 128, (c + 1) * 128, h * 64, (h + 1) * 64, k)
                    d["vpad%d" % h] = (t, k)
                loaded[c] = d

            def chunk_gen(c):
                par = c % 2
                d = loaded.pop(c)
                (r_, rk), (k_, kk_k), (v_, vk), (kk_, kkk), (b_, bk), (lw_, lwk) = (d[n] for n in names)
                g1, g1k = g_ring.next()
                MM(g1[:, 0:128], uti[:], lw_[:], r=["uti", lwk], w=[g1k])
                MM(g1[:, 128:256], ones[:], lw_[:], r=["ones", lwk], w=[g1k])
                MM(g1[:, 256:257], lw_[:], ones[:, 0:1], r=["ones", lwk], w=[g1k])
                Lsb, Lk = R["Lsb"].next(); eL, eLk = R["eL"].next(); enL, enLk = R["enL"].next()
                t1, t1k = R["t1"].next(); eLm, eLmk = R["eLm"].next(); t2, t2k = R["t2"].next(); eLCL, eLCLk = R["eLCL"].next()
                pC, pCk = pC_r.next()
                CP("act", Lsb[:], g1[:, 0:128], r=[g1k], w=[Lk])
                ACT(eL[:], g1[:, 0:128], AF.Exp, r=[g1k], w=[eLk])
                ACT(enL[:], g1[:, 0:128], AF.Exp, scale=-1.0, r=[g1k], w=[enLk])
                ACT(pC[:], g1[:, 256:257], AF.Exp, r=[g1k], w=[pCk])
                TT("dve", t2[:], g1[:, 128:256], Lsb[:], ALU.subtract, r=[g1k, Lk], w=[t2k])
                TT("dve", t1[:], Lsb[:], lw_[:], ALU.subtract, r=[Lk, lwk], w=[t1k])
                ACT(eLm[:], t1[:], AF.Exp, r=[t1k], w=[eLmk])
                ACT(eLCL[:], t2[:], AF.Exp, r=[t2k], w=[eLCLk])
                yield
                al, alk = R["al"].next(); be, bek = R["be"].next(); ka, kak = R["ka"].next()
                rho, rhok = R["rho"].next(); bep, bepk = R["bep"].next(); kap, kapk = R["kap"].next()
                TT("dve", al[:], kk_[:], eLm[:], ALU.mult, r=[kkk, eLmk], w=[alk])
                TT("dve", be[:], b_[:], enL[:], ALU.mult, r=[bk, enLk], w=[bek])
                TT("dve", ka[:], k_[:], enL[:], ALU.mult, r=[kk_k, enLk], w=[kak])
                TT("dve", rho[:], r_[:], eL[:], ALU.mult, r=[rk, eLk], w=[rhok])
                TT("dve", bep[:], b_[:], eLCL[:], ALU.mult, r=[bk, eLCLk], w=[bepk])
                TT("dve", kap[:], k_[:], eLCL[:], ALU.mult, r=[kk_k, eLCLk], w=[kapk])
                ARt, ARk = ARt_r.next(); BKt, BKk = BKt_r.next()
                g, gk = g_ring.next()
                TR(g[:, 0:128], al[:], ident[:], r=[alk, "ident"], w=[gk])
                TR(g[:, 128:256], rho[:], ident[:], r=[rhok, "ident"], w=[gk])
                TR(g[:, 256:384], be[:], ident[:], r=[bek, "ident"], w=[gk])
                TR(g[:, 384:512], ka[:], ident[:], r=[kak, "ident"], w=[gk])
                CP("act", ARt[:], g[:, 0:256], r=[gk], w=[ARk])
                CP("dve", BKt[:], g[:, 256:512], r=[gk], w=[BKk])
                yield
                P2, P2k = P_r[par].next()
                X2, X2k = X_r[par].next()
                hd = []
                for h in range(2):
                    hp = slice(h * 64, h * 64 + 64)
                    AAr, AArk = AAr_r[h].next(); BBr, BBrk = BBr_r[h].next()
                    g, gk = g_ring.next()
                    MM(g[:, 0:256], BKt[hp, 0:128], ARt[hp, 0:256], r=[BKk, ARk], w=[gk])
                    MM(g[:, 256:512], BKt[hp, 128:256], ARt[hp, 0:256], r=[BKk, ARk], w=[gk])
                    TT("dve", AAr[:], g[:, 0:256], mask2[:], ALU.mult, r=[gk, "mask2"], w=[AArk])
                    TT("dve", BBr[:], g[:, 256:512], mask2[:], ALU.mult, r=[gk, "mask2"], w=[BBrk])
                    hd.append(dict(AAr=AAr, AArk=AArk, BBr=BBr, BBrk=BBrk))
                g, gk = g_ring.next()
                for h in range(2):
                    hp = slice(h * 64, h * 64 + 64)
                    MM(g[:, h * 128:(h + 1) * 128], ARt[hp, 0:128], BKt[hp, 0:128], r=[BKk, ARk], w=[gk])
                    MM(g[:, 256 + h * 64:256 + (h + 1) * 64], hd[h]["BBr"][:, 0:128], v_[:, h * 64:(h + 1) * 64], r=[hd[h]["BBrk"], vk], w=[gk])
                for h in range(2):
                    TT("dve", P2[:, h * 128:(h + 1) * 128], g[:, h * 128:(h + 1) * 128], lts[:], ALU.mult, r=[gk, "lts", P2k], w=[P2k])
                    CP("act", X2[:, h * 128:h * 128 + 64], al[:, h * 64:(h + 1) * 64], r=[alk, X2k], w=[X2k])
                    CP("act", X2[:, h * 128 + 64:(h + 1) * 128], g[:, 256 + h * 64:256 + (h + 1) * 64], r=[gk, X2k], w=[X2k])
                yield
                PT = [hd[0]["AAr"][:, 0:128], hd[1]["AAr"][:, 0:128]]
                PTk = [hd[0]["AArk"], hd[1]["AArk"]]
                g, gk = g_ring.next()
                for h in range(2):
                    MM(g[:, h * 128:(h + 1) * 128], PT[h], X2[:, h * 128:(h + 1) * 128], r=[PTk[h], X2k], w=[gk])
                Xn, Xnk = X_r[par].next()
                TT("dve", Xn[:], X2[:], g[:, 0:256], ALU.subtract, r=[X2k, gk], w=[Xnk])
                X2, X2k = Xn, Xnk
                yield
                for lev in range(1, 7):
                    g, gk = g_ring.next()
                    for h in range(2):
                        MM(g[:, h * 128:(h + 1) * 128], P2[:, h * 128:(h + 1) * 128], PT[h], r=[P2k] + PTk, w=[gk])
                        if lev < 6:
                            MM(g[:, 256 + h * 128:256 + (h + 1) * 128], PT[h], P2[:, h * 128:(h + 1) * 128], r=[P2k] + PTk, w=[gk])
                    P2T, P2Tk = PT_r[par].next()
                    CP("act", P2T[:], g[:, 0:256], r=[gk], w=[P2Tk])
                    if lev < 6:
                        P2n, P2nk = P_r[par].next()
                        CP("dve", P2n[:], g[:, 256:512], r=[gk], w=[P2nk])
                    g3, g3k = g_ring.next()
                    for h in range(2):
                        MM(g3[:, h * 128:(h + 1) * 128], P2T[:, h * 128:(h + 1) * 128], X2[:, h * 128:(h + 1) * 128], r=[P2Tk, X2k], w=[g3k])
                    Xn, Xnk = X_r[par].next()
                    TT("dve", Xn[:], X2[:], g3[:, 0:256], ALU.add, r=[X2k, g3k], w=[Xnk])
                    X2, X2k = Xn, Xnk
                    PT = [P2T[:, 0:128], P2T[:, 128:256]]
                    PTk = [P2Tk, P2Tk]
                    if lev < 6:
                        P2, P2k = P2n, P2nk
                    yield
                ahcat, ahck = ahcat_r.next(); nutcat, nuck = nutcat_r.next()
                pads = []
                for h in range(2):
                    hc = slice(h * 64, h * 64 + 64)
                    ap_, apk = ahpad[h].next(); npd, npk = nutpad[h].next()
                    CP("act", ahcat[:, hc], X2[:, h * 128:h * 128 + 64], r=[X2k, ahck], w=[ahck])
                    CP("act", ap_[:, hc], X2[:, h * 128:h * 128 + 64], r=[X2k], w=[apk])
                    ACT(nutcat[:, hc], X2[:, h * 128 + 64:(h + 1) * 128], AF.Identity, scale=-1.0, r=[X2k, nuck], w=[nuck])
                    ACT(npd[:, hc], X2[:, h * 128 + 64:(h + 1) * 128], AF.Identity, scale=-1.0, r=[X2k], w=[npk])
                    pads.append((ap_, apk, npd, npk))
                g, gk = g_ring.next()
                for h in range(2):
                    MM(g[:, 0:128], pads[h][0][:], hd[h]["AAr"][:, 128:256], start=(h == 0), stop=(h == 1),
                       r=[pads[h][1], hd[h]["AArk"]], w=[gk])
                MMS(g[:, 128:256], ahcat[:], bep[:], r=[ahck, bepk], w=[gk])
                RhT, RhTk = R["RhT"].next()
                TT("dve", RhT[:], ARt[:, 128:256], g[:, 0:128], ALU.subtract, r=[ARk, gk], w=[RhTk])
                mtmp, mtk = R["mtmp"].next(); MT, MTk = R["MT"].next()
                TT("dve", mtmp[:], g[:, 128:256], blk[:], ALU.mult, r=[gk, "blk"], w=[mtk])
                STT("dve", MT[:], ident[:], pC[:, 0:1], mtmp[:], ALU.mult, ALU.subtract, r=["ident", pCk, mtk], w=[MTk])
                yield
                g, gk = g_ring.next()
                MM(g[:, 0:128], RhT[:], ST[:], start=True, stop=False, r=[RhTk, "ST"], w=[gk])
                for h in range(2):
                    vp, vpk = d["vpad%d" % h]
                    MM(g[:, 0:128], hd[h]["BBr"][:, 128:256], vp[:], start=False, stop=False, r=[hd[h]["BBrk"], vpk], w=[gk])
                    MM(g[:, 0:128], hd[h]["AAr"][:, 128:256], pads[h][2][:], start=False, stop=(h == 1),
                       r=[hd[h]["AArk"], pads[h][3]], w=[gk])
                MMS(g[:, 128:256], MT[:], ST[:], start=False, stop=False, r=[MTk, "ST"], w=[gk])
                MMS(g[:, 128:256], kap[:], v_[:], start=False, stop=False, r=[kapk, vk], w=[gk])
                MMS(g[:, 128:256], bep[:], nutcat[:], start=False, stop=True, r=[bepk, nuck], w=[gk])
                Ysb, Yk = R["Ysb"].next()
                CP("act", Ysb[:], g[:, 0:128], r=[gk], w=[Yk])
                TT("dve", ST[:], g[:, 128:256], blk[:], ALU.mult, r=[gk, "blk"], w=["ST"])
                DMA(yc[c * 128:(c + 1) * 128, :], Ysb[:], r=[Yk], w=[("ycd", c // 16)], is_output=True)
                if after_chunk is not None:
                    after_chunk(c)
                yield

            NSTEP = 12
            issue_loads(0)
            if NB > 1:
                issue_loads(1)
            active = []
            nxt = 0
            rounds = 0
            while nxt < NB or active:
                MAXFLY = int(os.environ.get("MAXFLY", "2"))
                if nxt < NB and (len(active) == 0 or (len(active) < MAXFLY and active[-1][1] >= NSTEP // MAXFLY)):
                    if nxt + 1 < NB and nxt >= 1:
                        issue_loads(nxt + 1)
                    active.append([chunk_gen(nxt), 0])
                    nxt += 1
                for item in list(active):
                    try:
                        next(item[0])
                        item[1] += 1
                    except StopIteration:
                        active.remove(item)
                rounds += 1
                yield

        import os
        DUM = os.environ.get("DUMMY", "")
        def dummy_gen():
            dA = sb("dumA", [128, 128]); dB = sb("dumB", [128, 128]); dC = sb("dumC", [128, 128], BF16)
            MS("pool", dA[:], 0.5, w=["dumA"]); MS("pool", dB[:], 0.0, w=["dumB"]); MS("pool", dC[:], 0.5, w=["dumC"])
            for it in range(NB):
                if "pe" in DUM:
                    g, gk = g_ring.next()
                    MM(g[:, 0:128], dA[:], dA[:], r=["dumA"], w=[gk])
                    MM(g[:, 128:256], dA[:], dA[:], r=["dumA"], w=[gk])
                if "bfmm" in DUM:
                    g, gk = g_ring.next()
                    MM(g[:, 0:128], dC[:], dC[:], r=["dumC"], w=[gk])
                if "act" in DUM:
                    ACT(dB[:], dA[:], AF.Exp, scale=-1.0, r=["dumA"], w=["dumB"])
                    CP("act", dB[:], dA[:], r=["dumA"], w=["dumB"])
                if "dve" in DUM:
                    TT("dve", dB[:], dA[:], dA[:], ALU.mult, r=["dumA"], w=["dumB"])
                if "pool" in DUM:
                    TT("pool", dB[:], dA[:], dA[:], ALU.mult, r=["dumA"], w=["dumB"])
                if "dma" in DUM:
                    DMA(dB[:], rw["r"][0:128, :], w=["dumB"])
                yield
        gens = []
        if DUM:
            do_rwkv = False
        if do_attn:
            gens.append(attn_gen())
        if do_rwkv:
            gens.append(rwkv_gen())
        if DUM:
            gens.append(dummy_gen())
        if len(gens) == 2 and os.environ.get("ILV"):
            ga, gr = gens
            na = 2 * sum(4 * g4 + 4 for g4 in range(NB // 4)) + 1
            nr = NB * 7 + 8
            next(ga, None)
            acc = 0.0
            done_a = False
            for _ in gr:
                acc += na / nr
                while acc >= 1.0 and not done_a:
                    acc -= 1.0
                    if next(ga, "END") == "END":
                        done_a = True
            if not done_a:
                for _ in ga:
                    pass
        else:
            for gen in gens:
                for _ in gen:
                    pass
        S.barrier()


NORM_EPS = 1e-6
GN_EPS = 64e-5
DFF = 2816
NFC = 22


def emit_C(nc, S, M, T, last, NT=16, NPH=2):
    TOKF = NT * 128
    NT = NT // NPH
    TOK = NT * 128
    NG = TOK // 512
    xF = T["x"]; ybF = T["yb"]; ycF = T["yc"]; gF = T["g"]; bonF = T["bonus"]
    maF = T["maT"]; gbF = T["gbT"]; gcF = T["gcT"]
    p_b = T["p_b"]; p_c = T["p_c"]; w_out = T["w_out"]
    lnx_g = T["c_lnx_g"]; lnx_b = T["c_lnx_b"]; nffn = T["norm_ffn"]
    w_gu = T["w_gate_up"]; w_dn = T["w_down"]
    if last:
        nfin = T["norm_final"]
    outF = T["xo"]
    HSV = lambda ap: ap.rearrange("p (h c) -> p h c", h=4)

    with ExitStack() as es:
        sb0, ps, sbring0 = mk_alloc(nc, es)
        _sfx = "_C%d" % (1 if last else 0)
        sb_g = lambda n, shp, dt=F32: sb0(n + _sfx, shp, dt)
        sb = sb_g
        sbring = lambda name, n, shp, dt=F32: Ring(name, [sb_g("%s%d" % (name, i), shp, dt) for i in range(n)])
        O = mkops(S)
        MM, TR, ACT, TT, TS, STT, CP, MS, RED, RECIP, DMA = O.MM, O.TR, O.ACT, O.TT, O.TS, O.STT, O.CP, O.MS, O.RED, O.RECIP, O.DMA
        identb = M["identb"]
        tp_ring = Ring("tp", [ps("tpbC%d" % i + _sfx, [128, 1024], BF16)[:] for i in range(2)])
        mm_ring = Ring("mm", [ps("mmbC%d" % i + _sfx, [128, 512])[:] for i in range(6)])
        S.excl.update(["tp", "mm"])

        def bc_load(sbf, name, vec_ap, n):
            t = sbf(name, [128, n])
            DMA(t[:], vec_ap.partition_broadcast(128), w=[name])
            S.const.add(name)
            return t

        LXG = bc_load(sb, "LXG", lnx_g, 512); LXB = bc_load(sb, "LXB", lnx_b, 512); NFb = bc_load(sb, "NFb", nffn, 1024)
        if last:
            NLb = bc_load(sb, "NLb", nfin, 1024)
        stg_ring = sbring("stg", 2, [128, 8, 256])
        sm_ring = sbring("sm", 4, [128, 8])
        X1 = sb("X1", [128, NT, 1024])
        WD = sb("WD", [128, NFC, 1024], BF16)
        junk = sb("junk", [128, 1024], BF16)

        cast_i = [0]

        def load_cast(src_ap3, dst_ap3, nk, ncols, dkey):
            stg, sk = stg_ring.next()
            DMA(stg[:, 0:nk, 0:ncols], src_ap3, w=[sk])
            cast_i[0] += 1
            CP("act" if cast_i[0] % 2 else "dve", dst_ap3, stg[:, 0:nk, 0:ncols], r=[sk, dkey], w=[dkey])

        v8 = lambda ap: ap.rearrange("p (h d) -> p h d", h=8)
        bc8 = lambda ap: ap.unsqueeze(2).to_broadcast([128, 8, 64])

        for ph in range(NPH):
            r0 = ph * TOK
            with stage(nc, S) as (lsb, lring):
                PB = lsb("PB", [128, 4, 1024], BF16); PC = lsb("PC", [128, 4, 1024], BF16); WO = lsb("WO", [128, 8, 1024], BF16)
                ld = {n: lring("ld_" + n, 2, [128, 512]) for n in ("yc", "g", "bon", "yb")}
                tmp_r = lring("tmpc", 2, [128, 512])
                ybf_r = lring("ybf", 2, [128, 1024], BF16)
                yT_r = lring("yT", 1, [128, 8, 512], BF16)
                mT_r = lring("mT", 1, [128, 8, 512], BF16)
                mg_r = {n: lring("mg_" + n, 2, [128, 512]) for n in ("ma", "gb", "gc")}
                xt_r = lring("xt", 2, [128, 1024])
                for tg in range(NG):
                    yT, yTk = yT_r.next()
                    for tt in range(4):
                        t = tg * 4 + tt
                        rows = slice(r0 + t * 128, r0 + (t + 1) * 128)
                        yc_, yck = ld["yc"].next(); g_, gk = ld["g"].next(); bo_, bok = ld["bon"].next(); yb_, ybk = ld["yb"].next()
                        DMA(HSV(yc_[:]), ycF[rows], w=[yck]); DMA(g_[:], gF[rows, :], w=[gk]); DMA(bo_[:], bonF[rows, :], w=[bok]); DMA(HSV(yb_[:]), ybF[rows], w=[ybk])
                        sm, smk = sm_ring.next(); sm2, sm2k = sm_ring.next()
                        tmp, tk = tmp_r.next()
                        RED("dve", sm[:, 0:8], v8(yc_[:]), ALU.add, r=[yck], w=[smk])
                        TS("dve", sm[:, 0:8], sm[:, 0:8], 1.0 / 64, None, ALU.mult, r=[smk], w=[smk])
                        TT("dve", v8(yc_[:]), v8(yc_[:]), bc8(sm[:, 0:8]), ALU.subtract, r=[yck, smk], w=[yck])
                        TT("dve", tmp[:], yc_[:], yc_[:], ALU.mult, r=[yck], w=[tk])
                        RED("dve", sm2[:, 0:8], v8(tmp[:]), ALU.add, r=[tk], w=[sm2k])
                        ACT(sm2[:, 0:8], sm2[:, 0:8], AF.Sqrt, bias=GN_EPS, scale=1.0 / 64, r=[sm2k], w=[sm2k])
                        RECIP(sm2[:, 0:8], sm2[:, 0:8], r=[sm2k], w=[sm2k])
                        TT("dve", v8(yc_[:]), v8(yc_[:]), bc8(sm2[:, 0:8]), ALU.mult, r=[yck, sm2k], w=[yck])
                        TT("dve", yc_[:], yc_[:], LXG[:], ALU.mult, r=[yck, "LXG"], w=[yck])
                        TT("dve", yc_[:], yc_[:], LXB[:], ALU.add, r=[yck, "LXB"], w=[yck])
                        TT("dve", yc_[:], yc_[:], bo_[:], ALU.add, r=[yck, bok], w=[yck])
                        ybf, ybfk = ybf_r.next()
                        TT("dve", ybf[:, 0:512], yc_[:], g_[:], ALU.mult, r=[yck, gk], w=[ybfk])
                        CP("act", ybf[:, 512:1024], yb_[:], r=[ybk, ybfk], w=[ybfk])
                        tp, tpk = tp_ring.next()
                        for c in range(8):
                            TR(tp[:, c * 128:(c + 1) * 128], ybf[:, c * 128:(c + 1) * 128], identb[:], r=[ybfk, "identb"], w=[tpk])
                        CP("act" if tt % 2 else "dve", yT[:, :, tt * 128:(tt + 1) * 128], tp.rearrange("p (c n) -> p c n", c=8), r=[tpk, yTk], w=[yTk])
                    if tg == 0:
                        for (W, src, nk, key) in ((PB, p_b, 4, "PB"), (PC, p_c, 4, "PC"), (WO, w_out, 8, "WO")):
                            for c0 in range(0, 1024, 256):
                                load_cast(src[:, c0:c0 + 256].rearrange("(k p) c -> p k c", p=128), W[:, :, c0:c0 + 256], nk, 256, key)
                    mT, mTk = mT_r.next()
                    cols = slice(r0 + tg * 512, r0 + (tg + 1) * 512)
                    for n in range(8):
                        ma, mak = mg_r["ma"].next(); gb, gbk = mg_r["gb"].next(); gc, gck = mg_r["gc"].next()
                        DMA(ma[:], maF[n * 128:(n + 1) * 128, cols], w=[mak]); DMA(gb[:], gbF[n * 128:(n + 1) * 128, cols], w=[gbk]); DMA(gc[:], gcF[n * 128:(n + 1) * 128, cols], w=[gck])
                        pb, pbk = mm_ring.next()
                        for c in range(4):
                            MM(pb[:, :], PB[:, c, n * 128:(n + 1) * 128], yT[:, 4 + c, :], start=(c == 0), stop=(c == 3), r=["PB", yTk], w=[pbk])
                        pc, pck = mm_ring.next()
                        for c in range(4):
                            MM(pc[:, :], PC[:, c, n * 128:(n + 1) * 128], yT[:, c, :], start=(c == 0), stop=(c == 3), r=["PC", yTk], w=[pck])
                        TT("dve", gb[:], gb[:], pb[:, :], ALU.mult, r=[gbk, pbk], w=[gbk])
                        TT("dve", gc[:], gc[:], pc[:, :], ALU.mult, r=[gck, pck], w=[gck])
                        TT("dve", ma[:], ma[:], gb[:], ALU.add, r=[mak, gbk], w=[mak])
                        TT("dve", mT[:, n, :], ma[:], gc[:], ALU.add, r=[mak, gck, mTk], w=[mTk])
                    for tt in range(4):
                        t = tg * 4 + tt
                        rows = slice(r0 + t * 128, r0 + (t + 1) * 128)
                        xt, xk = xt_r.next()
                        DMA(xt[:], xF[rows, :], w=[xk])
                        for half in range(2):
                            pm, pk = mm_ring.next()
                            for n in range(8):
                                MM(pm[:, :], mT[:, n, tt * 128:(tt + 1) * 128], WO[:, n, half * 512:(half + 1) * 512], start=(n == 0), stop=(n == 7), r=[mTk, "WO"], w=[pk])
                            TT("dve", X1[:, t, half * 512:(half + 1) * 512], xt[:, half * 512:(half + 1) * 512], pm[:, :], ALU.add, r=[xk, pk, ("X1", t)], w=[("X1", t)])
            with stage(nc, S) as (lsb, lring):
                H2 = lsb("H2", [128, 8, TOK], BF16)
                AT = lsb("AT", [128, NFC, TOK], BF16)
                xh_r = lring("xh", 2, [128, 1024], BF16)
                def h2_norm(t):
                    sm, smk = sm_ring.next()
                    MS("pool", sm[:], 0.0, w=[smk])
                    ACT(junk[:], X1[:, t, :], AF.Square, accum=sm[:, 0:1], r=[("X1", t), smk], w=["junk", smk])
                    ACT(sm[:, 1:2], sm[:, 0:1], AF.Sqrt, bias=NORM_EPS, scale=1.0 / 1024, r=[smk], w=[smk])
                    RECIP(sm[:, 2:3], sm[:, 1:2], r=[smk], w=[smk])
                    xh, xhk = xh_r.next()
                    STT("dve", xh[:], X1[:, t, :], sm[:, 2:3], NFb[:], ALU.mult, ALU.mult, r=[("X1", t), smk, "NFb"], w=[xhk])
                    return xh, xhk

                def h2_tr(t, xh, xhk):
                    tp, tpk = tp_ring.next()
                    for c_ in range(8):
                        TR(tp[:, c_ * 128:(c_ + 1) * 128], xh[:, c_ * 128:(c_ + 1) * 128], identb[:], r=[xhk, "identb"], w=[tpk])
                    CP("act" if t % 2 else "dve", H2[:, :, t * 128:(t + 1) * 128], tp.rearrange("p (c n) -> p c n", c=8), r=[tpk, "H2"], w=["H2"])

                nx = h2_norm(0)
                for t in range(NT):
                    cur = nx
                    if t + 1 < NT:
                        nx = h2_norm(t + 1)
                    h2_tr(t, *cur)
                if ph == 0:
                    for fc in range(NFC):
                        for c0 in range(0, 1024, 256):
                            load_cast(w_dn[fc * 128:(fc + 1) * 128, c0:c0 + 256].rearrange("p (o c) -> p o c", o=1), WD[:, fc:fc + 1, c0:c0 + 256], 1, 256, "WD")
                wgu_r = lring("wgu", 3, [128, 8, 256], BF16)
                sg_r = lring("sg", 2, [128, 512])
                def ffn_load(fc):
                    wg, wgk = wgu_r.next()
                    load_cast(w_gu[:, fc * 128:(fc + 1) * 128].rearrange("(k p) c -> p k c", p=128), wg[:, :, 0:128], 8, 128, wgk)
                    load_cast(w_gu[:, DFF + fc * 128:DFF + (fc + 1) * 128].rearrange("(k p) c -> p k c", p=128), wg[:, :, 128:256], 8, 128, wgk)
                    return wg, wgk
                wg_next = ffn_load(0)
                for fc in range(NFC):
                    wg, wgk = wg_next
                    if fc + 1 < NFC:
                        wg_next = ffn_load(fc + 1)
                    for tg in range(NG):
                        pg, pgk = mm_ring.next()
                        for k in range(8):
                            MM(pg[:, :], wg[:, k, 0:128], H2[:, k, tg * 512:(tg + 1) * 512], start=(k == 0), stop=(k == 7), r=[wgk, "H2"], w=[pgk])
                        pu, puk = mm_ring.next()
                        for k in range(8):
                            MM(pu[:, :], wg[:, k, 128:256], H2[:, k, tg * 512:(tg + 1) * 512], start=(k == 0), stop=(k == 7), r=[wgk, "H2"], w=[puk])
                        sg, sgk = sg_r.next()
                        ACT(sg[:], pg[:, :], AF.Silu, r=[pgk], w=[sgk])
                        TT("dve", AT[:, fc, tg * 512:(tg + 1) * 512], sg[:], pu[:, :], ALU.mult, r=[sgk, puk, ("AT", tg)], w=[("AT", tg)])
                ot_r = lring("ot", 2, [128, 1024])
                for t in range(NT):
                    rows = slice(r0 + t * 128, r0 + (t + 1) * 128)
                    ot, otk = ot_r.next()
                    for half in range(2):
                        pm, pk = mm_ring.next()
                        for fc in range(NFC):
                            MM(pm[:, :], AT[:, fc, t * 128:(t + 1) * 128], WD[:, fc, half * 512:(half + 1) * 512], start=(fc == 0), stop=(fc == NFC - 1),
                               r=[("AT", t // 4), "WD"], w=[pk])
                        TT("dve", ot[:, half * 512:(half + 1) * 512], X1[:, t, half * 512:(half + 1) * 512], pm[:, :], ALU.add, r=[("X1", t), pk, otk], w=[otk])
                    if last:
                        sm, smk = sm_ring.next()
                        MS("pool", sm[:], 0.0, w=[smk])
                        ACT(junk[:], ot[:], AF.Square, accum=sm[:, 0:1], r=[otk, smk], w=["junk", smk])
                        ACT(sm[:, 1:2], sm[:, 0:1], AF.Sqrt, bias=NORM_EPS, scale=1.0 / 1024, r=[smk], w=[smk])
                        RECIP(sm[:, 2:3], sm[:, 1:2], r=[smk], w=[smk])
                        STT("dve", ot[:], ot[:], sm[:, 2:3], NLb[:], ALU.mult, ALU.mult, r=[otk, smk, "NLb"], w=[otk])
                    DMA(outF[rows, :], ot[:], r=[otk], is_output=last)
        S.barrier()


WSHAPES = {"w_in": [1024, 8456], "norm_mix": [1024], "gate_bias": [3, 1024], "a_ln_g": [1024], "a_ln_b": [1024], "a_w_s": [4, 128, 128],
           "a_b_s": [4, 128], "b_f_bias": [8], "c_mu": [1792], "c_w0": [512], "c_w_up": [64, 512], "c_a0": [512], "c_a_up": [64, 512],
           "c_g_up": [128, 512], "c_k_k": [512], "c_k_a": [512], "c_r_k": [512], "c_lnx_g": [512], "c_lnx_b": [512], "p_a": [1024, 1024],
           "p_b": [512, 1024], "p_c": [512, 1024], "w_out": [1024, 1024], "norm_ffn": [1024], "w_gate_up": [1024, 5632], "w_down": [2816, 1024]}
W1SHAPES = {"c_v0": [512], "c_v_down": [512, 32], "c_v_up": [32, 512]}
RG = [[0, 1, 2, 3], [4, 5, 6, 7]]


import os


def build_fused(do_A=True, do_B=True, do_C=True, do_X=True, nlayers=2):
    nc = bass.Bass("TRN2", target_bir_lowering=False)
    TPC, SEQ = 2048, 8192

    def ext(n, shp, dt=F32):
        return nc.dram_tensor(n, shp, dt, kind="ExternalInput").ap()

    def itn(n, shp, dt=F32):
        return nc.dram_tensor(n, shp, dt).ap()

    x_ext = ext("x", [TPC, 1024]); xprev_ext = ext("xprev", [2, 128, 1024])
    W = [{n: ext("%s_%d" % (n, l), shp) for n, shp in WSHAPES.items()} for l in range(2)]
    for n, shp in W1SHAPES.items():
        W[1][n] = ext(n + "_1", shp)
    nfin = ext("norm_final", [1024])
    out_ext = nc.dram_tensor("out", [TPC, 1024], F32, kind="ExternalOutput").ap()
    RW = ("r", "k", "v", "kk", "b", "lw")
    qk_s = itn("qk_s", [1024, TPC], BF16); vv_s = itn("vv_s", [4 * TPC, 128], BF16); lf_s = itn("lf_s", [4 * TPC, 2])
    rw_s = [{n: itn("rw%s_s%d" % (n, l), [4 * TPC, 128]) for n in RW} for l in range(2)]
    g_s = itn("g_s", [TPC, 512]); bonus_s = itn("bonus_s", [TPC, 512])
    maT_s = itn("maT_s", [1024, TPC]); gbT_s = itn("gbT_s", [1024, TPC]); gcT_s = itn("gcT_s", [1024, TPC])
    G_qk = itn("G_qk", [4096, TPC], BF16); G_v = itn("G_v", [16 * TPC, 128], BF16); G_lf = itn("G_lf", [16 * TPC, 2])
    G_rw = {n: itn("G_rw" + n, [16 * TPC, 128]) for n in RW}
    qT_l = itn("qT_l", [128, SEQ], BF16); kT_l = itn("kT_l", [128, SEQ], BF16); vv_l = itn("vv_l", [SEQ, 128], BF16); lf_l = itn("lf_l", [SEQ, 2])
    rw_l = {n: itn("rw%s_l" % n, [SEQ, 128]) for n in RW}
    crow_i = itn("crow_i", [2, 3, SEQ], BF16)
    yb_s = itn("yb_s", [SEQ, 128]); yc_s = itn("yc_s", [SEQ, 128]); G_yb = itn("G_yb", [4 * SEQ, 128]); G_yc = itn("G_yc", [4 * SEQ, 128])
    yb_l = itn("yb_l", [4 * TPC, 128]); yc_l = itn("yc_l", [4 * TPC, 128])
    X1buf = itn("X1buf", [TPC, 1024]); XPV1 = itn("XPV1", [2, 128, 1024]); LASTROW = itn("LASTROW", [1, 1024]); GLR = itn("GLR", [4, 1024]); XP5 = itn("XP5", [5, 1024])

    with ExitStack() as es:
        S = Sched(nc, es)
        sb, ps, sbring = mk_alloc(nc, es)
        M = build_masks(S, sb)
        ZT = sb("ZT", [128, 1024])
        S.op("pool", lambda e: e.memset(ZT[:], 0.0), (), ["ZT"])

        def allgather(src2d, dst2d, reads=()):
            tok = S.dma("pool", None, None, reads=reads, custom=lambda e: e.collective_compute("AllGather", ALU.bypass, replica_groups=RG, ins=[src2d], outs=[dst2d]), inc=1)
            if not os.environ.get("CCPIPE"):
                S.prog["pool"].append(("wait", tok[0], tok[1]))
                S.waited["pool"][id(tok[0])] = tok[1]

        dynv = {}

        def dyn_val(e, mult):
            if mult not in dynv:
                dynv[mult] = e.snap((e.partition_id() % 4) * mult, min_val=0, max_val=3 * mult)
            return dynv[mult]

        def dyn_copy(dst_ap, G, base, mult, nrows, q="sp"):
            def f(e):
                if (q, mult) not in dynv:
                    dynv[(q, mult)] = e.snap((e.partition_id() % 4) * mult, min_val=0, max_val=3 * mult)
                return e.dma_start(out=dst_ap, in_=G[base:, :][bass.ds(dynv[(q, mult)], nrows), :])
            S.dma(q, None, None, custom=f)

        def copy(dst_ap, src_ap):
            S.dma("sp", dst_ap, src_ap)

        hview = lambda ap: ap.rearrange("(h t) c -> t h c", h=4)
        for l in range(nlayers):
            TA = dict(W[l])
            TA["nmix"] = W[l]["norm_mix"]
            TA["x"] = x_ext if l == 0 else X1buf
            TA["xprev"] = xprev_ext if l == 0 else XPV1
            qk4 = qk_s.rearrange("(h two p) t -> two h p t", h=4, two=2)
            TA.update(qT=qk4[0], kT=qk4[1], vv=hview(vv_s), lf=hview(lf_s),
                      g=g_s, bonus=bonus_s, maT=maT_s, gbT=gbT_s, gcT=gcT_s)
            for n in RW:
                TA["rw_" + n] = hview(rw_s[l][n])
            if l == 1:
                TA["vfirst"] = hview(rw_s[0]["v"])
            def gather_qkv():
                if do_X:
                    for c in range(4):
                        allgather(qk_s[c * 256:(c + 1) * 256, :], G_qk[c * 1024:(c + 1) * 1024, :])
                        allgather(vv_s[c * TPC:(c + 1) * TPC, :], G_v[c * 4 * TPC:(c + 1) * 4 * TPC, :])
                    allgather(lf_s, G_lf)
            if do_A:
                emit_A(nc, S, M, TA, l == 1, after_e=gather_qkv)
            else:
                gather_qkv()
            S.barrier()
            if do_X:
                for c in range(4):
                    for n in RW:
                        allgather(rw_s[l][n][c * TPC:(c + 1) * TPC, :], G_rw[n][c * 4 * TPC:(c + 1) * 4 * TPC, :])
                for n in RW:
                    dyn_copy(rw_l[n], G_rw[n], 0, 4 * TPC, 4 * TPC, q="pool")
            if do_X:
                for q in range(4):
                    dyn_copy(qT_l[:, q * TPC:(q + 1) * TPC], G_qk, q * 256, 1024, 128)
                    dyn_copy(kT_l[:, q * TPC:(q + 1) * TPC], G_qk, q * 256 + 128, 1024, 128)
                    dyn_copy(lf_l[q * TPC:(q + 1) * TPC, :], G_lf, q * 4 * TPC, TPC, TPC)
                dyn_copy(vv_l, G_v, 0, 4 * TPC, 4 * TPC)
            S.barrier(skip_q=("pool",))
            TB = dict(qT=qT_l, kT=kT_l, vv=vv_l, lf=lf_l, yb=yb_s, yc=yc_s, crow=crow_i)
            for n in RW:
                TB["rw_" + n] = rw_l[n]
            if do_B:
                emit_B(nc, S, M, TB, SEQ, str(l) + "a", do_attn=True, do_rwkv=False)
            S.barrier()
            if do_X:
                for c in range(4):
                    allgather(yb_s[c * TPC:(c + 1) * TPC, :], G_yb[c * 4 * TPC:(c + 1) * 4 * TPC, :])
                dyn_copy(yb_l, G_yb, 0, 4 * TPC, 4 * TPC, q="pool")

            def yc_hook(c):
                if do_X and (c + 1) % 16 == 0:
                    qd = (c + 1) // 16 - 1
                    allgather(yc_s[qd * TPC:(qd + 1) * TPC, :], G_yc[qd * 4 * TPC:(qd + 1) * 4 * TPC, :], reads=[("ycd", qd)])
            if do_B:
                emit_B(nc, S, M, TB, SEQ, str(l) + "r", do_attn=False, do_rwkv=True, after_chunk=yc_hook)
            S.barrier()
            if do_X:
                dyn_copy(yc_l, G_yc, 0, 4 * TPC, 4 * TPC)
            S.barrier()
            TC = dict(W[l])
            TC.update(x=(x_ext if l == 0 else X1buf), yb=hview(yb_l), yc=hview(yc_l), g=g_s, bonus=bonus_s, maT=maT_s, gbT=gbT_s, gcT=gcT_s,
                      xo=(X1buf if l == 0 else out_ext))
            if l == 1:
                TC["norm_final"] = nfin
            if l == 0:
                pass
            else:
                pass
            if do_C:
                emit_C(nc, S, M, TC, l == 1)
            if l == 0 and do_X:
                S.dma("sp", XPV1[0], ZT[:], reads=["ZT"]); S.dma("sp", XPV1[1], ZT[:], reads=["ZT"]); S.dma("sp", XP5[0:1, :], ZT[0:1, :], reads=["ZT"])
                S.barrier()
                copy(LASTROW, X1buf[TPC - 1:TPC, :]); copy(XPV1[1, 0:1, :], X1buf[TPC // 2 - 1:TPC // 2, :])
                S.barrier()
                allgather(LASTROW, GLR)
                S.barrier()
                copy(XP5[1:5, :], GLR)
                S.barrier()
                dyn_copy(XPV1[0, 0:1, :], XP5, 0, 1, 1)
                S.barrier()
        S.finish()
        print("fused stats", S.stats())
    return nc


_NC = {}


def _c(a):
    return np.ascontiguousarray(a)


def kernel(**inp):
    inp = {k: np.asarray(v) for k, v in inp.items()}
    NCORE, SEQ, TPC = 8, 8192, 2048
    if "nc" not in _NC:
        _NC["nc"] = build_fused()
    xflat = _c(inp["x"].reshape(2 * SEQ, 1024).astype(np.float32))
    wm = {}
    for l in range(2):
        for n in WSHAPES:
            a = inp[n][l]
            wm["%s_%d" % (n, l)] = _c(a.reshape(WSHAPES[n]).astype(np.float32))
    for n in W1SHAPES:
        wm[n + "_1"] = _c(inp[n][0].astype(np.float32))
    wm["norm_final"] = _c(inp["norm_final"].astype(np.float32))
    maps = []
    for i in range(NCORE):
        r0 = i * TPC
        xp = np.zeros((2, 128, 1024), np.float32)
        if i % 4 != 0:
            xp[0, 0] = xflat[r0 - 1]
        xp[1, 0] = xflat[r0 + TPC // 2 - 1]
        m = dict(wm)
        m["x"] = _c(xflat[r0:r0 + TPC]); m["xprev"] = xp
        maps.append(m)
    res = run_bass_kernel_spmd(_NC["nc"], maps, core_ids=list(range(NCORE))).results
    out = np.concatenate([np.asarray(res[i]["out"]) for i in range(NCORE)], axis=0)
    return _c(out.reshape(2, SEQ, 1024).astype(np.float32))
```

```python
import numpy as np
from contextlib import ExitStack
import concourse.bass as bass
import concourse.mybir as mybir
from concourse.bass_utils import run_bass_kernel_spmd

F32 = mybir.dt.float32
BF16 = mybir.dt.bfloat16
AF = mybir.ActivationFunctionType
ALU = mybir.AluOpType
AX = mybir.AxisListType

EPOCH = 20000
NDMASEM = 24


class Sched:
    ENGS = ("pe", "act", "dve", "pool", "sp")

    def __init__(self, nc, es):
        self.nc = nc
        self.es = es
        self.prog = {e: [] for e in self.ENGS}
        self.cnt = {e: 0 for e in self.ENGS}
        self.sem = {e: self._newsem("c_" + e) for e in self.ENGS}
        self.waited = {e: {} for e in self.ENGS}
        self.lastw = {}
        self.readers = {}
        self.dsem = {e: [self._newsem("d_%s%d" % (e, i)) for i in range(NDMASEM)] for e in ("sp", "pool", "act")}
        self.dcnt = {e: [0] * NDMASEM for e in ("sp", "pool", "act")}
        self.drr = {e: 0 for e in ("sp", "pool", "act")}
        self.nsem = 0
        self.out_tokens = []
        self.const = set()
        self.hist = {}
        self.excl = set()

    def _newsem(self, name):
        self._n = getattr(self, "_n", 0) + 1
        return self.es.enter_context(self.nc.semaphore("%s_%d" % (name, self._n)))

    def _need(self, eng, reads, writes, skip_same=False):
        toks = []
        for b in list(reads) + list(writes):
            t = self.lastw.get(b)
            if t is not None:
                toks.append(t)
        for b in writes:
            toks.extend(self.readers.get(b, ()))
        for (sem, val, teng) in toks:
            if skip_same and teng == eng:
                continue
            w = self.waited[eng]
            k = id(sem)
            if w.get(k, 0) >= val:
                continue
            w[k] = val
            self.prog[eng].append(("wait", sem, val))

    def _commit(self, tok, reads, writes):
        for b in writes:
            self.lastw[b] = tok
            self.readers[b] = []
        for b in reads:
            if b not in self.const:
                self.readers.setdefault(b, []).append(tok)

    def _excl(self, reads, writes):
        ex = [b for b in reads if isinstance(b, tuple) and b[0] in self.excl]
        if ex:
            writes = list(writes) + [b for b in ex if b not in writes]
            reads = [b for b in reads if b not in ex]
        return reads, writes

    def op(self, eng, fn, reads=(), writes=()):
        reads, writes = self._excl(reads, writes)
        self._need(eng, reads, writes, skip_same=(eng == "pe"))
        if self.cnt[eng] >= EPOCH:
            self.sem[eng] = self._newsem("c_" + eng)
            self.cnt[eng] = 0
        self.cnt[eng] += 1
        tok = (self.sem[eng], self.cnt[eng], eng)
        self.hist.setdefault(eng, {})[id(self.sem[eng])] = (self.sem[eng], self.cnt[eng])
        self.prog[eng].append(("op", fn, self.sem[eng], 1))
        self._commit(tok, reads, writes)
        return tok

    def dma(self, q, out, in_, reads=(), writes=(), is_output=False, custom=None, inc=16, **kw):
        i = self.drr[q]
        self.drr[q] = (i + 1) % NDMASEM
        sem = self.dsem[q][i]
        if self.dcnt[q][i] > 0:
            w = self.waited[q]
            if w.get(id(sem), 0) < self.dcnt[q][i]:
                w[id(sem)] = self.dcnt[q][i]
                self.prog[q].append(("wait", sem, self.dcnt[q][i]))
        self._need(q, reads, writes)
        self.dcnt[q][i] += inc
        tok = (sem, self.dcnt[q][i], "dma_" + q)
        if custom is not None:
            self.prog[q].append(("op", custom, sem, inc))
        else:
            self.prog[q].append(("op", lambda e, o=out, a=in_, k=kw: e.dma_start(out=o, in_=a, **k), sem, 16))
        self._commit(tok, reads, writes)
        if is_output:
            self.out_tokens.append(tok)
        return tok

    def barrier(self, skip_q=()):
        toks = []
        for e in self.ENGS:
            for (sem, val) in self.hist.get(e, {}).values():
                toks.append((sem, val))
        for q in self.dsem:
            if q in skip_q:
                continue
            for i in range(NDMASEM):
                if self.dcnt[q][i] > 0:
                    toks.append((self.dsem[q][i], self.dcnt[q][i]))
        for e in self.ENGS:
            w = self.waited[e]
            for (sem, val) in toks:
                if w.get(id(sem), 0) >= val:
                    continue
                w[id(sem)] = val
                self.prog[e].append(("wait", sem, val))

    def finish(self):
        for (sem, val, _) in self.out_tokens:
            self.prog["sp"].append(("wait", sem, val))
        nc = self.nc
        with nc.Block() as block:
            def replay(e, lst):
                for it in lst:
                    if it[0] == "wait":
                        e.wait_ge(it[1], it[2])
                    else:
                        it[1](e).then_inc(it[2], it[3])

            @block.sync
            def _(e):
                replay(e, self.prog["sp"])

            @block.tensor
            def _(e):
                replay(e, self.prog["pe"])

            @block.scalar
            def _(e):
                replay(e, self.prog["act"])

            @block.vector
            def _(e):
                replay(e, self.prog["dve"])

            @block.gpsimd
            def _(e):
                replay(e, self.prog["pool"])

    def stats(self):
        return {e: len(self.prog[e]) for e in self.ENGS}


class Ring:
    def __init__(self, name, tiles):
        self.name = name
        self.tiles = tiles
        self.i = 0

    def next(self):
        k = self.i % len(self.tiles)
        self.i += 1
        return self.tiles[k], (self.name, k)


def mk_alloc(nc, es):
    def sb(n, shp, dt=F32):
        return es.enter_context(nc.sbuf_tensor(n, shp, dt))

    def ps(n, shp, dt=F32):
        return es.enter_context(nc.psum_tensor(n, shp, dt))

    def sbring(name, n, shp, dt=F32):
        return Ring(name, [sb("%s%d" % (name, i), shp, dt) for i in range(n)])

    return sb, ps, sbring


def build_masks(S, sb):
    m = {}
    for n in ("ident", "ones", "uti", "uts", "lts", "blk"):
        m[n] = sb("m_" + n, [128, 128])
    m["mask2"] = sb("m_mask2", [128, 256])
    m["negm"] = sb("m_negm", [128, 128], BF16)
    m["identb"] = sb("m_identb", [128, 128], BF16)

    def sel(t, pat, cm, cmp, fill, key):
        S.op("pool", lambda e: e.affine_select(out=t, in_=t, pattern=[[pat, t.shape[-1]]], compare_op=cmp, fill=fill, base=0, channel_multiplier=cm), reads=[key], writes=[key])

    def ms(t, v, key):
        S.op("pool", lambda e: e.memset(t, v), writes=[key])

    ms(m["ident"][:], 0.0, "ident"); sel(m["ident"][:], -1, 1, ALU.not_equal, 1.0, "ident")
    ms(m["identb"][:], 0.0, "identb"); sel(m["identb"][:], -1, 1, ALU.not_equal, 1.0, "identb")
    ms(m["ones"][:], 1.0, "ones")
    ms(m["uti"][:], 1.0, "uti"); sel(m["uti"][:], 1, -1, ALU.is_ge, 0.0, "uti")
    ms(m["uts"][:], 1.0, "uts"); sel(m["uts"][:], 1, -1, ALU.is_gt, 0.0, "uts")
    ms(m["lts"][:], 1.0, "lts"); sel(m["lts"][:], -1, 1, ALU.is_gt, 0.0, "lts")
    ms(m["negm"][:], 0.0, "negm"); sel(m["negm"][:], 1, -1, ALU.is_ge, -30000.0, "negm")
    ms(m["blk"][:], 0.0, "blk")
    S.op("pool", lambda e: e.memset(m["blk"][0:64, 0:64], 1.0), reads=["blk"], writes=["blk"])
    S.op("pool", lambda e: e.memset(m["blk"][64:128, 64:128], 1.0), reads=["blk"], writes=["blk"])
    ms(m["mask2"][:], 1.0, "mask2")
    sel(m["mask2"][:, 0:128], 1, -1, ALU.is_gt, 0.0, "mask2")
    sel(m["mask2"][:, 128:256], 1, -1, ALU.is_ge, 0.0, "mask2")
    for k in m:
        S.const.add(k)
    return m


def mkops(S):
    class O:
        pass
    o = O()

    def MM(out, lhsT, rhs, start=True, stop=True, r=(), w=()):
        S.op("pe", lambda e: e.matmul(out, lhsT, rhs, start=start, stop=stop), r, w)

    def TR(out, in_, ident, r=(), w=()):
        S.op("pe", lambda e: e.transpose(out, in_, ident), r, w)

    def ACT(out, in_, func, bias=None, scale=1.0, accum=None, r=(), w=()):
        kw = dict(out=out, in_=in_, func=func, scale=scale)
        if bias is not None:
            kw["bias"] = bias
        if accum is not None:
            kw["accum_out"] = accum
        S.op("act", lambda e: e.activation(**kw), r, w)

    def TT(eng, out, in0, in1, op, r=(), w=()):
        S.op(eng, lambda e: e.tensor_tensor(out=out, in0=in0, in1=in1, op=op), r, w)

    def TS(eng, out, in0, s1, s2, op0, op1=None, r=(), w=()):
        if op1 is None:
            S.op(eng, lambda e: e.tensor_scalar(out=out, in0=in0, scalar1=s1, scalar2=None, op0=op0), r, w)
        else:
            S.op(eng, lambda e: e.tensor_scalar(out=out, in0=in0, scalar1=s1, scalar2=s2, op0=op0, op1=op1), r, w)

    def STT(eng, out, in0, scalar, in1, op0, op1, r=(), w=()):
        S.op(eng, lambda e: e.scalar_tensor_tensor(out=out, in0=in0, scalar=scalar, in1=in1, op0=op0, op1=op1), r, w)

    def CP(eng, out, in_, r=(), w=()):
        if eng == "act":
            S.op("act", lambda e: e.copy(out, in_), r, w)
        else:
            S.op(eng, lambda e: e.tensor_copy(out, in_), r, w)

    def MS(eng, out, val, w=()):
        S.op(eng, lambda e: e.memset(out, val), (), w)

    def RED(eng, out, in_, op, r=(), w=()):
        S.op(eng, lambda e: e.tensor_reduce(out=out, in_=in_, axis=AX.X, op=op), r, w)

    def RECIP(out, in_, r=(), w=()):
        S.op("dve", lambda e: e.reciprocal(out, in_), r, w)

    def DMA(out, in_, r=(), w=(), is_output=False, q="sp", **kw):
        S.dma(q, out, in_, reads=r, writes=w, is_output=is_output, **kw)

    o.MM, o.TR, o.ACT, o.TT, o.TS, o.STT, o.CP, o.MS, o.RED, o.RECIP, o.DMA = MM, TR, ACT, TT, TS, STT, CP, MS, RED, RECIP, DMA
    return o


from contextlib import contextmanager


_stage_n = [0]


@contextmanager
def stage(nc, S):
    _stage_n[0] += 1
    sfx = "_s%d" % _stage_n[0]
    with ExitStack() as es2:
        sb0, ps, sbring0 = mk_alloc(nc, es2)

        def sb(n, shp, dt=F32):
            return sb0(n + sfx, shp, dt)

        def sbring(name, n, shp, dt=F32):
            r = sbring0(name + sfx, n, shp, dt)
            return r

        yield sb, sbring
        S.barrier()


A_U, A_V, B_Q, B_K, B_V, B_F, C_R, C_K, C_V, C_WLO, C_ALO, C_GLO, G0 = 0, 1024, 2048, 2560, 3072, 3584, 3592, 4104, 4616, 5128, 5192, 5256, 5384
IN_COLS = 8456
NORM_EPS = 1e-6
LN_EPS = 1e-5
STG = 256
import os as _os
WQ2 = bool(_os.environ.get('WQ2'))


def emit_A(nc, S, M, T, layer1, NT=16, NPH=2, after_e=None):
    TOKF = NT * 128
    NT = NT // NPH
    TOK = NT * 128
    NG = TOK // 512
    x = T["x"]; xprev = T["xprev"]
    w_in = T["w_in"]; nmix = T["nmix"]; gate_bias = T["gate_bias"]
    a_ln_g = T["a_ln_g"]; a_ln_b = T["a_ln_b"]; a_w_s = T["a_w_s"]; a_b_s = T["a_b_s"]
    b_f_bias = T["b_f_bias"]; c_mu = T["c_mu"]; c_w0 = T["c_w0"]; c_w_up = T["c_w_up"]
    c_a0 = T["c_a0"]; c_a_up = T["c_a_up"]; c_g_up = T["c_g_up"]
    c_k_k = T["c_k_k"]; c_k_a = T["c_k_a"]; c_r_k = T["c_r_k"]; p_a = T["p_a"]
    if layer1:
        c_v0 = T["c_v0"]; c_v_down = T["c_v_down"]; c_v_up = T["c_v_up"]; vfirst = T["vfirst"]
    o_qT = T["qT"]; o_kT = T["kT"]; o_vv = T["vv"]; o_lf = T["lf"]
    o_rw = {n: T["rw_" + n] for n in ("r", "k", "v", "kk", "b", "lw")}
    o_g = T["g"]; o_bonus = T["bonus"]
    o_maT = T["maT"]; o_gbT = T["gbT"]; o_gcT = T["gcT"]
    HSV = lambda ap: ap.rearrange("p (h c) -> p h c", h=4)

    with ExitStack() as es:
        sb0, ps, sbring0 = mk_alloc(nc, es)
        _sfx = "_A%d" % (1 if layer1 else 0)
        sb_g = lambda n, shp, dt=F32: sb0(n + _sfx, shp, dt)
        sb = sb_g
        sbring = lambda name, n, shp, dt=F32: Ring(name, [sb_g("%s%d" % (name, i), shp, dt) for i in range(n)])
        O = mkops(S)
        MM, TR, ACT, TT, TS, STT, CP, MS, RED, RECIP, DMA = O.MM, O.TR, O.ACT, O.TT, O.TS, O.STT, O.CP, O.MS, O.RED, O.RECIP, O.DMA
        ident, ones, uti, identb = M["ident"], M["ones"], M["uti"], M["identb"]

        tp_banks = [ps("tpbA%d" % i + _sfx, [128, 1024], BF16) for i in range(2)]
        tp_ring = Ring("tp", [b[:] for b in tp_banks])
        mm_banks = [ps("mmbA%d" % i + _sfx, [128, 512]) for i in range(6)]
        mm_ring = Ring("mm", [b[:] for b in mm_banks])
        S.excl.update(["tp", "mm"])

        def col_load(name, vec_ap, ncol):
            t = sb(name, [128, ncol])
            DMA(t[:], vec_ap.rearrange("(c p) -> p c", p=128), w=[name], allow_slow_non_contiguous=True)
            S.const.add(name)
            return t

        def bc_load(name, vec_ap, n):
            t = sb(name, [128, n])
            DMA(t[:], vec_ap.partition_broadcast(128), w=[name])
            S.const.add(name)
            return t

        GB = [col_load("GBc%d" % k, gate_bias[k], 8) for k in range(3)]
        LNG = col_load("LNG", a_ln_g, 8)
        LNB = col_load("LNB", a_ln_b, 8)
        BFb = bc_load("BFb", b_f_bias, 8)
        WTb = sb("WTb", [128, 4, 128], BF16); T2 = sb("T2", [128, 8, 128])
        sm_ring = sbring("sm", 4, [128, 8])
        gsb, gsbring = sb, sbring
        st0 = stage(nc, S); sb, sbring = st0.__enter__()
        WSf = sb("WSf", [128, 4, 128]); WTf = sb("WTf", [128, 4, 128])
        BSb = sb("BSb", [128, 4, 128])
        DMA(WSf[:], a_w_s.rearrange("g t s -> t g s"), w=["WSf"])
        for g in range(4):
            DMA(BSb[:, g, :], a_b_s[g].partition_broadcast(128), w=[("BSb", g)])
        for g in range(4):
            pm, pk = mm_ring.next()
            TR(pm[:, 0:128], WSf[:, g, :], ident[:], r=["WSf", "ident"], w=[pk])
            TT("dve", WTf[:, g, :], pm[:, 0:128], uti[:], ALU.mult, r=[pk, "uti"], w=[("WTf", g)])
            CP("pool", WTb[:, g, :], WTf[:, g, :], r=[("WTf", g)], w=["WTb"])
            pm, pk = mm_ring.next()
            MM(pm[:, 0:128], ones[:], WTf[:, g, :], r=["ones", ("WTf", g)], w=[pk])
            for dc in (2 * g, 2 * g + 1):
                STT("dve", T2[:, dc, :], pm[:, 0:128], LNB[:, dc:dc + 1], BSb[:, g, :], ALU.mult, ALU.add, r=[pk, "LNB", ("BSb", g)], w=["T2"])
        S.const.add("WTb"); S.const.add("T2")
        st0.__exit__(None, None, None)

        HT = gsb("HT", [128, 8, TOK], BF16)
        HS = gsb("HS", [128, 8, TOK], BF16)
        stg_ring = gsbring("stg", 2, [128, 8, STG])
        wb_ring = gsbring("wb", 3, [128, 8, 1024], BF16)
        xF, vfirstF = x, (vfirst if layer1 else None)
        oF = dict(qT=o_qT, kT=o_kT, vv=o_vv, lf=o_lf, g=o_g, bonus=o_bonus, maT=o_maT, gbT=o_gbT, gcT=o_gcT)
        o_rwF = o_rw
        for ph in range(NPH):
            r0 = ph * TOK
            x = xF[r0:r0 + TOK, :]
            if layer1:
                vfirst = vfirstF[r0:r0 + TOK]
            o_qT = oF["qT"][:, :, r0:r0 + TOK]; o_kT = oF["kT"][:, :, r0:r0 + TOK]; o_vv = oF["vv"][r0:r0 + TOK]; o_lf = oF["lf"][r0:r0 + TOK]
            o_g = oF["g"][r0:r0 + TOK, :]; o_bonus = oF["bonus"][r0:r0 + TOK, :]
            o_maT = oF["maT"][:, r0:r0 + TOK]; o_gbT = oF["gbT"][:, r0:r0 + TOK]; o_gcT = oF["gcT"][:, r0:r0 + TOK]
            o_rw = {n: o_rwF[n][r0:r0 + TOK] for n in o_rwF}
            st1 = stage(nc, S); sb, sbring = st1.__enter__()
            def bc_load(name, vec_ap, n):
                t = sb(name, [128, n])
                DMA(t[:], vec_ap.partition_broadcast(128), w=[name])
                S.const.add(name)
                return t
            Gb = bc_load("Gb", nmix, 1024)
            x_ring = sbring("xt", 3, [128, 1024])
            xh_ring = sbring("xh", 2, [128, 1024], BF16)
            junk = sb("junk", [128, 1024], BF16)
            def s1_norm(t):
                xt, xk = x_ring.next()
                DMA(xt[:], xprev[ph] if t < 0 else x[t * 128:(t + 1) * 128, :], w=[xk])
                sm, smk = sm_ring.next()
                MS("pool", sm[:], 0.0, w=[smk])
                ACT(junk[:], xt[:], AF.Square, accum=sm[:, 0:1], r=[xk, smk], w=["junk", smk])
                ACT(sm[:, 1:2], sm[:, 0:1], AF.Sqrt, bias=NORM_EPS, scale=1.0 / 1024, r=[smk], w=[smk])
                RECIP(sm[:, 2:3], sm[:, 1:2], r=[smk], w=[smk])
                xh, xhk = xh_ring.next()
                STT("dve", xh[:], xt[:], sm[:, 2:3], Gb[:], ALU.mult, ALU.mult, r=[xk, smk, "Gb"], w=[xhk])
                return xh, xhk

            def s1_tr(t, xh, xhk):
                tp, tpk = tp_ring.next()
                for mc in range(8):
                    TR(tp[:, mc * 128:(mc + 1) * 128], xh[:, mc * 128:(mc + 1) * 128], identb[:], r=[xhk, "identb"], w=[tpk])
                tp3 = tp.rearrange("p (c n) -> p c n", c=8)
                eng = "act" if (t % 2 == 0) else "dve"
                if t < 0:
                    CP(eng, HS[:, :, 0:1], tp3[:, :, 0:1], r=[tpk], w=["HS"])
                else:
                    CP(eng, HT[:, :, t * 128:(t + 1) * 128], tp3, r=[tpk, "HTX"], w=["HTX"])

            nx = s1_norm(-1)
            for t in range(-1, NT):
                cur = nx
                if t + 1 < NT:
                    nx = s1_norm(t + 1)
                s1_tr(t, *cur)
            CP("dve", HS[:, :, 1:TOK], HT[:, :, 0:TOK - 1], r=["HTX", "HS"], w=["HS"])
            st1.__exit__(None, None, None)

            cast_i = [0]

            def load_w(src2d, c0, ncols, dst, dst_c0, dkey, mu=None):
                for o0 in range(0, ncols, STG):
                    n = min(STG, ncols - o0)
                    stg, sk = stg_ring.next()
                    DMA(stg[:, :, 0:n], src2d[:, c0 + o0:c0 + o0 + n].rearrange("(mc p) c -> p mc c", p=128), w=[sk], q=("pool" if (cast_i[0] % 2 and WQ2) else "sp"))
                    cast_i[0] += 1
                    ceng = "act" if cast_i[0] % 2 else "dve"
                    if mu is None:
                        CP(ceng, dst[:, :, dst_c0 + o0:dst_c0 + o0 + n], stg[:, :, 0:n], r=[sk, dkey], w=[dkey])
                    else:
                        TT("dve", dst[:, :, dst_c0 + o0:dst_c0 + o0 + n], stg[:, :, 0:n], mu[:, o0:o0 + n].unsqueeze(1).to_broadcast([128, 8, n]), ALU.mult,
                           r=[sk, dkey, "MU", "OMMU"], w=[dkey])

            def fm_group(wt, wk, c0, Mw, tg, wt2=None, c02=None):
                pm, pk = mm_ring.next()
                for mc in range(8):
                    MM(pm[0:Mw, :], wt[:, mc, c0:c0 + Mw], HT[:, mc, tg * 512:(tg + 1) * 512], start=(mc == 0), stop=(mc == 7 and wt2 is None),
                       r=[wk, "HTX", "HS"], w=[pk])
                if wt2 is not None:
                    for mc in range(8):
                        MM(pm[0:Mw, :], wt2[:, mc, c02:c02 + Mw], HS[:, mc, tg * 512:(tg + 1) * 512], start=False, stop=(mc == 7), r=[wk, "HTX", "HS"], w=[pk])
                return pm, pk

            def tm_group(t, wt, wk, c0, N, wt2=None, c02=None, wk2=None):
                pm, pk = mm_ring.next()
                for mc in range(8):
                    MM(pm[:, 0:N], HT[:, mc, t * 128:(t + 1) * 128], wt[:, mc, c0:c0 + N], start=(mc == 0), stop=(mc == 7 and wt2 is None),
                       r=[wk, "HTX", "HS"], w=[pk])
                if wt2 is not None:
                    for mc in range(8):
                        MM(pm[:, 0:N], HS[:, mc, t * 128:(t + 1) * 128], wt2[:, mc, c02:c02 + N], start=False, stop=(mc == 7), r=[wk2, "HTX", "HS"], w=[pk])
                return pm, pk

            ev_i = [0]

            def ev_eng():
                ev_i[0] += 1
                return "act" if ev_i[0] % 2 else "dve"

            st2 = stage(nc, S); sb, sbring = st2.__enter__()
            junk = sb("junk2", [128, 1024], BF16)
            UT = sb("UT", [128, 8, TOK], BF16)
            uw = []
            for half in range(2):
                wt, wk = wb_ring.next()
                load_w(w_in, A_U + half * 512, 512, wt, 0, wk)
                uw.append((wt, wk))
            vwt, vwk = wb_ring.next()
            load_w(w_in, A_V, 512, vwt, 0, vwk)
            load_w(w_in, A_V + 512, 512, vwt, 512, vwk)
            for half in range(2):
                wt, wk = uw[half]
                for jc in range(4):
                    for tg in range(NG):
                        pm, pk = fm_group(wt, wk, jc * 128, 128, tg)
                        ACT(UT[:, half * 4 + jc, tg * 512:(tg + 1) * 512], pm[:, :], AF.Gelu_apprx_tanh, r=[pk], w=[("UT", half * 4 + jc, tg)])
            pat, pak = wb_ring.next()
            load_w(p_a, 0, 512, pat, 0, pak)
            load_w(p_a, 512, 512, pat, 512, pak)
            g0t, g0k = wb_ring.next()
            load_w(w_in, G0, 512, g0t, 0, g0k)
            wt, wk = vwt, vwk
            gv_ring = sbring("gv", 2, [128, 1024])
            nb_ring = sbring("nb", 2, [128, 1024], BF16)
            tmp_ring = sbring("tmpA", 3, [128, 128])
            smb_ring = sbring("smb", 3, [128, 8])

            def b_proj(t):
                gv, gvk = gv_ring.next()
                sm, smk = smb_ring.next()
                MS("pool", sm[:], 0.0, w=[smk])
                for half in range(2):
                    pm, pk = tm_group(t, wt, wk, half * 512, 512)
                    ACT(gv[:, half * 512:(half + 1) * 512], pm[:, :], AF.Gelu_apprx_tanh, accum=sm[:, half:half + 1], r=[pk], w=[gvk, smk])
                return gv, gvk, sm, smk

            def b_mix(t, gv, gvk, sm, smk):
                ACT(junk[:], gv[:], AF.Square, accum=sm[:, 2:3], r=[gvk, smk], w=["junk", smk])
                TT("dve", sm[:, 3:4], sm[:, 0:1], sm[:, 1:2], ALU.add, r=[smk], w=[smk])
                TS("dve", sm[:, 3:4], sm[:, 3:4], 1.0 / 1024, None, ALU.mult, r=[smk], w=[smk])
                TT("dve", sm[:, 4:5], sm[:, 3:4], sm[:, 3:4], ALU.mult, r=[smk], w=[smk])
                STT("dve", sm[:, 5:6], sm[:, 2:3], 1.0 / 1024, sm[:, 4:5], ALU.mult, ALU.subtract, r=[smk], w=[smk])
                ACT(sm[:, 6:7], sm[:, 5:6], AF.Sqrt, bias=LN_EPS, scale=1.0, r=[smk], w=[smk])
                RECIP(sm[:, 7:8], sm[:, 6:7], r=[smk], w=[smk])
                nb, nbk = nb_ring.next()
                TS("dve", nb[:], gv[:], sm[:, 3:4], sm[:, 7:8], ALU.subtract, ALU.mult, r=[gvk, smk], w=[nbk])
                tg = t // 4
                for dc in range(8):
                    pm, pk = mm_ring.next()
                    MM(pm[:, 0:128], nb[:, dc * 128:(dc + 1) * 128], WTb[:, dc // 2, :], r=[nbk, "WTb"], w=[pk])
                    tmp, tk = tmp_ring.next()
                    STT("dve", tmp[:], pm[:, 0:128], LNG[:, dc:dc + 1], T2[:, dc, :], ALU.mult, ALU.add, r=[pk, "LNG", "T2"], w=[tk])
                    TT("dve", UT[:, dc, t * 128:(t + 1) * 128], tmp[:], UT[:, dc, t * 128:(t + 1) * 128], ALU.mult, r=[tk, ("UT", dc, tg)], w=[("UT", dc, tg)])

            pend_b = b_proj(0)
            for t in range(NT):
                cur = pend_b
                if t + 1 < NT:
                    pend_b = b_proj(t + 1)
                b_mix(t, *cur)
            gt_ring = sbring("gt", 3, [128, 512])
            def gate_load(gi, half):
                wt, wk = wb_ring.next()
                load_w(w_in, G0 + gi * 1024 + half * 512, 512, wt, 0, wk)
                return wt, wk
            gl = [(gi, half) for gi in range(3) for half in range(2)]
            g_next = (g0t, g0k)
            for gidx, (gi, half) in enumerate(gl):
                if True:
                    wt, wk = g_next
                    if gidx + 1 < len(gl):
                        g_next = gate_load(*gl[gidx + 1])
                    for jc in range(4):
                        nch = half * 4 + jc
                        for tg in range(NG):
                            pm, pk = fm_group(wt, wk, jc * 128, 128, tg)
                            gt, gk = gt_ring.next()
                            ACT(gt[:], pm[:, :], AF.Sigmoid, bias=GB[gi][:, nch:nch + 1], r=[pk, "GBc%d" % gi], w=[gk])
                            if gi == 0:
                                pm2, pk2 = mm_ring.next()
                                for dc in range(8):
                                    MM(pm2[:, :], pat[:, dc, nch * 128:(nch + 1) * 128], UT[:, dc, tg * 512:(tg + 1) * 512], start=(dc == 0), stop=(dc == 7),
                                       r=[pak, ("UT", dc, tg)], w=[pk2])
                                TT("dve", gt[:], gt[:], pm2[:, :], ALU.mult, r=[gk, pk2], w=[gk])
                            dst = (o_maT, o_gbT, o_gcT)[gi]
                            DMA(dst[nch * 128:(nch + 1) * 128, tg * 512:(tg + 1) * 512], gt[:], r=[gk], is_output=True, q="pool")
            st2.__exit__(None, None, None)
            st3 = stage(nc, S); sb, sbring = st3.__enter__()
            qk_ring = sbring("qkb", 3, [128, 512], BF16)
            for qi, (c0, dst) in enumerate(((B_Q, o_qT), (B_K, o_kT))):
                wt, wk = wb_ring.next()
                load_w(w_in, c0, 512, wt, 0, wk)
                for jc in range(4):
                    for tg in range(NG):
                        pm, pk = fm_group(wt, wk, jc * 128, 128, tg)
                        qb, qbk = qk_ring.next()
                        CP(ev_eng(), qb[:], pm[:, :], r=[pk], w=[qbk])
                        DMA(dst[jc, :, tg * 512:(tg + 1) * 512], qb[:], r=[qbk], is_output=True, q="pool")
            wt, wk = wb_ring.next()
            load_w(w_in, B_V, 512, wt, 0, wk)
            load_w(w_in, B_F, 8, wt, 512, wk)
            lf_ring = sbring("lfr", 3, [128, 8])
            for t in range(NT):
                pm, pk = tm_group(t, wt, wk, 0, 512)
                qb, qbk = qk_ring.next()
                CP(ev_eng(), qb[:], pm[:, :], r=[pk], w=[qbk])
                DMA(o_vv[t * 128:(t + 1) * 128], HSV(qb[:]), r=[qbk], is_output=True, q="pool")
                pm, pk = tm_group(t, wt, wk, 512, 8)
                lf, lfk = lf_ring.next()
                TT("dve", lf[:], pm[:, 0:8], BFb[:], ALU.add, r=[pk, "BFb"], w=[lfk])
                ACT(lf[:], lf[:], AF.Exp, scale=-1.0, r=[lfk], w=[lfk])
                ACT(lf[:], lf[:], AF.Ln, bias=1.0, r=[lfk], w=[lfk])
                TS("dve", lf[:], lf[:], -1.0, None, ALU.mult, r=[lfk], w=[lfk])
                DMA(o_lf[t * 128:(t + 1) * 128], HSV(lf[:]), r=[lfk], is_output=True, q="pool")
            st3.__exit__(None, None, None)
            if ph == NPH - 1 and after_e is not None:
                after_e()
            st4 = stage(nc, S); sb, sbring = st4.__enter__()
            def bc_load(name, vec_ap, n):
                t = sb(name, [128, n])
                DMA(t[:], vec_ap.partition_broadcast(128), w=[name])
                S.const.add(name)
                return t
            MU = bc_load("MU", c_mu, 1792)
            OMMU = sb("OMMU", [128, 1792])
            TS("dve", OMMU[:], MU[:], -1.0, 1.0, ALU.mult, ALU.add, r=["MU"], w=["OMMU"])
            S.const.add("OMMU")
            KKb = bc_load("KKb", c_k_k, 512); KAb = bc_load("KAb", c_k_a, 512); RKb = bc_load("RKb", c_r_k, 512)
            WUP = sb("WUP", [65, 512]); AUP = sb("AUP", [65, 512]); GUPf = sb("GUPf", [128, 512]); GUP = sb("GUP", [128, 512], BF16)
            DMA(WUP[0:64, :], c_w_up, w=["WUP"]); DMA(WUP[64:65, :], c_w0.partition_broadcast(1), r=["WUP"], w=["WUP"])
            DMA(AUP[0:64, :], c_a_up, w=["AUP"]); DMA(AUP[64:65, :], c_a0.partition_broadcast(1), r=["AUP"], w=["AUP"])
            DMA(GUPf[:], c_g_up, w=["GUPf"])
            CP("dve", GUP[:], GUPf[:], r=["GUPf"], w=["GUP"])
            for k in ("WUP", "AUP", "GUP"):
                S.const.add(k)
            if layer1:
                VUPf = sb("VUPf", [33, 512]); VUP = sb("VUP", [33, 512], BF16)
                DMA(VUPf[0:32, :], c_v_up, w=["VUPf"]); DMA(VUPf[32:33, :], c_v0.partition_broadcast(1), r=["VUPf"], w=["VUPf"])
                CP("dve", VUP[:], VUPf[:], r=["VUPf"], w=["VUP"])
                VDf = sb("VDf", [128, 4, 32]); VDb = sb("VDb", [128, 4, 32], BF16)
                DMA(VDf[:], c_v_down.rearrange("(c p) n -> p c n", p=128), w=["VDf"])
                CP("dve", VDb[:], VDf[:], r=["VDf"], w=["VDb"])
                S.const.add("VUP"); S.const.add("VDb")
            w1 = sb("w1t", [128, 8, 512], BF16); w1k = "w1t"
            load_w(w_in, C_WLO, 256, w1, 0, w1k, mu=OMMU[:, 1536:1792])
            load_w(w_in, C_WLO, 256, w1, 256, w1k, mu=MU[:, 1536:1792])
            if layer1:
                wv = sb("wvt", [128, 8, 1024], BF16); wvk = "wvt"
                load_w(w_in, C_V, 512, wv, 0, wvk, mu=OMMU[:, 1024:1536])
                load_w(w_in, C_V, 512, wv, 512, wvk, mu=MU[:, 1024:1536])
            wa, wak = wb_ring.next(); wb_, wbk = wb_ring.next(); wc, wck = wb_ring.next()
            load_w(w_in, C_R, 512, wa, 0, wak, mu=OMMU[:, 0:512]); load_w(w_in, C_K, 512, wa, 512, wak, mu=OMMU[:, 512:1024])
            load_w(w_in, C_V, 512, wb_, 0, wbk, mu=OMMU[:, 1024:1536]); load_w(w_in, C_R, 512, wb_, 512, wbk, mu=MU[:, 0:512])
            load_w(w_in, C_K, 512, wc, 0, wck, mu=MU[:, 512:1024]); load_w(w_in, C_V, 512, wc, 512, wck, mu=MU[:, 1024:1536])
            TW = sb("TW", [65, TOK]); TA = sb("TA", [65, TOK]); TG = sb("TG", [128, TOK], BF16)
            MS("dve", TW[64:65, :], 1.0, w=["TW"]); MS("dve", TA[64:65, :], 1.0, w=["TA"])
            for tg in range(NG):
                pm, pk = fm_group(w1, w1k, 0, 64, tg, wt2=w1, c02=256)
                ACT(TW[0:64, tg * 512:(tg + 1) * 512], pm[0:64, :], AF.Tanh, r=[pk, "TW"], w=["TW"])
                pm, pk = fm_group(w1, w1k, 64, 64, tg, wt2=w1, c02=256 + 64)
                CP("dve", TA[0:64, tg * 512:(tg + 1) * 512], pm[0:64, :], r=[pk, "TA"], w=["TA"])
                pm, pk = fm_group(w1, w1k, 128, 128, tg, wt2=w1, c02=256 + 128)
                ACT(TG[:, tg * 512:(tg + 1) * 512], pm[:, :], AF.Sigmoid, r=[pk, "TG"], w=["TG"])
            if layer1:
                vt_ring = sbring("VT", 2, [128, 4, 512], BF16)
                TV = sb("TV", [33, TOK], BF16)
                MS("dve", TV[32:33, :], 1.0, w=["TV"])
                for tg in range(NG):
                    VT, vtk = vt_ring.next()
                    for jc in range(4):
                        pm, pk = fm_group(wv, wvk, jc * 128, 128, tg, wt2=wv, c02=512 + jc * 128)
                        CP(ev_eng(), VT[:, jc, :], pm[:, :], r=[pk, vtk], w=[vtk])
                    pm, pk = mm_ring.next()
                    for jc in range(4):
                        MM(pm[0:32, :], VDb[:, jc, :], VT[:, jc, :], start=(jc == 0), stop=(jc == 3), r=["VDb", vtk], w=[pk])
                    CP("dve", TV[0:32, tg * 512:(tg + 1) * 512], pm[0:32, :], r=[pk, "TV"], w=["TV"])
            names = ("r", "k", "v", "a", "e", "g", "kq", "t", "rk")
            E = {n: sbring("e_" + n, 1, [128, 512]) for n in names}
            for n2, n1 in (("kk", "kq"), ("b", "a"), ("bon", "rk"), ("lw", "e")):
                E[n2] = E[n1]
            if layer1:
                E["vf"] = sbring("e_vf", 1, [128, 512]); E["s"] = sbring("e_s", 1, [128, 512])
            v8 = lambda ap: ap.rearrange("p (h d) -> p h d", h=8)
            bc8 = lambda ap: ap.unsqueeze(2).to_broadcast([128, 8, 64])
            for t in range(NT):
                rows = slice(t * 128, (t + 1) * 128)
                tl = slice(t * 128, (t + 1) * 128)
                r_, rk_ = E["r"].next(); k_, kk_ = E["k"].next(); v_, vk_ = E["v"].next(); a_, ak_ = E["a"].next(); e_, ek_ = E["e"].next()
                g_, gk_ = E["g"].next(); kq, kqk = E["kq"].next(); kkn, kknk = E["kk"].next(); b_, bk_ = E["b"].next(); tt_, ttk = E["t"].next()
                rkp, rkpk = E["rk"].next(); bon, bonk = E["bon"].next(); lw, lwk = E["lw"].next()
                sm, smk = sm_ring.next()
                pm, pk = tm_group(t, wa, wak, 0, 512, wt2=wb_, c02=512, wk2=wbk)
                CP("act", r_[:], pm[:, :], r=[pk], w=[rk_])
                DMA(o_rw["r"][rows], HSV(r_[:]), r=[rk_], is_output=True)
                pm, pk = tm_group(t, wa, wak, 512, 512, wt2=wc, c02=0, wk2=wck)
                CP("dve", k_[:], pm[:, :], r=[pk], w=[kk_])
                pm, pk = tm_group(t, wb_, wbk, 0, 512, wt2=wc, c02=512, wk2=wck)
                CP("act", v_[:], pm[:, :], r=[pk], w=[vk_])
                if layer1:
                    vf, vfk = E["vf"].next(); s_, sk_ = E["s"].next()
                    DMA(HSV(vf[:]), vfirst[rows], w=[vfk])
                    pm, pk = mm_ring.next()
                    MM(pm[:, :], TV[:, tl], VUP[:, :], r=["TV", "VUP"], w=[pk])
                    ACT(s_[:], pm[:, :], AF.Sigmoid, r=[pk], w=[sk_])
                    TT("dve", vf[:], vf[:], v_[:], ALU.subtract, r=[vfk, vk_], w=[vfk])
                    TT("dve", vf[:], vf[:], s_[:], ALU.mult, r=[vfk, sk_], w=[vfk])
                    TT("dve", v_[:], v_[:], vf[:], ALU.add, r=[vfk, vk_], w=[vk_])
                DMA(o_rw["v"][rows], HSV(v_[:]), r=[vk_], is_output=True)
                pm, pk = mm_ring.next()
                MM(pm[:, :], TA[:, tl], AUP[:, :], r=["TA", "AUP"], w=[pk])
                ACT(a_[:], pm[:, :], AF.Sigmoid, r=[pk], w=[ak_])
                pm, pk = mm_ring.next()
                MM(pm[:, :], TW[:, tl], WUP[:, :], r=["TW", "WUP"], w=[pk])
                ACT(e_[:], pm[:, :], AF.Exp, scale=-1.0, r=[pk], w=[ek_])
                ACT(e_[:], e_[:], AF.Ln, bias=1.0, r=[ek_], w=[ek_])
                ACT(e_[:], e_[:], AF.Exp, scale=-1.0, bias=-0.5, r=[ek_], w=[ek_])
                TS("dve", lw[:], e_[:], -1.0, None, ALU.mult, r=[ek_], w=[lwk])
                DMA(o_rw["lw"][rows], HSV(lw[:]), r=[lwk], is_output=True)
                pm, pk = mm_ring.next()
                MM(pm[:, :], TG[:, tl], GUP[:, :], r=["TG", "GUP"], w=[pk])
                CP("dve", g_[:], pm[:, :], r=[pk], w=[gk_])
                DMA(o_g[rows, :], g_[:], r=[gk_], is_output=True)
                TT("dve", kq[:], k_[:], KKb[:], ALU.mult, r=[kk_, "KKb"], w=[kqk])
                TT("dve", tt_[:], kq[:], kq[:], ALU.mult, r=[kqk], w=[ttk])
                RED("dve", sm[:, 0:8], v8(tt_[:]), ALU.add, r=[ttk], w=[smk])
                ACT(sm[:, 0:8], sm[:, 0:8], AF.Sqrt, r=[smk], w=[smk])
                TS("dve", sm[:, 0:8], sm[:, 0:8], 1e-12, None, ALU.max, r=[smk], w=[smk])
                RECIP(sm[:, 0:8], sm[:, 0:8], r=[smk], w=[smk])
                TT("dve", v8(kkn[:]), v8(kq[:]), bc8(sm[:, 0:8]), ALU.mult, r=[kqk, smk], w=[kknk])
                DMA(o_rw["kk"][rows], HSV(kkn[:]), r=[kknk], is_output=True)
                STT("dve", tt_[:], a_[:], -1.0, KAb[:], ALU.add, ALU.mult, r=[ak_, "KAb", ttk], w=[ttk])
                TT("dve", b_[:], kkn[:], a_[:], ALU.mult, r=[kknk, ak_, ttk], w=[bk_])
                DMA(o_rw["b"][rows], HSV(b_[:]), r=[bk_], is_output=True)
                STT("dve", k_[:], tt_[:], 1.0, k_[:], ALU.add, ALU.mult, r=[ttk, kk_], w=[kk_])
                DMA(o_rw["k"][rows], HSV(k_[:]), r=[kk_], is_output=True)
                TT("dve", rkp[:], r_[:], k_[:], ALU.mult, r=[rk_, kk_], w=[rkpk])
                TT("dve", rkp[:], rkp[:], RKb[:], ALU.mult, r=[rkpk, "RKb"], w=[rkpk])
                sm2, sm2k = sm_ring.next()
                RED("dve", sm2[:, 0:8], v8(rkp[:]), ALU.add, r=[rkpk], w=[sm2k])
                TT("dve", v8(bon[:]), v8(v_[:]), bc8(sm2[:, 0:8]), ALU.mult, r=[vk_, sm2k], w=[bonk])
                DMA(o_bonus[rows, :], bon[:], r=[bonk], is_output=True)
            st4.__exit__(None, None, None)
        S.barrier()


def emit_B(nc, S, M, T, SEQ, tag, do_attn=True, do_rwkv=True, after_chunk=None):
    import os
    NB = SEQ // 128
    qT = T["qT"]; kT = T["kT"]; vv = T["vv"]; lf = T["lf"]; crow = T["crow"]
    rw = {n: T["rw_" + n] for n in ("r", "k", "v", "kk", "b", "lw")}
    yb = T["yb"]; yc = T["yc"]

    with ExitStack() as es:
        sb0, ps, sbring0 = mk_alloc(nc, es)
        _sfx = "_B" + tag
        sb_g = lambda n, shp, dt=F32: sb0(n + _sfx, shp, dt)
        sb = sb_g
        sbring = lambda name, n, shp, dt=F32: Ring(name, [sb_g("%s%d" % (name, i), shp, dt) for i in range(n)])

        def MM(out, lhsT, rhs, start=True, stop=True, r=(), w=()):
            S.op("pe", lambda e: e.matmul(out, lhsT, rhs, start=start, stop=stop), r, w)

        def MMS(out, lhsT, rhs, start=True, stop=True, r=(), w=()):
            S.op("pe", lambda e: e.matmul(out, lhsT, rhs, start=start, stop=stop, skip_group_check=True), r, w)

        def TR(out, in_, ident, r=(), w=()):
            S.op("pe", lambda e: e.transpose(out, in_, ident), r, w)

        def ACT(out, in_, func, bias=None, scale=1.0, r=(), w=()):
            if bias is None:
                S.op("act", lambda e: e.activation(out=out, in_=in_, func=func, scale=scale), r, w)
            else:
                S.op("act", lambda e: e.activation(out=out, in_=in_, func=func, bias=bias, scale=scale), r, w)

        def TT(eng, out, in0, in1, op, r=(), w=()):
            S.op(eng, lambda e: e.tensor_tensor(out=out, in0=in0, in1=in1, op=op), r, w)

        def TS(eng, out, in0, s1, s2, op0, op1=None, r=(), w=()):
            if op1 is None:
                S.op(eng, lambda e: e.tensor_scalar(out=out, in0=in0, scalar1=s1, scalar2=None, op0=op0), r, w)
            else:
                S.op(eng, lambda e: e.tensor_scalar(out=out, in0=in0, scalar1=s1, scalar2=s2, op0=op0, op1=op1), r, w)

        def STT(eng, out, in0, scalar, in1, op0, op1, r=(), w=()):
            S.op(eng, lambda e: e.scalar_tensor_tensor(out=out, in0=in0, scalar=scalar, in1=in1, op0=op0, op1=op1), r, w)

        def CP(eng, out, in_, r=(), w=()):
            if eng == "act":
                S.op("act", lambda e: e.copy(out, in_), r, w)
            else:
                S.op(eng, lambda e: e.tensor_copy(out, in_), r, w)

        def MS(eng, out, val, w=()):
            S.op(eng, lambda e: e.memset(out, val), (), w)

        def DMA(out, in_, r=(), w=(), is_output=False):
            S.dma("sp", out, in_, reads=r, writes=w, is_output=is_output)

        ident, ones, uti, uts, lts, blk, mask2 = (M[k] for k in ("ident", "ones", "uti", "uts", "lts", "blk", "mask2"))

        banks = [ps("bankB%d" % i + _sfx, [128, 512]) for i in range(8)]
        if do_attn and do_rwkv:
            cfg = ((0, 1), (2, 3), (4, 5, 6, 7))
        elif do_attn:
            cfg = ((0, 1, 2), (3, 4), (5, 6, 7))
        else:
            cfg = ((), (), (0, 1, 2, 3, 4, 5, 6, 7))
        st_ring = Ring("st", [banks[b][:, 0:512] for b in cfg[0]])
        o_ring = Ring("o", [banks[b][:, 0:512] for b in cfg[1]])
        g_ring = Ring("g", [banks[b][:, 0:512] for b in cfg[2]])
        S.excl.update(["st", "o", "g"])

        def attn_gen():
            QTa = [sb("QTa%d" % h, [67, SEQ], BF16) for h in range(2)]
            KTa = [sb("KTa%d" % h, [67, SEQ], BF16) for h in range(2)]
            VP = sb("VP", [128, NB, 2, 66], BF16)
            LF = sb("LF", [128, 2 * NB])
            Csb = sb("Csb", [128, 2 * NB])
            NC = sb("NC", [128, 2 * NB])
            TOTT = sb("TOTT", [128, 128])
            CH = 2048 if SEQ >= 2048 else SEQ
            for h in range(2):
                for c0 in range(0, SEQ, CH):
                    DMA(QTa[h][0:64, c0:c0 + CH], qT[h * 64:(h + 1) * 64, c0:c0 + CH], w=[("QTa", h)])
                    DMA(KTa[h][0:64, c0:c0 + CH], kT[h * 64:(h + 1) * 64, c0:c0 + CH], w=[("KTa", h)])
                TS("dve", QTa[h][0:64, :], QTa[h][0:64, :], 0.125, None, ALU.mult, r=[("QTa", h)], w=[("QTa", h)])
                MS("dve", KTa[h][64:67, :], 1.0, w=[("KTa1", h)])
            MS("dve", VP[:, :, :, 64:66], 1.0, w=["VP1"])
            vsrc = vv.rearrange("(n p) (h d) -> p n h d", p=128, h=2)
            for n0 in range(0, NB, 8):
                n1 = min(NB, n0 + 8)
                for hh in range(2):
                    DMA(VP[:, n0:n1, hh, 0:64], vsrc[:, n0:n1, hh, :], r=["VP1"], w=[("VP", n0 // 8)])
            LF3 = sb("LF3", [128, NB, 2])
            lf3 = lf.rearrange("(n p) h -> p n h", p=128)
            for n0 in range(0, NB, 16):
                n1 = min(NB, n0 + 16)
                DMA(LF3[:, n0:n1, :], lf3[:, n0:n1, :], w=[("LF3", n0)])
            for hh in range(2):
                CP("dve", LF[:, hh * NB:(hh + 1) * NB], LF3[:, :, hh], r=[("LF3", n0) for n0 in range(0, NB, 16)] + ["LF"], w=["LF"])
            cs = {n: sb("cs_" + n, [128, NB]) for n in ("hf", "r1", "lf", "r2")}
            csb = {n: sb("csb_" + n, [128, NB], BF16) for n in ("h", "l")}
            CR = sb("CR", [NB, 3, 128], BF16)
            for h in range(2):
                lfh = LF[:, h * NB:(h + 1) * NB]
                Ch = Csb[:, h * NB:(h + 1) * NB]
                g, gk = g_ring.next()
                MM(g[0:NB, 0:128], lfh, ones[:], r=["LF", "ones"], w=[gk])
                CP("dve", TOTT[0:NB, :], g[0:NB, 0:128], r=[gk], w=["TOTT"])
                g, gk = g_ring.next()
                MM(g[:, 0:NB], uti[:], lfh, start=True, stop=False, r=["LF", "uti"], w=[gk])
                MM(g[:, 0:NB], TOTT[0:NB, :], uts[0:NB, 0:NB], start=False, stop=True, r=["TOTT", "uts"], w=[gk])
                CP("dve", Ch, g[:, 0:NB], r=[gk], w=["Csb"])
                TS("dve", NC[:, h * NB:(h + 1) * NB], Ch, -1.0, None, ALU.mult, r=["Csb"], w=["NC"])
                CP("dve", csb["h"][:], Ch, r=["Csb"], w=["csb_h"])
                CP("dve", cs["hf"][:], csb["h"][:], r=["csb_h"], w=["cs_hf"])
                TT("dve", cs["r1"][:], Ch, cs["hf"][:], ALU.subtract, r=["Csb", "cs_hf"], w=["cs_r1"])
                CP("dve", csb["l"][:], cs["r1"][:], r=["cs_r1"], w=["csb_l"])
                CP("dve", cs["lf"][:], csb["l"][:], r=["csb_l"], w=["cs_lf"])
                TT("dve", cs["r2"][:], cs["r1"][:], cs["lf"][:], ALU.subtract, r=["cs_r1", "cs_lf"], w=["cs_r2"])
                for ti, nm in enumerate(("hf", "lf", "r2")):
                    g, gk = g_ring.next()
                    TR(g[0:NB, 0:128], cs[nm][:], ident[:], r=["cs_" + nm, "ident"], w=[gk])
                    CP("dve", CR[:, ti, :], g[0:NB, 0:128], r=[gk, "CR"], w=["CR"])
                DMA(crow[h].rearrange("t (n p) -> n t p", p=128), CR[:], r=["CR"], w=[("crow", h)])
                DMA(QTa[h][64:67, :], crow[h], r=[("crow", h), ("QTa", h)], w=[("QTa", h)])
            yield
            pt_ring = sbring("pt", 3, [128, 512], BF16)
            rinv_ring = sbring("rinv", 4, [128, 1])
            ybt_ring = sbring("ybt", 4, [128, 64])
            if not do_rwkv:
                S.barrier(skip_q=("pool",))
                st_ring.tiles.extend(g_ring.tiles)
            LA = len(st_ring.tiles) - 1
            NG4 = NB // 4
            for h in range(2):
                steps = [(g4, j) for g4 in range(NG4) for j in range(4 * g4 + 4)]
                st_info = {}

                def issue_st(n):
                    g4, j = steps[n]
                    a = max(0, j - 4 * g4)
                    st, stk = st_ring.next()
                    diag = j >= 4 * g4
                    MM(st[:, a * 128:512], KTa[h][0:67, j * 128:(j + 1) * 128], QTa[h][0:67, g4 * 512 + a * 128:(g4 + 1) * 512], start=True, stop=not diag,
                       r=[("QTa", h), ("KTa", h), ("KTa1", h)], w=[stk])
                    if diag:
                        MM(st[:, a * 128:(a + 1) * 128], M["identb"][:], M["negm"][:], start=False, stop=True, r=["identb", "negm"], w=[stk])
                    st_info[n] = (st, stk)

                for n in range(min(LA, len(steps))):
                    issue_st(n)
                o = ok = None
                for n, (g4, j) in enumerate(steps):
                    a = max(0, j - 4 * g4)
                    if j == 0:
                        o, ok = o_ring.next()
                    st, stk = st_info.pop(n)
                    pt, ptk = pt_ring.next()
                    ACT(pt[:, a * 128:512], st[:, a * 128:512], AF.Exp, bias=NC[:, h * NB + j:h * NB + j + 1], scale=1.0, r=[stk, "NC"], w=[ptk])
                    if n + LA < len(steps):
                        issue_st(n + LA)
                    for ii in range(a, 4):
                        i = 4 * g4 + ii
                        MMS(o[:, ii * 128:ii * 128 + 65], pt[:, ii * 128:(ii + 1) * 128], VP[:, j, h, 0:65], start=(j == 0 and ii == 0), stop=(j == i),
                            r=[ptk, "VP1", ("VP", j // 8)], w=[ok])
                    if j == 4 * g4 + 3:
                        for ii in range(4):
                            i = 4 * g4 + ii
                            rinv, rk = rinv_ring.next()
                            ybt, yk = ybt_ring.next()
                            S.op("dve", (lambda a_, b_: (lambda e: e.reciprocal(a_, b_)))(rinv[:], o[:, ii * 128 + 64:ii * 128 + 65]), [ok], [rk])
                            TS("dve", ybt[:], o[:, ii * 128:ii * 128 + 64], rinv[:, 0:1], None, ALU.mult, r=[ok, rk], w=[yk])
                            DMA(yb[i * 128:(i + 1) * 128, h * 64:(h + 1) * 64], ybt[:], r=[yk], is_output=True)
                    yield

        def rwkv_gen():
            NL = 3
            names = ("r", "k", "v", "kk", "b", "lw")
            ld = {n: sbring("ld_" + n, NL, [128, 128]) for n in names}
            vpad = [sbring("vpad%d" % h, NL, [128, 128]) for h in range(2)]
            ST = sb("ST", [128, 128])
            MS("dve", ST[:], 0.0, w=["ST"])
            for h in range(2):
                for ti, t in enumerate(vpad[h].tiles):
                    MS("dve", t[:], 0.0, w=[(vpad[h].name, ti)])
            ahpad = [sbring("ahpad%d" % h, 2, [128, 128]) for h in range(2)]
            nutpad = [sbring("nutpad%d" % h, 2, [128, 128]) for h in range(2)]
            for h in range(2):
                for rg in (ahpad[h], nutpad[h]):
                    for ti, t in enumerate(rg.tiles):
                        MS("dve", t[:], 0.0, w=[(rg.name, ti)])
            ahcat_r = sbring("ahcat", 2, [128, 128])
            nutcat_r = sbring("nutcat", 2, [128, 128])
            R = {n: sbring("t_" + n, 2, [128, 128]) for n in
                 ("Lsb", "eL", "enL", "t1", "eLm", "t2", "eLCL", "al", "be", "ka", "rho", "bep", "kap", "RhT", "mtmp", "MT", "Ysb")}
            pC_r = sbring("pC", 2, [128, 1])
            ARt_r = sbring("ARt", 2, [128, 256])
            BKt_r = sbring("BKt", 2, [128, 256])
            AAr_r = [sbring("AAr%d" % h, 2, [128, 256]) for h in range(2)]
            BBr_r = [sbring("BBr%d" % h, 2, [128, 256]) for h in range(2)]
            P_r = [sbring("P2_%d" % par, 2, [128, 256]) for par in range(2)]
            PT_r = [sbring("PT2_%d" % par, 2, [128, 256]) for par in range(2)]
            X_r = [sbring("X2_%d" % par, 2, [128, 256]) for par in range(2)]
            loaded = {}

            def issue_loads(c):
                d = {}
                rows = slice(c * 128, (c + 1) * 128)
                def ldma(dst, src, r0_, r1_, c0_, c1_, k):
                    if callable(src):
                        S.dma("sp", None, None, writes=[k], custom=lambda e: e.dma_start(out=dst, in_=src(e)[r0_:r1_, c0_:c1_]))
                    else:
                        DMA(dst, src[r0_:r1_, c0_:c1_], w=[k])
                for n in names:
                    t, k = ld[n].next()
                    ldma(t[:], rw[n], c * 128, (c + 1) * 128, 0, 128, k)
                    d[n] = (t, k)
                for h in range(2):
                    t, k = vpad[h].next()
                    ldma(t[:, h * 64:(h + 1) * 64], rw["v"], c * 128, (c + 1) * 128, h * 64, (h + 1) * 64, k)
                    d["vpad%d" % h] = (t, k)
                loaded[c] = d

            def chunk_gen(c):
                par = c % 2
                d = loaded.pop(c)
                (r_, rk), (k_, kk_k), (v_, vk), (kk_, kkk), (b_, bk), (lw_, lwk) = (d[n] for n in names)
                g1, g1k = g_ring.next()
                MM(g1[:, 0:128], uti[:], lw_[:], r=["uti", lwk], w=[g1k])
                MM(g1[:, 128:256], ones[:], lw_[:], r=["ones", lwk], w=[g1k])
                MM(g1[:, 256:257], lw_[:], ones[:, 0:1], r=["ones", lwk], w=[g1k])
                Lsb, Lk = R["Lsb"].next(); eL, eLk = R["eL"].next(); enL, enLk = R["enL"].next()
                t1, t1k = R["t1"].next(); eLm, eLmk = R["eLm"].next(); t2, t2k = R["t2"].next(); eLCL, eLCLk = R["eLCL"].next()
                pC, pCk = pC_r.next()
                CP("act", Lsb[:], g1[:, 0:128], r=[g1k], w=[Lk])
                ACT(eL[:], g1[:, 0:128], AF.Exp, r=[g1k], w=[eLk])
                ACT(enL[:], g1[:, 0:128], AF.Exp, scale=-1.0, r=[g1k], w=[enLk])
                ACT(pC[:], g1[:, 256:257], AF.Exp, r=[g1k], w=[pCk])
                TT("dve", t2[:], g1[:, 128:256], Lsb[:], ALU.subtract, r=[g1k, Lk], w=[t2k])
                TT("dve", t1[:], Lsb[:], lw_[:], ALU.subtract, r=[Lk, lwk], w=[t1k])
                ACT(eLm[:], t1[:], AF.Exp, r=[t1k], w=[eLmk])
                ACT(eLCL[:], t2[:], AF.Exp, r=[t2k], w=[eLCLk])
                yield
                al, alk = R["al"].next(); be, bek = R["be"].next(); ka, kak = R["ka"].next()
                rho, rhok = R["rho"].next(); bep, bepk = R["bep"].next(); kap, kapk = R["kap"].next()
                TT("dve", al[:], kk_[:], eLm[:], ALU.mult, r=[kkk, eLmk], w=[alk])
                TT("dve", be[:], b_[:], enL[:], ALU.mult, r=[bk, enLk], w=[bek])
                TT("dve", ka[:], k_[:], enL[:], ALU.mult, r=[kk_k, enLk], w=[kak])
                TT("dve", rho[:], r_[:], eL[:], ALU.mult, r=[rk, eLk], w=[rhok])
                TT("dve", bep[:], b_[:], eLCL[:], ALU.mult, r=[bk, eLCLk], w=[bepk])
                TT("dve", kap[:], k_[:], eLCL[:], ALU.mult, r=[kk_k, eLCLk], w=[kapk])
                ARt, ARk = ARt_r.next(); BKt, BKk = BKt_r.next()
                g, gk = g_ring.next()
                TR(g[:, 0:128], al[:], ident[:], r=[alk, "ident"], w=[gk])
                TR(g[:, 128:256], rho[:], ident[:], r=[rhok, "ident"], w=[gk])
                TR(g[:, 256:384], be[:], ident[:], r=[bek, "ident"], w=[gk])
                TR(g[:, 384:512], ka[:], ident[:], r=[kak, "ident"], w=[gk])
                CP("act", ARt[:], g[:, 0:256], r=[gk], w=[ARk])
                CP("dve", BKt[:], g[:, 256:512], r=[gk], w=[BKk])
                yield
                P2, P2k = P_r[par].next()
                X2, X2k = X_r[par].next()
                hd = []
                for h in range(2):
                    hp = slice(h * 64, h * 64 + 64)
                    AAr, AArk = AAr_r[h].next(); BBr, BBrk = BBr_r[h].next()
                    g, gk = g_ring.next()
                    MM(g[:, 0:256], BKt[hp, 0:128], ARt[hp, 0:256], r=[BKk, ARk], w=[gk])
                    MM(g[:, 256:512], BKt[hp, 128:256], ARt[hp, 0:256], r=[BKk, ARk], w=[gk])
                    TT("dve", AAr[:], g[:, 0:256], mask2[:], ALU.mult, r=[gk, "mask2"], w=[AArk])
                    TT("dve", BBr[:], g[:, 256:512], mask2[:], ALU.mult, r=[gk, "mask2"], w=[BBrk])
                    hd.append(dict(AAr=AAr, AArk=AArk, BBr=BBr, BBrk=BBrk))
                g, gk = g_ring.next()
                for h in range(2):
                    hp = slice(h * 64, h * 64 + 64)
                    MM(g[:, h * 128:(h + 1) * 128], ARt[hp, 0:128], BKt[hp, 0:128], r=[BKk, ARk], w=[gk])
                    MM(g[:, 256 + h * 64:256 + (h + 1) * 64], hd[h]["BBr"][:, 0:128], v_[:, h * 64:(h + 1) * 64], r=[hd[h]["BBrk"], vk], w=[gk])
                for h in range(2):
                    TT("dve", P2[:, h * 128:(h + 1) * 128], g[:, h * 128:(h + 1) * 128], lts[:], ALU.mult, r=[gk, "lts", P2k], w=[P2k])
                    CP("act", X2[:, h * 128:h * 128 + 64], al[:, h * 64:(h + 1) * 64], r=[alk, X2k], w=[X2k])
                    CP("act", X2[:, h * 128 + 64:(h + 1) * 128], g[:, 256 + h * 64:256 + (h + 1) * 64], r=[gk, X2k], w=[X2k])
                yield
                PT = [hd[0]["AAr"][:, 0:128], hd[1]["AAr"][:, 0:128]]
                PTk = [hd[0]["AArk"], hd[1]["AArk"]]
                g, gk = g_ring.next()
                for h in range(2):
                    MM(g[:, h * 128:(h + 1) * 128], PT[h], X2[:, h * 128:(h + 1) * 128], r=[PTk[h], X2k], w=[gk])
                Xn, Xnk = X_r[par].next()
                TT("dve", Xn[:], X2[:], g[:, 0:256], ALU.subtract, r=[X2k, gk], w=[Xnk])
                X2, X2k = Xn, Xnk
                yield
                for lev in range(1, 7):
                    g, gk = g_ring.next()
                    for h in range(2):
                        MM(g[:, h * 128:(h + 1) * 128], P2[:, h * 128:(h + 1) * 128], PT[h], r=[P2k] + PTk, w=[gk])
                        if lev < 6:
                            MM(g[:, 256 + h * 128:256 + (h + 1) * 128], PT[h], P2[:, h * 128:(h + 1) * 128], r=[P2k] + PTk, w=[gk])
                    P2T, P2Tk = PT_r[par].next()
                    CP("act", P2T[:], g[:, 0:256], r=[gk], w=[P2Tk])
                    if lev < 6:
                        P2n, P2nk = P_r[par].next()
                        CP("dve", P2n[:], g[:, 256:512], r=[gk], w=[P2nk])
                    g3, g3k = g_ring.next()
                    for h in range(2):
                        MM(g3[:, h * 128:(h + 1) * 128], P2T[:, h * 128:(h + 1) * 128], X2[:, h * 128:(h + 1) * 128], r=[P2Tk, X2k], w=[g3k])
                    Xn, Xnk = X_r[par].next()
                    TT("dve", Xn[:], X2[:], g3[:, 0:256], ALU.add, r=[X2k, g3k], w=[Xnk])
                    X2, X2k = Xn, Xnk
                    PT = [P2T[:, 0:128], P2T[:, 128:256]]
                    PTk = [P2Tk, P2Tk]
                    if lev < 6:
                        P2, P2k = P2n, P2nk
                    yield
                ahcat, ahck = ahcat_r.next(); nutcat, nuck = nutcat_r.next()
                pads = []
                for h in range(2):
                    hc = slice(h * 64, h * 64 + 64)
                    ap_, apk = ahpad[h].next(); npd, npk = nutpad[h].next()
                    CP("act", ahcat[:, hc], X2[:, h * 128:h * 128 + 64], r=[X2k, ahck], w=[ahck])
                    CP("act", ap_[:, hc], X2[:, h * 128:h * 128 + 64], r=[X2k], w=[apk])
                    ACT(nutcat[:, hc], X2[:, h * 128 + 64:(h + 1) * 128], AF.Identity, scale=-1.0, r=[X2k, nuck], w=[nuck])
                    ACT(npd[:, hc], X2[:, h * 128 + 64:(h + 1) * 128], AF.Identity, scale=-1.0, r=[X2k], w=[npk])
                    pads.append((ap_, apk, npd, npk))
                g, gk = g_ring.next()
                for h in range(2):
                    MM(g[:, 0:128], pads[h][0][:], hd[h]["AAr"][:, 128:256], start=(h == 0), stop=(h == 1),
                       r=[pads[h][1], hd[h]["AArk"]], w=[gk])
                MMS(g[:, 128:256], ahcat[:], bep[:], r=[ahck, bepk], w=[gk])
                RhT, RhTk = R["RhT"].next()
                TT("dve", RhT[:], ARt[:, 128:256], g[:, 0:128], ALU.subtract, r=[ARk, gk], w=[RhTk])
                mtmp, mtk = R["mtmp"].next(); MT, MTk = R["MT"].next()
                TT("dve", mtmp[:], g[:, 128:256], blk[:], ALU.mult, r=[gk, "blk"], w=[mtk])
                STT("dve", MT[:], ident[:], pC[:, 0:1], mtmp[:], ALU.mult, ALU.subtract, r=["ident", pCk, mtk], w=[MTk])
                yield
                g, gk = g_ring.next()
                MM(g[:, 0:128], RhT[:], ST[:], start=True, stop=False, r=[RhTk, "ST"], w=[gk])
                for h in range(2):
                    vp, vpk = d["vpad%d" % h]
                    MM(g[:, 0:128], hd[h]["BBr"][:, 128:256], vp[:], start=False, stop=False, r=[hd[h]["BBrk"], vpk], w=[gk])
                    MM(g[:, 0:128], hd[h]["AAr"][:, 128:256], pads[h][2][:], start=False, stop=(h == 1),
                       r=[hd[h]["AArk"], pads[h][3]], w=[gk])
                MMS(g[:, 128:256], MT[:], ST[:], start=False, stop=False, r=[MTk, "ST"], w=[gk])
                MMS(g[:, 128:256], kap[:], v_[:], start=False, stop=False, r=[kapk, vk], w=[gk])
                MMS(g[:, 128:256], bep[:], nutcat[:], start=False, stop=True, r=[bepk, nuck], w=[gk])
                Ysb, Yk = R["Ysb"].next()
                CP("act", Ysb[:], g[:, 0:128], r=[gk], w=[Yk])
                TT("dve", ST[:], g[:, 128:256], blk[:], ALU.mult, r=[gk, "blk"], w=["ST"])
                DMA(yc[c * 128:(c + 1) * 128, :], Ysb[:], r=[Yk], w=[("ycd", c // 16)], is_output=True)
                if after_chunk is not None:
                    after_chunk(c)
                yield

            NSTEP = 12
            issue_loads(0)
            if NB > 1:
                issue_loads(1)
            active = []
            nxt = 0
            rounds = 0
            while nxt < NB or active:
                MAXFLY = int(os.environ.get("MAXFLY", "2"))
                if nxt < NB and (len(active) == 0 or (len(active) < MAXFLY and active[-1][1] >= NSTEP // MAXFLY)):
                    if nxt + 1 < NB and nxt >= 1:
                        issue_loads(nxt + 1)
                    active.append([chunk_gen(nxt), 0])
                    nxt += 1
                for item in list(active):
                    try:
                        next(item[0])
                        item[1] += 1
                    except StopIteration:
                        active.remove(item)
                rounds += 1
                yield

        import os
        DUM = os.environ.get("DUMMY", "")
        def dummy_gen():
            dA = sb("dumA", [128, 128]); dB = sb("dumB", [128, 128]); dC = sb("dumC", [128, 128], BF16)
            MS("pool", dA[:], 0.5, w=["dumA"]); MS("pool", dB[:], 0.0, w=["dumB"]); MS("pool", dC[:], 0.5, w=["dumC"])
            for it in range(NB):
                if "pe" in DUM:
                    g, gk = g_ring.next()
                    MM(g[:, 0:128], dA[:], dA[:], r=["dumA"], w=[gk])
                    MM(g[:, 128:256], dA[:], dA[:], r=["dumA"], w=[gk])
                if "bfmm" in DUM:
                    g, gk = g_ring.next()
                    MM(g[:, 0:128], dC[:], dC[:], r=["dumC"], w=[gk])
                if "act" in DUM:
                    ACT(dB[:], dA[:], AF.Exp, scale=-1.0, r=["dumA"], w=["dumB"])
                    CP("act", dB[:], dA[:], r=["dumA"], w=["dumB"])
                if "dve" in DUM:
                    TT("dve", dB[:], dA[:], dA[:], ALU.mult, r=["dumA"], w=["dumB"])
                if "pool" in DUM:
                    TT("pool", dB[:], dA[:], dA[:], ALU.mult, r=["dumA"], w=["dumB"])
                if "dma" in DUM:
                    DMA(dB[:], rw["r"][0:128, :], w=["dumB"])
                yield
        gens = []
        if DUM:
            do_rwkv = False
        if do_attn:
            gens.append(attn_gen())
        if do_rwkv:
            gens.append(rwkv_gen())
        if DUM:
            gens.append(dummy_gen())
        if len(gens) == 2 and os.environ.get("ILV"):
            ga, gr = gens
            na = 2 * sum(4 * g4 + 4 for g4 in range(NB // 4)) + 1
            nr = NB * 7 + 8
            next(ga, None)
            acc = 0.0
            done_a = False
            for _ in gr:
                acc += na / nr
                while acc >= 1.0 and not done_a:
                    acc -= 1.0
                    if next(ga, "END") == "END":
                        done_a = True
            if not done_a:
                for _ in ga:
                    pass
        else:
            for gen in gens:
                for _ in gen:
                    pass
        S.barrier()


NORM_EPS = 1e-6
GN_EPS = 64e-5
DFF = 2816
NFC = 22


def emit_C(nc, S, M, T, last, NT=16, NPH=2):
    TOKF = NT * 128
    NT = NT // NPH
    TOK = NT * 128
    NG = TOK // 512
    xF = T["x"]; ybF = T["yb"]; ycF = T["yc"]; gF = T["g"]; bonF = T["bonus"]
    maF = T["maT"]; gbF = T["gbT"]; gcF = T["gcT"]
    p_b = T["p_b"]; p_c = T["p_c"]; w_out = T["w_out"]
    lnx_g = T["c_lnx_g"]; lnx_b = T["c_lnx_b"]; nffn = T["norm_ffn"]
    w_gu = T["w_gate_up"]; w_dn = T["w_down"]
    if last:
        nfin = T["norm_final"]
    outF = T["xo"]
    HSV = lambda ap: ap.rearrange("p (h c) -> p h c", h=4)

    with ExitStack() as es:
        sb0, ps, sbring0 = mk_alloc(nc, es)
        _sfx = "_C%d" % (1 if last else 0)
        sb_g = lambda n, shp, dt=F32: sb0(n + _sfx, shp, dt)
        sb = sb_g
        sbring = lambda name, n, shp, dt=F32: Ring(name, [sb_g("%s%d" % (name, i), shp, dt) for i in range(n)])
        O = mkops(S)
        MM, TR, ACT, TT, TS, STT, CP, MS, RED, RECIP, DMA = O.MM, O.TR, O.ACT, O.TT, O.TS, O.STT, O.CP, O.MS, O.RED, O.RECIP, O.DMA
        identb = M["identb"]
        tp_ring = Ring("tp", [ps("tpbC%d" % i + _sfx, [128, 1024], BF16)[:] for i in range(2)])
        mm_ring = Ring("mm", [ps("mmbC%d" % i + _sfx, [128, 512])[:] for i in range(6)])
        S.excl.update(["tp", "mm"])

        def bc_load(sbf, name, vec_ap, n):
            t = sbf(name, [128, n])
            DMA(t[:], vec_ap.partition_broadcast(128), w=[name])
            S.const.add(name)
            return t

        LXG = bc_load(sb, "LXG", lnx_g, 512); LXB = bc_load(sb, "LXB", lnx_b, 512); NFb = bc_load(sb, "NFb", nffn, 1024)
        if last:
            NLb = bc_load(sb, "NLb", nfin, 1024)
        stg_ring = sbring("stg", 2, [128, 8, 256])
        sm_ring = sbring("sm", 4, [128, 8])
        X1 = sb("X1", [128, NT, 1024])
        WD = sb("WD", [128, NFC, 1024], BF16)
        junk = sb("junk", [128, 1024], BF16)

        cast_i = [0]

        def load_cast(src_ap3, dst_ap3, nk, ncols, dkey):
            stg, sk = stg_ring.next()
            DMA(stg[:, 0:nk, 0:ncols], src_ap3, w=[sk])
            cast_i[0] += 1
            CP("act" if cast_i[0] % 2 else "dve", dst_ap3, stg[:, 0:nk, 0:ncols], r=[sk, dkey], w=[dkey])

        v8 = lambda ap: ap.rearrange("p (h d) -> p h d", h=8)
        bc8 = lambda ap: ap.unsqueeze(2).to_broadcast([128, 8, 64])

        for ph in range(NPH):
            r0 = ph * TOK
            with stage(nc, S) as (lsb, lring):
                PB = lsb("PB", [128, 4, 1024], BF16); PC = lsb("PC", [128, 4, 1024], BF16); WO = lsb("WO", [128, 8, 1024], BF16)
                for (W, src, nk, key) in ((PB, p_b, 4, "PB"), (PC, p_c, 4, "PC"), (WO, w_out, 8, "WO")):
                    for c0 in range(0, 1024, 256):
                        load_cast(src[:, c0:c0 + 256].rearrange("(k p) c -> p k c", p=128), W[:, :, c0:c0 + 256], nk, 256, key)
                ld = {n: lring("ld_" + n, 2, [128, 512]) for n in ("yc", "g", "bon", "yb")}
                tmp_r = lring("tmpc", 2, [128, 512])
                ybf_r = lring("ybf", 2, [128, 1024], BF16)
                yT_r = lring("yT", 1, [128, 8, 512], BF16)
                mT_r = lring("mT", 1, [128, 8, 512], BF16)
                mg_r = {n: lring("mg_" + n, 2, [128, 512]) for n in ("ma", "gb", "gc")}
                xt_r = lring("xt", 2, [128, 1024])
                for tg in range(NG):
                    yT, yTk = yT_r.next()
                    for tt in range(4):
                        t = tg * 4 + tt
                        rows = slice(r0 + t * 128, r0 + (t + 1) * 128)
                        yc_, yck = ld["yc"].next(); g_, gk = ld["g"].next(); bo_, bok = ld["bon"].next(); yb_, ybk = ld["yb"].next()
                        DMA(HSV(yc_[:]), ycF[rows], w=[yck]); DMA(g_[:], gF[rows, :], w=[gk]); DMA(bo_[:], bonF[rows, :], w=[bok]); DMA(HSV(yb_[:]), ybF[rows], w=[ybk])
                        sm, smk = sm_ring.next(); sm2, sm2k = sm_ring.next()
                        tmp, tk = tmp_r.next()
                        RED("dve", sm[:, 0:8], v8(yc_[:]), ALU.add, r=[yck], w=[smk])
                        TS("dve", sm[:, 0:8], sm[:, 0:8], 1.0 / 64, None, ALU.mult, r=[smk], w=[smk])
                        TT("dve", v8(yc_[:]), v8(yc_[:]), bc8(sm[:, 0:8]), ALU.subtract, r=[yck, smk], w=[yck])
                        TT("dve", tmp[:], yc_[:], yc_[:], ALU.mult, r=[yck], w=[tk])
                        RED("dve", sm2[:, 0:8], v8(tmp[:]), ALU.add, r=[tk], w=[sm2k])
                        ACT(sm2[:, 0:8], sm2[:, 0:8], AF.Sqrt, bias=GN_EPS, scale=1.0 / 64, r=[sm2k], w=[sm2k])
                        RECIP(sm2[:, 0:8], sm2[:, 0:8], r=[sm2k], w=[sm2k])
                        TT("dve", v8(yc_[:]), v8(yc_[:]), bc8(sm2[:, 0:8]), ALU.mult, r=[yck, sm2k], w=[yck])
                        TT("dve", yc_[:], yc_[:], LXG[:], ALU.mult, r=[yck, "LXG"], w=[yck])
                        TT("dve", yc_[:], yc_[:], LXB[:], ALU.add, r=[yck, "LXB"], w=[yck])
                        TT("dve", yc_[:], yc_[:], bo_[:], ALU.add, r=[yck, bok], w=[yck])
                        ybf, ybfk = ybf_r.next()
                        TT("dve", ybf[:, 0:512], yc_[:], g_[:], ALU.mult, r=[yck, gk], w=[ybfk])
                        CP("act", ybf[:, 512:1024], yb_[:], r=[ybk, ybfk], w=[ybfk])
                        tp, tpk = tp_ring.next()
                        for c in range(8):
                            TR(tp[:, c * 128:(c + 1) * 128], ybf[:, c * 128:(c + 1) * 128], identb[:], r=[ybfk, "identb"], w=[tpk])
                        CP("act" if tt % 2 else "dve", yT[:, :, tt * 128:(tt + 1) * 128], tp.rearrange("p (c n) -> p c n", c=8), r=[tpk, yTk], w=[yTk])
                    mT, mTk = mT_r.next()
                    cols = slice(r0 + tg * 512, r0 + (tg + 1) * 512)
                    for n in range(8):
                        ma, mak = mg_r["ma"].next(); gb, gbk = mg_r["gb"].next(); gc, gck = mg_r["gc"].next()
                        DMA(ma[:], maF[n * 128:(n + 1) * 128, cols], w=[mak]); DMA(gb[:], gbF[n * 128:(n + 1) * 128, cols], w=[gbk]); DMA(gc[:], gcF[n * 128:(n + 1) * 128, cols], w=[gck])
                        pb, pbk = mm_ring.next()
                        for c in range(4):
                            MM(pb[:, :], PB[:, c, n * 128:(n + 1) * 128], yT[:, 4 + c, :], start=(c == 0), stop=(c == 3), r=["PB", yTk], w=[pbk])
                        pc, pck = mm_ring.next()
                        for c in range(4):
                            MM(pc[:, :], PC[:, c, n * 128:(n + 1) * 128], yT[:, c, :], start=(c == 0), stop=(c == 3), r=["PC", yTk], w=[pck])
                        TT("dve", gb[:], gb[:], pb[:, :], ALU.mult, r=[gbk, pbk], w=[gbk])
                        TT("dve", gc[:], gc[:], pc[:, :], ALU.mult, r=[gck, pck], w=[gck])
                        TT("dve", ma[:], ma[:], gb[:], ALU.add, r=[mak, gbk], w=[mak])
                        TT("dve", mT[:, n, :], ma[:], gc[:], ALU.add, r=[mak, gck, mTk], w=[mTk])
                    for tt in range(4):
                        t = tg * 4 + tt
                        rows = slice(r0 + t * 128, r0 + (t + 1) * 128)
                        xt, xk = xt_r.next()
                        DMA(xt[:], xF[rows, :], w=[xk])
                        for half in range(2):
                            pm, pk = mm_ring.next()
                            for n in range(8):
                                MM(pm[:, :], mT[:, n, tt * 128:(tt + 1) * 128], WO[:, n, half * 512:(half + 1) * 512], start=(n == 0), stop=(n == 7), r=[mTk, "WO"], w=[pk])
                            TT("dve", X1[:, t, half * 512:(half + 1) * 512], xt[:, half * 512:(half + 1) * 512], pm[:, :], ALU.add, r=[xk, pk, ("X1", t)], w=[("X1", t)])
            with stage(nc, S) as (lsb, lring):
                H2 = lsb("H2", [128, 8, TOK], BF16)
                AT = lsb("AT", [128, NFC, TOK], BF16)
                xh_r = lring("xh", 2, [128, 1024], BF16)
                def h2_norm(t):
                    sm, smk = sm_ring.next()
                    MS("pool", sm[:], 0.0, w=[smk])
                    ACT(junk[:], X1[:, t, :], AF.Square, accum=sm[:, 0:1], r=[("X1", t), smk], w=["junk", smk])
                    ACT(sm[:, 1:2], sm[:, 0:1], AF.Sqrt, bias=NORM_EPS, scale=1.0 / 1024, r=[smk], w=[smk])
                    RECIP(sm[:, 2:3], sm[:, 1:2], r=[smk], w=[smk])
                    xh, xhk = xh_r.next()
                    STT("dve", xh[:], X1[:, t, :], sm[:, 2:3], NFb[:], ALU.mult, ALU.mult, r=[("X1", t), smk, "NFb"], w=[xhk])
                    return xh, xhk

                def h2_tr(t, xh, xhk):
                    tp, tpk = tp_ring.next()
                    for c_ in range(8):
                        TR(tp[:, c_ * 128:(c_ + 1) * 128], xh[:, c_ * 128:(c_ + 1) * 128], identb[:], r=[xhk, "identb"], w=[tpk])
                    CP("act" if t % 2 else "dve", H2[:, :, t * 128:(t + 1) * 128], tp.rearrange("p (c n) -> p c n", c=8), r=[tpk, "H2"], w=["H2"])

                nx = h2_norm(0)
                for t in range(NT):
                    cur = nx
                    if t + 1 < NT:
                        nx = h2_norm(t + 1)
                    h2_tr(t, *cur)
                if ph == 0:
                    for fc in range(NFC):
                        for c0 in range(0, 1024, 256):
                            load_cast(w_dn[fc * 128:(fc + 1) * 128, c0:c0 + 256].rearrange("p (o c) -> p o c", o=1), WD[:, fc:fc + 1, c0:c0 + 256], 1, 256, "WD")
                wgu_r = lring("wgu", 3, [128, 8, 256], BF16)
                sg_r = lring("sg", 2, [128, 512])
                def ffn_load(fc):
                    wg, wgk = wgu_r.next()
                    load_cast(w_gu[:, fc * 128:(fc + 1) * 128].rearrange("(k p) c -> p k c", p=128), wg[:, :, 0:128], 8, 128, wgk)
                    load_cast(w_gu[:, DFF + fc * 128:DFF + (fc + 1) * 128].rearrange("(k p) c -> p k c", p=128), wg[:, :, 128:256], 8, 128, wgk)
                    return wg, wgk
                wg_next = ffn_load(0)
                for fc in range(NFC):
                    wg, wgk = wg_next
                    if fc + 1 < NFC:
                        wg_next = ffn_load(fc + 1)
                    for tg in range(NG):
                        pg, pgk = mm_ring.next()
                        for k in range(8):
                            MM(pg[:, :], wg[:, k, 0:128], H2[:, k, tg * 512:(tg + 1) * 512], start=(k == 0), stop=(k == 7), r=[wgk, "H2"], w=[pgk])
                        pu, puk = mm_ring.next()
                        for k in range(8):
                            MM(pu[:, :], wg[:, k, 128:256], H2[:, k, tg * 512:(tg + 1) * 512], start=(k == 0), stop=(k == 7), r=[wgk, "H2"], w=[puk])
                        sg, sgk = sg_r.next()
                        ACT(sg[:], pg[:, :], AF.Silu, r=[pgk], w=[sgk])
                        TT("dve", AT[:, fc, tg * 512:(tg + 1) * 512], sg[:], pu[:, :], ALU.mult, r=[sgk, puk, ("AT", tg)], w=[("AT", tg)])
                ot_r = lring("ot", 2, [128, 1024])
                for t in range(NT):
                    rows = slice(r0 + t * 128, r0 + (t + 1) * 128)
                    ot, otk = ot_r.next()
                    for half in range(2):
                        pm, pk = mm_ring.next()
                        for fc in range(NFC):
                            MM(pm[:, :], AT[:, fc, t * 128:(t + 1) * 128], WD[:, fc, half * 512:(half + 1) * 512], start=(fc == 0), stop=(fc == NFC - 1),
                               r=[("AT", t // 4), "WD"], w=[pk])
                        TT("dve", ot[:, half * 512:(half + 1) * 512], X1[:, t, half * 512:(half + 1) * 512], pm[:, :], ALU.add, r=[("X1", t), pk, otk], w=[otk])
                    if last:
                        sm, smk = sm_ring.next()
                        MS("pool", sm[:], 0.0, w=[smk])
                        ACT(junk[:], ot[:], AF.Square, accum=sm[:, 0:1], r=[otk, smk], w=["junk", smk])
                        ACT(sm[:, 1:2], sm[:, 0:1], AF.Sqrt, bias=NORM_EPS, scale=1.0 / 1024, r=[smk], w=[smk])
                        RECIP(sm[:, 2:3], sm[:, 1:2], r=[smk], w=[smk])
                        STT("dve", ot[:], ot[:], sm[:, 2:3], NLb[:], ALU.mult, ALU.mult, r=[otk, smk, "NLb"], w=[otk])
                    DMA(outF[rows, :], ot[:], r=[otk], is_output=last)
        S.barrier()


WSHAPES = {"w_in": [1024, 8456], "norm_mix": [1024], "gate_bias": [3, 1024], "a_ln_g": [1024], "a_ln_b": [1024], "a_w_s": [4, 128, 128],
           "a_b_s": [4, 128], "b_f_bias": [8], "c_mu": [1792], "c_w0": [512], "c_w_up": [64, 512], "c_a0": [512], "c_a_up": [64, 512],
           "c_g_up": [128, 512], "c_k_k": [512], "c_k_a": [512], "c_r_k": [512], "c_lnx_g": [512], "c_lnx_b": [512], "p_a": [1024, 1024],
           "p_b": [512, 1024], "p_c": [512, 1024], "w_out": [1024, 1024], "norm_ffn": [1024], "w_gate_up": [1024, 5632], "w_down": [2816, 1024]}
W1SHAPES = {"c_v0": [512], "c_v_down": [512, 32], "c_v_up": [32, 512]}
RG = [[0, 1, 2, 3], [4, 5, 6, 7]]


import os


def build_fused(do_A=True, do_B=True, do_C=True, do_X=True, nlayers=2):
    nc = bass.Bass("TRN2", target_bir_lowering=False)
    TPC, SEQ = 2048, 8192

    def ext(n, shp, dt=F32):
        return nc.dram_tensor(n, shp, dt, kind="ExternalInput").ap()

    def itn(n, shp, dt=F32):
        return nc.dram_tensor(n, shp, dt).ap()

    x_ext = ext("x", [TPC, 1024]); xprev_ext = ext("xprev", [2, 128, 1024])
    W = [{n: ext("%s_%d" % (n, l), shp) for n, shp in WSHAPES.items()} for l in range(2)]
    for n, shp in W1SHAPES.items():
        W[1][n] = ext(n + "_1", shp)
    nfin = ext("norm_final", [1024])
    out_ext = nc.dram_tensor("out", [TPC, 1024], F32, kind="ExternalOutput").ap()
    RW = ("r", "k", "v", "kk", "b", "lw")
    qk_s = itn("qk_s", [1024, TPC], BF16); vv_s = itn("vv_s", [4 * TPC, 128], BF16); lf_s = itn("lf_s", [4 * TPC, 2])
    rw_s = [{n: itn("rw%s_s%d" % (n, l), [4 * TPC, 128]) for n in RW} for l in range(2)]
    g_s = itn("g_s", [TPC, 512]); bonus_s = itn("bonus_s", [TPC, 512])
    maT_s = itn("maT_s", [1024, TPC]); gbT_s = itn("gbT_s", [1024, TPC]); gcT_s = itn("gcT_s", [1024, TPC])
    G_qk = itn("G_qk", [4096, TPC], BF16); G_v = itn("G_v", [16 * TPC, 128], BF16); G_lf = itn("G_lf", [16 * TPC, 2])
    G_rw = {n: itn("G_rw" + n, [16 * TPC, 128]) for n in RW}
    qT_l = itn("qT_l", [128, SEQ], BF16); kT_l = itn("kT_l", [128, SEQ], BF16); vv_l = itn("vv_l", [SEQ, 128], BF16); lf_l = itn("lf_l", [SEQ, 2])
    rw_l = {n: itn("rw%s_l" % n, [SEQ, 128]) for n in RW}
    crow_i = itn("crow_i", [2, 3, SEQ], BF16)
    yb_s = itn("yb_s", [SEQ, 128]); yc_s = itn("yc_s", [SEQ, 128]); G_yb = itn("G_yb", [4 * SEQ, 128]); G_yc = itn("G_yc", [4 * SEQ, 128])
    yb_l = itn("yb_l", [4 * TPC, 128]); yc_l = itn("yc_l", [4 * TPC, 128])
    X1buf = itn("X1buf", [TPC, 1024]); XPV1 = itn("XPV1", [2, 128, 1024]); LASTROW = itn("LASTROW", [1, 1024]); GLR = itn("GLR", [4, 1024]); XP5 = itn("XP5", [5, 1024])

    with ExitStack() as es:
        S = Sched(nc, es)
        sb, ps, sbring = mk_alloc(nc, es)
        M = build_masks(S, sb)
        ZT = sb("ZT", [128, 1024])
        S.op("pool", lambda e: e.memset(ZT[:], 0.0), (), ["ZT"])

        def allgather(src2d, dst2d, reads=()):
            tok = S.dma("pool", None, None, reads=reads, custom=lambda e: e.collective_compute("AllGather", ALU.bypass, replica_groups=RG, ins=[src2d], outs=[dst2d]), inc=1)
            if not os.environ.get("CCPIPE"):
                S.prog["pool"].append(("wait", tok[0], tok[1]))
                S.waited["pool"][id(tok[0])] = tok[1]

        dynv = {}

        def dyn_val(e, mult):
            if mult not in dynv:
                dynv[mult] = e.snap((e.partition_id() % 4) * mult, min_val=0, max_val=3 * mult)
            return dynv[mult]

        def dyn_copy(dst_ap, G, base, mult, nrows, q="sp"):
            def f(e):
                if (q, mult) not in dynv:
                    dynv[(q, mult)] = e.snap((e.partition_id() % 4) * mult, min_val=0, max_val=3 * mult)
                return e.dma_start(out=dst_ap, in_=G[base:, :][bass.ds(dynv[(q, mult)], nrows), :])
            S.dma(q, None, None, custom=f)

        def copy(dst_ap, src_ap):
            S.dma("sp", dst_ap, src_ap)

        hview = lambda ap: ap.rearrange("(h t) c -> t h c", h=4)
        for l in range(nlayers):
            TA = dict(W[l])
            TA["nmix"] = W[l]["norm_mix"]
            TA["x"] = x_ext if l == 0 else X1buf
            TA["xprev"] = xprev_ext if l == 0 else XPV1
            qk4 = qk_s.rearrange("(h two p) t -> two h p t", h=4, two=2)
            TA.update(qT=qk4[0], kT=qk4[1], vv=hview(vv_s), lf=hview(lf_s),
                      g=g_s, bonus=bonus_s, maT=maT_s, gbT=gbT_s, gcT=gcT_s)
            for n in RW:
                TA["rw_" + n] = hview(rw_s[l][n])
            if l == 1:
                TA["vfirst"] = hview(rw_s[0]["v"])
            def gather_qkv():
                if do_X:
                    for c in range(4):
                        allgather(qk_s[c * 256:(c + 1) * 256, :], G_qk[c * 1024:(c + 1) * 1024, :])
                        allgather(vv_s[c * TPC:(c + 1) * TPC, :], G_v[c * 4 * TPC:(c + 1) * 4 * TPC, :])
                    allgather(lf_s, G_lf)
            if do_A:
                emit_A(nc, S, M, TA, l == 1, after_e=gather_qkv)
            else:
                gather_qkv()
            S.barrier()
            if do_X:
                for c in range(4):
                    for n in RW:
                        allgather(rw_s[l][n][c * TPC:(c + 1) * TPC, :], G_rw[n][c * 4 * TPC:(c + 1) * 4 * TPC, :])
                for n in RW:
                    dyn_copy(rw_l[n], G_rw[n], 0, 4 * TPC, 4 * TPC, q="pool")
            if do_X:
                for q in range(4):
                    dyn_copy(qT_l[:, q * TPC:(q + 1) * TPC], G_qk, q * 256, 1024, 128)
                    dyn_copy(kT_l[:, q * TPC:(q + 1) * TPC], G_qk, q * 256 + 128, 1024, 128)
                    dyn_copy(lf_l[q * TPC:(q + 1) * TPC, :], G_lf, q * 4 * TPC, TPC, TPC)
                dyn_copy(vv_l, G_v, 0, 4 * TPC, 4 * TPC)
            S.barrier(skip_q=("pool",))
            TB = dict(qT=qT_l, kT=kT_l, vv=vv_l, lf=lf_l, yb=yb_s, yc=yc_s, crow=crow_i)
            for n in RW:
                TB["rw_" + n] = rw_l[n]
            if do_B:
                emit_B(nc, S, M, TB, SEQ, str(l) + "a", do_attn=True, do_rwkv=False)
            S.barrier()
            if do_X:
                for c in range(4):
                    allgather(yb_s[c * TPC:(c + 1) * TPC, :], G_yb[c * 4 * TPC:(c + 1) * 4 * TPC, :])
                dyn_copy(yb_l, G_yb, 0, 4 * TPC, 4 * TPC, q="pool")

            def yc_hook(c):
                if do_X and (c + 1) % 16 == 0:
                    qd = (c + 1) // 16 - 1
                    allgather(yc_s[qd * TPC:(qd + 1) * TPC, :], G_yc[qd * 4 * TPC:(qd + 1) * 4 * TPC, :], reads=[("ycd", qd)])
            if do_B:
                emit_B(nc, S, M, TB, SEQ, str(l) + "r", do_attn=False, do_rwkv=True, after_chunk=yc_hook)
            S.barrier()
            if do_X:
                dyn_copy(yc_l, G_yc, 0, 4 * TPC, 4 * TPC)
            S.barrier()
            TC = dict(W[l])
            TC.update(x=(x_ext if l == 0 else X1buf), yb=hview(yb_l), yc=hview(yc_l), g=g_s, bonus=bonus_s, maT=maT_s, gbT=gbT_s, gcT=gcT_s,
                      xo=(X1buf if l == 0 else out_ext))
            if l == 1:
                TC["norm_final"] = nfin
            if l == 0:
                pass
            else:
                pass
            if do_C:
                emit_C(nc, S, M, TC, l == 1)
            if l == 0 and do_X:
                S.dma("sp", XPV1[0], ZT[:], reads=["ZT"]); S.dma("sp", XPV1[1], ZT[:], reads=["ZT"]); S.dma("sp", XP5[0:1, :], ZT[0:1, :], reads=["ZT"])
                S.barrier()
                copy(LASTROW, X1buf[TPC - 1:TPC, :]); copy(XPV1[1, 0:1, :], X1buf[TPC // 2 - 1:TPC // 2, :])
                S.barrier()
                allgather(LASTROW, GLR)
                S.barrier()
                copy(XP5[1:5, :], GLR)
                S.barrier()
                dyn_copy(XPV1[0, 0:1, :], XP5, 0, 1, 1)
                S.barrier()
        S.finish()
        print("fused stats", S.stats())
    return nc


_NC = {}


def _c(a):
    return np.ascontiguousarray(a)


def kernel(**inp):
    inp = {k: np.asarray(v) for k, v in inp.items()}
    NCORE, SEQ, TPC = 8, 8192, 2048
    if "nc" not in _NC:
        _NC["nc"] = build_fused()
    xflat = _c(inp["x"].reshape(2 * SEQ, 1024).astype(np.float32))
    wm = {}
    for l in range(2):
        for n in WSHAPES:
            a = inp[n][l]
            wm["%s_%d" % (n, l)] = _c(a.reshape(WSHAPES[n]).astype(np.float32))
    for n in W1SHAPES:
        wm[n + "_1"] = _c(inp[n][0].astype(np.float32))
    wm["norm_final"] = _c(inp["norm_final"].astype(np.float32))
    maps = []
    for i in range(NCORE):
        r0 = i * TPC
        xp = np.zeros((2, 128, 1024), np.float32)
        if i % 4 != 0:
            xp[0, 0] = xflat[r0 - 1]
        xp[1, 0] = xflat[r0 + TPC // 2 - 1]
        m = dict(wm)
        m["x"] = _c(xflat[r0:r0 + TPC]); m["xprev"] = xp
        maps.append(m)
    res = run_bass_kernel_spmd(_NC["nc"], maps, core_ids=list(range(NCORE))).results
    out = np.concatenate([np.asarray(res[i]["out"]) for i in range(NCORE)], axis=0)
    return _c(out.reshape(2, SEQ, 1024).astype(np.float32))
```
